# Optimizing a Trainium2 kernel written in Bass

```python
import math
import jax, jax.numpy as jnp
from jax import lax
import numpy as np

D_MODEL = 2048
BATCH = 4
SEQ = 4096
DEPTH = 2

GRID_W = 64
CTX_LEN = 256
W_CONV = D_MODEL // 4
W_HYENA = D_MODEL // 4
N_DIFF_HEADS = 8
DIFF_HEAD_DIM = 64
W_DIFF = N_DIFF_HEADS * 2 * DIFF_HEAD_DIM
N_BRANCH = 3
N_IN = 2 * W_CONV + 3 * W_HYENA + 3 * W_DIFF + N_BRANCH * D_MODEL
CONV_K = 31
SHORT_K = 3
HYENA_ORDER = 2
HYENA_BANDS = 8
HYENA_EMB = 1 + 2 * HYENA_BANDS
HYENA_FILT = 64
HYENA_TARGET = 1e-2
HYENA_FAST_PCT = 0.3
HYENA_SLOW_PCT = 1.5
D_FF = 5632
FFN_K = 3
ROPE_BASE = 10000.0
Q_BLOCK = 128
EPS = 1e-6

kernel_name = 'hybrid_conv_hyena_diffattn_dit'


def rms_norm(x, g):
    xf = x.astype(jnp.float32)
    y = xf * lax.rsqrt(jnp.mean(xf * xf, axis=-1, keepdims=True) + EPS)
    return (y * g.astype(jnp.float32)).astype(x.dtype)


def layer_norm(x, g, b):
    xf = x.astype(jnp.float32)
    mu = jnp.mean(xf, axis=-1, keepdims=True)
    var = jnp.mean(jnp.square(xf - mu), axis=-1, keepdims=True)
    y = (xf - mu) * lax.rsqrt(var + EPS) * g.astype(jnp.float32) + b.astype(jnp.float32)
    return y.astype(x.dtype)


def modulate(h, shift, scale):
    return h * (1 + scale) + shift


def depthwise_conv(x, w, b):
    k = w.shape[0]
    pad = (k - 1) // 2
    y = lax.conv_general_dilated(x, w[:, None, :].astype(x.dtype), (1,), [(pad, pad)],
                                 dimension_numbers=('NWC', 'WIO', 'NWC'),
                                 feature_group_count=x.shape[-1])
    return y + b


def axial_rope(rows):
    row = jnp.repeat(jnp.arange(rows, dtype=jnp.int32), GRID_W).astype(jnp.float32)
    col = jnp.tile(jnp.arange(GRID_W, dtype=jnp.int32), rows).astype(jnp.float32)
    n_freq = DIFF_HEAD_DIM // 4
    inv = ROPE_BASE ** (-jnp.arange(n_freq, dtype=jnp.float32) / n_freq)
    ang = jnp.concatenate([row[:, None] * inv, col[:, None] * inv], axis=-1)
    return jnp.cos(ang), jnp.sin(ang)


def apply_rope(x, cos, sin):
    half = DIFF_HEAD_DIM // 2
    x1 = x[..., :half].astype(jnp.float32)
    x2 = x[..., half:].astype(jnp.float32)
    c = cos[:, None, None, :]
    s = sin[:, None, None, :]
    return jnp.concatenate([x1 * c - x2 * s, x1 * s + x2 * c], axis=-1).astype(x.dtype)


def split_qk_heads(t):
    return t.reshape(t.shape[0], t.shape[1], N_DIFF_HEADS, 2, DIFF_HEAD_DIM)


def split_v_heads(t):
    return t.reshape(t.shape[0], t.shape[1], N_DIFF_HEADS, 2 * DIFF_HEAD_DIM)


def diff_softmax_mix(q, k, v, lam):
    s = jnp.einsum('bqhmd,bkhmd->bmhqk', q, k, preferred_element_type=jnp.float32)
    p = jax.nn.softmax(s * (DIFF_HEAD_DIM ** -0.5), axis=-1)
    a = p[:, 0] - lam * p[:, 1]
    return jnp.einsum('bhqk,bkhd->bqhd', a.astype(v.dtype), v)


def diff_output(o, p, lam_init):
    o = rms_norm(o, p['subln_g']) * (1.0 - lam_init)
    return o.reshape(o.shape[0], o.shape[1], W_DIFF) @ p['w_c_out']


def conformer_conv(u, p):
    a, gate = jnp.split(u, 2, axis=-1)
    y = a * jax.nn.sigmoid(gate)
    y = depthwise_conv(y, p['conv_a_w'], p['conv_a_b'])
    y = jax.nn.silu(layer_norm(y, p['ln_a_g'], p['ln_a_b']))
    return y @ p['w_a_out']


def hyena_filters(length, p):
    pos = jnp.arange(length, dtype=jnp.float32)
    t = pos / max(length - 1, 1)
    bands = jnp.linspace(1e-4, HYENA_BANDS - 1, HYENA_BANDS, dtype=jnp.float32)
    ang = (2.0 * math.pi * pos / length)[:, None] * bands[None, :]
    feats = jnp.concatenate([t[:, None], jnp.cos(ang), -jnp.sin(ang)], axis=-1)
    f32 = jnp.float32
    h = jnp.sin(feats @ p['filt_w1'].astype(f32) + p['filt_b1'].astype(f32))
    h = jnp.sin(h @ p['filt_w2'].astype(f32) + p['filt_b2'].astype(f32))
    h = h @ p['filt_w3'].astype(f32)
    deltas = jnp.abs(jnp.linspace(math.log(HYENA_TARGET) / HYENA_SLOW_PCT,
                                  math.log(HYENA_TARGET) / HYENA_FAST_PCT, W_HYENA, dtype=f32))
    decay = jnp.exp(-t[:, None] * deltas[None, :])
    h = h.reshape(length, HYENA_ORDER, 2, W_HYENA) * decay[:, None, None, :]
    h_fwd, h_bwd = h[:, :, 0], h[:, :, 1]
    buf = jnp.concatenate([h_fwd, jnp.zeros((1, HYENA_ORDER, W_HYENA), f32), h_bwd[1:][::-1]], axis=0)
    return buf / jnp.sum(jnp.abs(buf), axis=0, keepdims=True)


def hyena_branch(u, p):
    length = u.shape[1]
    n = 2 * length
    u = depthwise_conv(u, p['short_b_w'], p['short_b_b'])
    v, x1, x2 = jnp.split(u, 3, axis=-1)
    hf = jnp.fft.rfft(hyena_filters(length, p), n=n, axis=0)
    z = v
    for o, gate in enumerate((x1, x2)):
        zf = jnp.fft.rfft(z.astype(jnp.float32), n=n, axis=1)
        conv = jnp.fft.irfft(zf * hf[None, :, o, :], n=n, axis=1)[:, :length]
        z = gate * (conv.astype(z.dtype) + z * p['hyena_skip'][o])
    return z @ p['w_b_out']


def gated_merge(br_a, br_b, br_c, g_logits, p):
    g = jax.nn.sigmoid(g_logits + p['b_gate'])
    g_a, g_b, g_c = jnp.split(g, N_BRANCH, axis=-1)
    return (g_a * br_a + g_b * br_b + g_c * br_c) @ p['w_o']


def token_mixers(hx, hc, p, rope_cos, rope_sin, lam_init, update_ctx):
    cuts = [2 * W_CONV,
            2 * W_CONV + 3 * W_HYENA,
            2 * W_CONV + 3 * W_HYENA + W_DIFF,
            2 * W_CONV + 3 * W_HYENA + 2 * W_DIFF,
            2 * W_CONV + 3 * W_HYENA + 3 * W_DIFF]
    ax, bx, qx, kx, vx, gx = jnp.split(hx @ p['w_in'], cuts, axis=-1)
    ac, bc, qc, kc, vc, gc = jnp.split(hc @ p['w_in'], cuts, axis=-1)
    lq1, lk1, lq2, lk2 = p['diff_lambda'].astype(jnp.float32)
    lam = jnp.exp(jnp.sum(lq1 * lk1)) - jnp.exp(jnp.sum(lq2 * lk2)) + lam_init
    qx = apply_rope(split_qk_heads(qx), rope_cos, rope_sin)
    kx = apply_rope(split_qk_heads(kx), rope_cos, rope_sin)
    kc = split_qk_heads(kc)
    vx = split_v_heads(vx)
    vc = split_v_heads(vc)
    k_all = jnp.concatenate([kc, kx], axis=1)
    v_all = jnp.concatenate([vc, vx], axis=1)
    b, t = hx.shape[0], hx.shape[1]
    n_blk = t // Q_BLOCK
    q_blocks = jnp.moveaxis(qx.reshape(b, n_blk, Q_BLOCK, N_DIFF_HEADS, 2, DIFF_HEAD_DIM), 1, 0)
    ox = lax.map(lambda qb: diff_softmax_mix(qb, k_all, v_all, lam), q_blocks)
    ox = jnp.moveaxis(ox, 0, 1).reshape(b, t, N_DIFF_HEADS, 2 * DIFF_HEAD_DIM)
    out_x = gated_merge(conformer_conv(ax, p), hyena_branch(bx, p), diff_output(ox, p, lam_init), gx, p)
    if not update_ctx:
        return out_x, None
    oc = diff_softmax_mix(split_qk_heads(qc), kc, vc, lam)
    out_c = gated_merge(conformer_conv(ac, p), hyena_branch(bc, p), diff_output(oc, p, lam_init), gc, p)
    return out_x, out_c


def conv_ffn(h, p):
    u = depthwise_conv(h @ p['w_up'], p['conv_f_w'], p['conv_f_b'])
    a, v = jnp.split(u, 2, axis=-1)
    return (jax.nn.silu(a) * v) @ p['w_down']


def setup_inputs(seed: int = 0) -> dict:
    key = jax.random.key(seed)
    ks = jax.random.split(key, 40)
    counter = [0]

    def nrm(shape, scale):
        k = ks[counter[0]]
        counter[0] += 1
        return jax.random.normal(k, shape, jnp.float32) * scale

    L, D = DEPTH, D_MODEL
    return {
        'x': nrm((BATCH, SEQ, D), 1.0),
        'c': nrm((BATCH, D), 1.0),
        'ctx': nrm((BATCH, CTX_LEN, D), 1.0),
        'c_ctx': nrm((D,), 1.0),
        'w_ada': nrm((L, D, 6 * D), 0.5 * D ** -0.5),
        'b_ada': nrm((L, 6 * D), 0.02),
        'norm1_g': 1.0 + nrm((L, D), 0.02),
        'norm2_g': 1.0 + nrm((L, D), 0.02),
        'w_in': nrm((L, D, N_IN), D ** -0.5),
        'b_gate': nrm((L, N_BRANCH * D), 0.02),
        'conv_a_w': nrm((L, CONV_K, W_CONV), CONV_K ** -0.5),
        'conv_a_b': nrm((L, W_CONV), 0.02),
        'ln_a_g': 1.0 + nrm((L, W_CONV), 0.02),
        'ln_a_b': nrm((L, W_CONV), 0.02),
        'w_a_out': nrm((L, W_CONV, D), W_CONV ** -0.5),
        'short_b_w': nrm((L, SHORT_K, 3 * W_HYENA), SHORT_K ** -0.5),
        'short_b_b': nrm((L, 3 * W_HYENA), 0.02),
        'filt_w1': nrm((L, HYENA_EMB, HYENA_FILT), HYENA_EMB ** -0.5),
        'filt_b1': nrm((L, HYENA_FILT), 0.1),
        'filt_w2': nrm((L, HYENA_FILT, HYENA_FILT), HYENA_FILT ** -0.5),
        'filt_b2': nrm((L, HYENA_FILT), 0.1),
        'filt_w3': nrm((L, HYENA_FILT, HYENA_ORDER * 2 * W_HYENA), HYENA_FILT ** -0.5),
        'hyena_skip': nrm((L, HYENA_ORDER, W_HYENA), 1.0),
        'w_b_out': nrm((L, W_HYENA, D), W_HYENA ** -0.5),
        'diff_lambda': nrm((L, 4, DIFF_HEAD_DIM), 0.1),
        'subln_g': 1.0 + nrm((L, N_DIFF_HEADS, 2 * DIFF_HEAD_DIM), 0.02),
        'w_c_out': nrm((L, W_DIFF, D), W_DIFF ** -0.5),
        'w_o': nrm((L, D, D), D ** -0.5),
        'w_up': nrm((L, D, 2 * D_FF), D ** -0.5),
        'conv_f_w': nrm((L, FFN_K, 2 * D_FF), FFN_K ** -0.5),
        'conv_f_b': nrm((L, 2 * D_FF), 0.02),
        'w_down': nrm((L, D_FF, D), D_FF ** -0.5),
        'final_g': 1.0 + nrm((D,), 0.02),
    }


def reference(x, c, ctx, c_ctx, w_ada, b_ada, norm1_g, norm2_g, w_in, b_gate,
              conv_a_w, conv_a_b, ln_a_g, ln_a_b, w_a_out,
              short_b_w, short_b_b, filt_w1, filt_b1, filt_w2, filt_b2, filt_w3,
              hyena_skip, w_b_out, diff_lambda, subln_g, w_c_out, w_o,
              w_up, conv_f_w, conv_f_b, w_down, final_g):
    rows = x.shape[1] // GRID_W
    rope_cos, rope_sin = axial_rope(rows)
    for l in range(DEPTH):
        p = {
            'w_in': w_in[l], 'b_gate': b_gate[l],
            'conv_a_w': conv_a_w[l], 'conv_a_b': conv_a_b[l],
            'ln_a_g': ln_a_g[l], 'ln_a_b': ln_a_b[l], 'w_a_out': w_a_out[l],
            'short_b_w': short_b_w[l], 'short_b_b': short_b_b[l],
            'filt_w1': filt_w1[l], 'filt_b1': filt_b1[l], 'filt_w2': filt_w2[l],
            'filt_b2': filt_b2[l], 'filt_w3': filt_w3[l],
            'hyena_skip': hyena_skip[l], 'w_b_out': w_b_out[l],
            'diff_lambda': diff_lambda[l], 'subln_g': subln_g[l], 'w_c_out': w_c_out[l],
            'w_o': w_o[l], 'w_up': w_up[l], 'conv_f_w': conv_f_w[l],
            'conv_f_b': conv_f_b[l], 'w_down': w_down[l],
        }
        update_ctx = l < DEPTH - 1
        lam_init = 0.8 - 0.6 * math.exp(-0.3 * l)
        mx = jnp.split((jax.nn.silu(c) @ w_ada[l] + b_ada[l])[:, None, :], 6, axis=-1)
        mc = jnp.split((jax.nn.silu(c_ctx) @ w_ada[l] + b_ada[l])[None, None, :], 6, axis=-1)
        hx = modulate(rms_norm(x, norm1_g[l]), mx[0], mx[1])
        hc = modulate(rms_norm(ctx, norm1_g[l]), mc[0], mc[1])
        out_x, out_c = token_mixers(hx, hc, p, rope_cos, rope_sin, lam_init, update_ctx)
        x = x + mx[2] * out_x
        hx = modulate(rms_norm(x, norm2_g[l]), mx[3], mx[4])
        x = x + mx[5] * conv_ffn(hx, p)
        if update_ctx:
            ctx = ctx + mc[2] * out_c
            hc = modulate(rms_norm(ctx, norm2_g[l]), mc[3], mc[4])
            ctx = ctx + mc[5] * conv_ffn(hc, p)
    return rms_norm(x, final_g)
```

```python
import math
from contextlib import ExitStack
import numpy as np
import ml_dtypes
import concourse.bass as bass
import concourse.mybir as mybir
from concourse.bass_utils import run_bass_kernel_spmd

F32 = mybir.dt.float32
BF16 = mybir.dt.bfloat16
AF = mybir.ActivationFunctionType
ALU = mybir.AluOpType
AX = mybir.AxisListType

D = 2048
SEQ = 4096
CTX = 256
TA = CTX + SEQ
DEPTH = 2
NIN = 11776
DFF = 5632
EPS = 1e-6
OFF_A, OFF_G, OFF_B, OFF_Q, OFF_K, OFF_V, OFF_GATE = 0, 512, 1024, 2560, 3584, 4608, 5632
TILES = [(0, 256)] + [(256 + 512 * i, 512) for i in range(8)]
NF = 4224
NFC = 384

ENGS = ("pe", "act", "dve", "pool", "sp")


class Tok:
    __slots__ = ("w", "r", "excl")

    def __init__(self, excl=False):
        self.w = None
        self.r = []
        self.excl = excl


class Sched:
    LIMIT = 32000

    def __init__(self, nc, es, n_dma_sems=8):
        self.nc = nc
        self.es = es
        self.lists = {e: [] for e in ENGS}
        self.count = {e: 0 for e in ENGS}
        self.gen = {e: 0 for e in ENGS}
        self.known = {e: {} for e in ENGS}
        self.semobj = {}
        self.nsem = 0
        for e in ENGS:
            self.semobj[("e", e, 0)] = self._newsem()
        self.dkeys, self.dnext, self.dval, self.dgen = {}, {}, {}, {}
        for q in ("sp", "pool", "act"):
            self.dkeys[q] = []
            for i in range(n_dma_sems):
                key = ("d", q, i, 0)
                self.semobj[key] = self._newsem()
                self.dkeys[q].append(key)
            self.dnext[q] = 0
            self.dval[q] = [0] * n_dma_sems
            self.dgen[q] = [0] * n_dma_sems
        self.final = {}

    def _newsem(self):
        self.nsem += 1
        return self.es.enter_context(self.nc.semaphore(f"sm{self.nsem}"))

    def _wait(self, eng, ev):
        if ev is None:
            return
        key, val = ev
        if key[0] == "e" and key[1] == eng and eng == "pe":
            return
        kn = self.known[eng]
        if kn.get(key, 0) >= val:
            return
        kn[key] = val
        self.lists[eng].append(("wait", key, val))

    def _deps(self, eng, reads, writes):
        for t in reads:
            self._wait(eng, t.w)
        for t in writes:
            self._wait(eng, t.w)
            for ev in t.r:
                self._wait(eng, ev)

    def _record(self, ev, reads, writes):
        for t in reads:
            t.r.append(ev)
        for t in writes:
            t.w = ev
            t.r = []

    def op(self, eng, fn, reads=(), writes=()):
        if any(t.excl for t in reads):
            writes = list(writes) + [t for t in reads if t.excl and t not in writes]
            reads = [t for t in reads if not t.excl]
        self._deps(eng, reads, writes)
        if self.count[eng] >= self.LIMIT:
            self.final[("e", eng, self.gen[eng])] = self.count[eng]
            self.gen[eng] += 1
            self.count[eng] = 0
            self.semobj[("e", eng, self.gen[eng])] = self._newsem()
        self.count[eng] += 1
        key = ("e", eng, self.gen[eng])
        ev = (key, self.count[eng])
        self.lists[eng].append(("op", fn, key))
        self._record(ev, reads, writes)
        return ev

    def dma(self, q, out, in_, reads=(), writes=()):
        self._deps(q, reads, writes)
        i = self.dnext[q]
        self.dnext[q] = (i + 1) % len(self.dkeys[q])
        key = self.dkeys[q][i]
        if self.dval[q][i] > 0:
            self._wait(q, (key, self.dval[q][i]))
        if self.dval[q][i] >= self.LIMIT:
            self.final[key] = self.dval[q][i]
            self.dgen[q][i] += 1
            key = ("d", q, i, self.dgen[q][i])
            self.semobj[key] = self._newsem()
            self.dkeys[q][i] = key
            self.dval[q][i] = 0
        self.dval[q][i] += 16
        ev = (key, self.dval[q][i])
        self.lists[q].append(("dma", out, in_, key))
        self._record(ev, reads, writes)
        return ev

    def _all_events(self):
        evs = [(("e", e, self.gen[e]), self.count[e]) for e in ENGS if self.count[e] > 0]
        for q in self.dkeys:
            for i, v in enumerate(self.dval[q]):
                if v > 0:
                    evs.append((self.dkeys[q][i], v))
        evs += list(self.final.items())
        return evs

    def barrier(self, engines=ENGS):
        evs = self._all_events()
        for e in engines:
            kn = self.known[e]
            for key, val in evs:
                if kn.get(key, 0) < val:
                    kn[key] = val
                    self.lists[e].append(("wait", key, val))

    def emit(self, block):
        lists, semobj = self.lists, self.semobj

        def replay(engname, e):
            for item in lists[engname]:
                k = item[0]
                if k == "wait":
                    e.wait_ge(semobj[item[1]], item[2])
                elif k == "op":
                    item[1](e).then_inc(semobj[item[2]], 1)
                else:
                    e.dma_start(out=item[1], in_=item[2]).then_inc(semobj[item[3]], 16)

        @block.tensor
        def _(e):
            replay("pe", e)

        @block.scalar
        def _(e):
            replay("act", e)

        @block.vector
        def _(e):
            replay("dve", e)

        @block.gpsimd
        def _(e):
            replay("pool", e)

        @block.sync
        def _(e):
            replay("sp", e)


def I_mm(out, lhsT, rhs, start, stop):
    return lambda e: e.matmul(out, lhsT, rhs, start=start, stop=stop)


def I_tr(out, in_, ident):
    return lambda e: e.transpose(out, in_, ident)


def I_act(out, in_, func, bias=None, scale=None, accum_out=None):
    kw = {}
    if bias is not None:
        kw["bias"] = bias
    if scale is not None:
        kw["scale"] = scale
    if accum_out is not None:
        kw["accum_out"] = accum_out
    return lambda e: e.activation(out=out, in_=in_, func=func, **kw)


def I_copy(out, in_):
    return lambda e: e.tensor_copy(out=out, in_=in_)


def I_tt(out, a, b, op):
    return lambda e: e.tensor_tensor(out=out, in0=a, in1=b, op=op)


def I_ts(out, a, s1, s2, op0, op1=None):
    if op1 is None:
        return lambda e: e.tensor_scalar(out=out, in0=a, scalar1=s1, scalar2=None, op0=op0)
    return lambda e: e.tensor_scalar(out=out, in0=a, scalar1=s1, scalar2=s2, op0=op0, op1=op1)


def I_stt(out, a, s, b, op0, op1):
    return lambda e: e.scalar_tensor_tensor(out=out, in0=a, scalar=s, in1=b, op0=op0, op1=op1)


def I_memset(out, v):
    return lambda e: e.memset(out, v)


def I_recip(out, in_):
    return lambda e: e.reciprocal(out=out, in_=in_)


def I_red(out, in_, op):
    if op == "max":
        return lambda e: e.reduce_max(out=out, in_=in_, axis=AX.X)
    return lambda e: e.reduce_sum(out=out, in_=in_, axis=AX.X)


_UID = [0]


def SB(es, nc, name, shape, dt):
    _UID[0] += 1
    return es.enter_context(nc.sbuf_tensor(f"{name}_u{_UID[0]}", shape, dt))


class Cx:
    pass


HOPT = {}


def build(dbg=(), stop_after=None, n_layers=DEPTH, only=None, feed=(), hopt=None):
    global HOPT
    HOPT = hopt or {}
    nc = bass.Bass("TRN2", target_bir_lowering=False)
    cx = Cx()
    cx.nc = nc
    cx.dbg_out = {}

    def din(name, shape, dt=F32):
        return nc.dram_tensor(name, list(shape), dt, kind="ExternalInput").ap()

    def scratch(name, shape, dt):
        if name in feed:
            return din(name, shape, dt)
        if name in dbg:
            ap = nc.dram_tensor(name, list(shape), dt, kind="ExternalOutput").ap()
            cx.dbg_out[name] = ap
            return ap
        return nc.dram_tensor(name, list(shape), dt).ap()

    specs = {
        "x": ([SEQ, D], F32), "ctx": ([CTX, D], F32), "c2": ([2, D], F32),
        "w_ada": ([DEPTH, D, 6 * D], F32), "w_in": ([DEPTH, D, NIN], F32),
        "w_a_out": ([DEPTH, 512, D], F32), "w_b_out": ([DEPTH, 512, D], F32), "w_c_out": ([DEPTH, 1024, D], F32),
        "w_o": ([DEPTH, D, D], F32), "w_up": ([DEPTH, D, 2 * DFF], F32), "w_down": ([DEPTH, DFF, D], F32),
        "filt_w1": ([DEPTH, 17, 64], F32), "filt_w2": ([DEPTH, 64, 64], F32), "filt_w3": ([DEPTH, 64, 2048], F32),
        "vecs": ([DEPTH, NVR, 12288], F32), "conv_a_w": ([DEPTH, 31, 512], F32), "short_b_w": ([DEPTH, 3, 1536], F32),
        "conv_f_w": ([DEPTH, 3, 2 * DFF], F32), "diff_lambda": ([DEPTH, 1, 256], F32), "subln_g": ([DEPTH, 1, 1024], F32),
        "final_g": ([1, D], F32), "ident": ([128, 128], F32), "ropec": ([128, TA], F32), "ropes": ([128, TA], F32),
        "hy_feats": ([17, TA], F32), "hy_decay": ([TA, 512], F32), "hy_decayb": ([TA, 512], F32), "hy_wf": ([128, 72], F32),
        "TC": ([NF, NF], BF16), "TS": ([NF, NF], BF16), "TCc": ([NFC, NFC], BF16), "TSc": ([NFC, NFC], BF16),
    }

    class LazyI(dict):
        def __missing__(self, name):
            shape, dt = specs[name]
            ap = din(name, shape, dt)
            self[name] = ap
            return ap

    I = LazyI()
    if only is None:
        for name in specs:
            I[name]
    out = nc.dram_tensor("out", [SEQ, D], F32, kind="ExternalOutput").ap()
    cx.I = I
    cx.out = out

    cx.xT = scratch("xT", [D, TA], F32)
    cx.hT = scratch("hT", [D, TA], BF16)
    cx.yA = scratch("yA", [512, TA], F32)
    cx.bT = scratch("bT", [1536, TA], F32)
    cx.qT = scratch("qT", [1024, TA], BF16)
    cx.kT = scratch("kT", [1024, TA], BF16)
    cx.V = scratch("V", [TA, 1024], BF16)
    cx.gates = scratch("gates", [6144, TA], BF16)
    cx.zcat = scratch("zcat", [2048, TA], BF16)
    cx.fT = scratch("fT", [DFF, TA], BF16)
    cx.modd = scratch("modd", [128, 96 * 2], F32)
    cx.ubuf = scratch("ubuf", [1536, TA], F32)
    cx.hpm = scratch("hpm", [TA, 2048], BF16)
    cx.Hd = scratch("Hd", [NF + NFC, 2048], F32)
    cx.z1 = scratch("z1", [512, TA], F32)

    with ExitStack() as es:
        S = Sched(nc, es)
        cx.S = S
        cx.ident = nc.alloc_sbuf_tensor("sb_ident", [128, 128], F32)
        cx.identb = nc.alloc_sbuf_tensor("sb_identb", [128, 128], BF16)
        cx.ones_b = nc.alloc_sbuf_tensor("sb_ones_b", [128, 128], BF16)
        cx.ones_f = nc.alloc_sbuf_tensor("sb_ones_f", [128, 128], F32)
        cx.T_const = Tok()
        cx.mod = nc.alloc_sbuf_tensor("sb_mod", [128, 6 * 16 * 2], F32)
        cx.modA = nc.alloc_sbuf_tensor("sb_modA", [128, 2 * 16 * 2], F32)
        cx.T_mod = Tok()
        cx.vec = nc.alloc_sbuf_tensor("sb_vec", [128, 96 * NVR], F32)
        cx.T_vec = Tok()
        cx.ps = [nc.alloc_psum_tensor(f"ps{i}", [128, 512], F32) for i in range(8)]
        cx.T_ps = [Tok(excl=True) for _ in range(8)]
        cx.ps_next = 0
        cx.ps_reserved = set()

        S.dma("sp", cx.ident[:], I["ident"][:, :], writes=[cx.T_const])
        S.op("dve", I_copy(cx.identb[:], cx.ident[:]), reads=[cx.T_const], writes=[cx.T_const])
        S.op("dve", I_memset(cx.ones_b[:], 1.0), writes=[cx.T_const])
        S.op("dve", I_memset(cx.ones_f[:], 1.0), writes=[cx.T_const])

        phases = [("A", phase_A, None)]
        for l in range(n_layers):
            phases += [("vec%d" % l, phase_vec, l), ("B%d" % l, phase_B, l), ("C%d" % l, phase_norm, (l, 0)),
                       ("D%d" % l, phase_D, l), ("E%d" % l, phase_E, l), ("F%d" % l, phase_F, l), ("H%d" % l, phase_H, l),
                       ("G%d" % l, phase_G, l), ("I%d" % l, phase_norm, (l, 1)), ("J%d" % l, phase_J, l)]
        phases.append(("final", phase_final, None))
        if only is not None:
            phases = [p for p in phases if p[0] in only]
        for name, fn, arg in phases:
            S.barrier()
            if arg is None:
                fn(cx)
            else:
                fn(cx, arg)
            if stop_after == name:
                break
        S.barrier()
        with nc.Block() as block:
            S.emit(block)
    return nc, cx


def next_ps(cx):
    while True:
        i = cx.ps_next
        cx.ps_next = (i + 1) % 8
        if i not in cx.ps_reserved:
            return cx.ps[i], cx.T_ps[i]


VEC_ROWS = ["b_ada", "norm1_g", "norm2_g", "b_gate", "conv_a_b", "ln_a_g", "ln_a_b", "short_b_b",
            "hyena_skip", "conv_f_b", "filt_b1", "filt_b2"]
NVR = len(VEC_ROWS)


def vec_ap(cx, row, chunk):
    r = VEC_ROWS.index(row)
    o = chunk * NVR + r
    return cx.vec[:, o:o + 1]


def phase_A(cx):
    nc, S = cx.nc, cx.S
    xT_r = cx.xT.rearrange("(c p) t -> p c t", p=128)
    with ExitStack() as es:
        xin = [SB(es, nc, f"A_xin{i}", [128, D], F32) for i in range(2)]
        xo = [SB(es, nc, f"A_xo{i}", [128, D], F32) for i in range(2)]
        Tin = [Tok(), Tok()]
        To = [Tok(), Tok()]
        for bi in range(TA // 128):
            b = bi % 2
            src = cx.I["ctx"][bi * 128:(bi + 1) * 128, :] if bi < 2 else cx.I["x"][(bi - 2) * 128:(bi - 1) * 128, :]
            S.dma("sp", xin[b][:], src, writes=[Tin[b]])
            for g4 in range(4):
                ps, Tp = next_ps(cx)
                for j in range(4):
                    c = g4 * 4 + j
                    S.op("pe", I_tr(ps[:, j * 128:(j + 1) * 128], xin[b][:, c * 128:(c + 1) * 128], cx.ident[:]),
                         reads=[Tin[b], cx.T_const], writes=[Tp])
                eng = "dve" if g4 % 2 == 0 else "act"
                if eng == "dve":
                    S.op("dve", I_copy(xo[b][:, g4 * 512:(g4 + 1) * 512], ps[:]), reads=[Tp], writes=[To[b]])
                else:
                    S.op("act", I_act(xo[b][:, g4 * 512:(g4 + 1) * 512], ps[:], AF.Identity), reads=[Tp], writes=[To[b]])
            S.dma("sp", xT_r[:, :, bi * 128:(bi + 1) * 128], xo[b][:].rearrange("p (c t) -> p c t", c=16),
                  reads=[To[b]])


def phase_vec(cx, l):
    nc, S = cx.nc, cx.S
    with ExitStack() as es:
        raw = SB(es, nc, "V_raw", [NVR, 12288], F32)
        Traw = Tok()
        S.dma("sp", raw[:], cx.I["vecs"][l], writes=[Traw])
        for g in range(96 // 16):
            ps, Tp = next_ps(cx)
            for j in range(16):
                ch = g * 16 + j
                S.op("pe", I_tr(ps[:, j * NVR:(j + 1) * NVR], raw[:, ch * 128:(ch + 1) * 128], cx.ident[0:NVR, 0:NVR]),
                     reads=[Traw, cx.T_const], writes=[Tp])
            S.op("dve", I_copy(cx.vec[:, g * 16 * NVR:(g + 1) * 16 * NVR], ps[:, 0:16 * NVR]), reads=[Tp],
                 writes=[cx.T_vec])


def load_w_chunk(cx, wbuf, Tw, segs, K):
    S = cx.S
    KC = K // 128
    for (dc, ap) in segs:
        n = ap.shape[1]
        S.dma("pool", wbuf[:, 0:KC, dc:dc + n], ap.rearrange("(kc p) n -> p kc n", p=128), writes=[Tw])


def phase_B(cx, l):
    nc, S = cx.nc, cx.S
    with ExitStack() as es:
        craw = SB(es, nc, "B_craw", [2, D], F32)
        cs = SB(es, nc, "B_cs", [128, 32], BF16)
        csf = SB(es, nc, "B_csf", [128, 32], F32)
        NB = 4
        wb = [SB(es, nc, f"B_w{i}", [128, 16, 128], BF16) for i in range(NB)]
        Tw = [Tok() for _ in range(NB)]
        Tc = Tok()
        S.dma("sp", craw[:], cx.I["c2"][:, :], writes=[Tc])
        for g in range(2):
            ps, Tp = next_ps(cx)
            for j in range(8):
                ch = g * 8 + j
                S.op("pe", I_tr(ps[:, j * 2:(j + 1) * 2], craw[:, ch * 128:(ch + 1) * 128], cx.ident[0:2, 0:2]),
                     reads=[Tc, cx.T_const], writes=[Tp])
            S.op("act", I_act(csf[:, g * 16:(g + 1) * 16], ps[:, 0:16], AF.Silu), reads=[Tp], writes=[Tc])
        S.op("dve", I_copy(cs[:], csf[:]), reads=[Tc], writes=[Tc])
        wsrc = cx.I["w_ada"][l]
        nj = 96

        def issue_load(j):
            load_w_chunk(cx, wb[j % NB], Tw[j % NB], [(0, wsrc[:, j * 128:(j + 1) * 128])], D)

        for j in range(min(NB - 1, nj)):
            issue_load(j)
        ps, Tp = None, None
        for j in range(nj):
            if j + NB - 1 < nj:
                issue_load(j + NB - 1)
            if j % 32 == 0:
                ps, Tp = next_ps(cx)
            o = (j % 32) * 2
            for kc in range(16):
                S.op("pe", I_mm(ps[:, o:o + 2], wb[j % NB][:, kc, :], cs[:, kc * 2:(kc + 1) * 2], kc == 0, kc == 15),
                     reads=[Tw[j % NB], Tc], writes=[Tp])
            S.op("act", I_act(cx.mod[:, j * 2:(j + 1) * 2], ps[:, o:o + 2], AF.Identity, bias=vec_ap(cx, "b_ada", j)),
                 reads=[Tp, cx.T_vec], writes=[cx.T_mod])
        for sub in range(2):
            gname = "norm1_g" if sub == 0 else "norm2_g"
            which_scale = 1 + 3 * sub
            for c in range(16):
                S.op("dve", I_ts(cx.modA[:, (sub * 16 + c) * 2:(sub * 16 + c) * 2 + 2],
                                 cx.mod[:, (which_scale * 16 + c) * 2:(which_scale * 16 + c) * 2 + 2],
                                 1.0, vec_ap(cx, gname, c), ALU.add, ALU.mult),
                     reads=[cx.T_mod, cx.T_vec], writes=[cx.T_mod])
        if "modd" in cx.dbg_out:
            S.dma("sp", cx.modd[:, :], cx.mod[:], reads=[cx.T_mod])


def mod_ap(cx, which, c, grp):
    o = (which * 16 + c) * 2 + grp
    return cx.mod[:, o:o + 1]


def modA_ap(cx, sub, c, grp):
    o = (sub * 16 + c) * 2 + grp
    return cx.modA[:, o:o + 1]


def phase_norm(cx, arg):
    l, sub = arg
    nc, S = cx.nc, cx.S
    xT_r = cx.xT.rearrange("(c p) t -> p c t", p=128)
    hT_r = cx.hT.rearrange("(c p) t -> p c t", p=128)
    with ExitStack() as es:
        xb = [SB(es, nc, f"N_x{i}", [128, 16, 512], F32) for i in range(2)]
        hb = [SB(es, nc, f"N_h{i}", [128, 16, 512], BF16) for i in range(2)]
        sq = SB(es, nc, "N_sq", [128, 16, 512], BF16)
        rs = SB(es, nc, "N_rs", [128, 512], F32)
        tmp = [SB(es, nc, f"N_tmp{i}", [128, 512], F32) for i in range(2)]
        Tx = [Tok(), Tok()]
        Th = [Tok(), Tok()]
        Tsq, Trs = Tok(), Tok()
        Ttmp = [Tok(), Tok()]
        for ti, (t0, n) in enumerate(TILES):
            b = ti % 2
            grp = 1 if t0 < CTX else 0
            S.dma("sp", xb[b][:, :, 0:n], xT_r[:, :, t0:t0 + n], writes=[Tx[b]])
            for c in range(16):
                S.op("act", I_act(sq[:, c, 0:n], xb[b][:, c, 0:n], AF.Square), reads=[Tx[b]], writes=[Tsq])
            ps, Tp = next_ps(cx)
            for c in range(16):
                S.op("pe", I_mm(ps[:, 0:n], cx.ones_b[:], sq[:, c, 0:n], c == 0, c == 15), reads=[Tsq, cx.T_const],
                     writes=[Tp])
            S.op("act", I_act(rs[:, 0:n], ps[:, 0:n], AF.Sqrt, bias=EPS, scale=1.0 / D), reads=[Tp], writes=[Trs])
            S.op("dve", I_recip(rs[:, 0:n], rs[:, 0:n]), reads=[Trs], writes=[Trs])
            for c in range(16):
                tb = c % 2
                S.op("dve", I_tt(tmp[tb][:, 0:n], xb[b][:, c, 0:n], rs[:, 0:n], ALU.mult), reads=[Tx[b], Trs],
                     writes=[Ttmp[tb]])
                S.op("act", I_act(hb[b][:, c, 0:n], tmp[tb][:, 0:n], AF.Identity,
                                  bias=mod_ap(cx, 3 * sub, c, grp), scale=modA_ap(cx, sub, c, grp)),
                     reads=[Ttmp[tb], cx.T_mod], writes=[Th[b]])
            S.dma("sp", hT_r[:, :, t0:t0 + n], hb[b][:, :, 0:n], reads=[Th[b]])


def run_jobs(cx, jobs, act, Tact, tiles, K, NB=6, tag="J"):
    nc, S = cx.nc, cx.S
    KC = K // 128
    with ExitStack() as es:
        wb = [SB(es, nc, f"{tag}_w{i}", [128, KC, 128], BF16) for i in range(NB)]
        Tw = [Tok() for _ in range(NB)]
        flat = []
        for ji, jb in enumerate(jobs):
            for ci in range(len(jb["cols"])):
                flat.append((ji, ci))
        slot = {}

        def issue(fi):
            ji, ci = flat[fi]
            s = fi % NB
            slot[(ji, ci)] = s
            load_w_chunk(cx, wb[s], Tw[s], jobs[ji]["cols"][ci], K)

        nxt = 0
        fpos = 0
        for ji, jb in enumerate(jobs):
            ncol = len(jb["cols"])
            assert ncol <= NB
            while nxt < len(flat) and nxt < fpos + NB:
                issue(nxt)
                nxt += 1
            for ti, (c0, n, t0) in enumerate(tiles):
                pss = []
                for ci in range(ncol):
                    s = slot[(ji, ci)]
                    ps, Tp = next_ps(cx)
                    for kc in range(KC):
                        S.op("pe", I_mm(ps[:, 0:n], wb[s][:, kc, :], act[:, kc, c0:c0 + n], kc == 0, kc == KC - 1),
                             reads=[Tw[s], Tact], writes=[Tp])
                    pss.append((ps, Tp))
                jb["epi"](jb, ti, t0, n, pss)
            fpos += ncol
        S.barrier()


class Stage:
    def __init__(self, nc, es, name, dt, n):
        self.t = [SB(es, nc, f"{name}{i}", [128, 512], dt) for i in range(n)]
        self.T = [Tok() for _ in range(n)]
        self.i = 0

    def next(self):
        i = self.i
        self.i = (i + 1) % len(self.t)
        return self.t[i], self.T[i]


def phase_D(cx, l):
    nc, S = cx.nc, cx.S
    W = cx.I["w_in"][l]
    hT_r = cx.hT.rearrange("(c p) t -> p c t", p=128)
    halves = [(0, 2304), (2304, 2048)]
    for (s0, sn) in halves:
        S.barrier()
        with ExitStack() as es:
            act = SB(es, nc, "D_act", [128, 16, 2304], BF16)
            Tact = Tok()
            rc = SB(es, nc, "D_rc", [128, 2304], F32)
            rsn = SB(es, nc, "D_rs", [128, 2304], F32)
            Trope = Tok()
            for c in range(16):
                S.dma("sp", act[:, c, 0:sn], hT_r[:, c, s0:s0 + sn], writes=[Tact])
            S.dma("sp", rc[:, 0:sn], cx.I["ropec"][:, s0:s0 + sn], writes=[Trope])
            S.dma("sp", rsn[:, 0:sn], cx.I["ropes"][:, s0:s0 + sn], writes=[Trope])
            tiles = [(t0 - s0, n, t0) for (t0, n) in TILES if s0 <= t0 < s0 + sn]
            sf = Stage(nc, es, "D_sf", F32, 4)
            sb = Stage(nc, es, "D_sb", BF16, 4)
            st = Stage(nc, es, "D_st", F32, 4)

            def col(c0):
                return [(0, W[:, c0:c0 + 128])]

            def col_swapped(c0):
                return [(0, W[:, c0 + 32:c0 + 64]), (32, W[:, c0:c0 + 32]),
                        (64, W[:, c0 + 96:c0 + 128]), (96, W[:, c0 + 64:c0 + 96])]

            jobs = []

            def epi_glu(jb, ti, t0, n, pss):
                (pa, Ta), (pg, Tg) = pss
                t1, T1 = st.next()
                S.op("act", I_act(t1[:, 0:n], pg[:, 0:n], AF.Sigmoid), reads=[Tg], writes=[T1])
                o, To = sf.next()
                S.op("dve", I_tt(o[:, 0:n], pa[:, 0:n], t1[:, 0:n], ALU.mult), reads=[Ta, T1], writes=[To])
                r0 = jb["j"] * 128
                S.dma("sp", cx.yA[r0:r0 + 128, t0:t0 + n], o[:, 0:n], reads=[To])

            for j in range(4):
                jobs.append(dict(cols=[col(OFF_A + j * 128), col(OFF_G + j * 128)], epi=epi_glu, j=j))

            def epi_b(jb, ti, t0, n, pss):
                (p, Tp), = pss
                o, To = sf.next()
                S.op("act", I_act(o[:, 0:n], p[:, 0:n], AF.Identity), reads=[Tp], writes=[To])
                r0 = jb["j"] * 128
                S.dma("sp", cx.bT[r0:r0 + 128, t0:t0 + n], o[:, 0:n], reads=[To])

            for j in range(12):
                jobs.append(dict(cols=[col(OFF_B + j * 128)], epi=epi_b, j=j))

            def epi_rope(jb, ti, t0, n, pss):
                (p, Tp), (psw, Tsw) = pss
                c0 = t0 - s0
                t1, T1 = st.next()
                t2, T2 = st.next()
                S.op("dve", I_tt(t1[:, 0:n], p[:, 0:n], rc[:, c0:c0 + n], ALU.mult), reads=[Tp, Trope], writes=[T1])
                S.op("dve", I_tt(t2[:, 0:n], psw[:, 0:n], rsn[:, c0:c0 + n], ALU.mult), reads=[Tsw, Trope], writes=[T2])
                o, To = sb.next()
                S.op("dve", I_tt(o[:, 0:n], t1[:, 0:n], t2[:, 0:n], ALU.add), reads=[T1, T2], writes=[To])
                r0 = jb["j"] * 128
                S.dma("sp", jb["dst"][r0:r0 + 128, t0:t0 + n], o[:, 0:n], reads=[To])

            for j in range(8):
                jobs.append(dict(cols=[col(OFF_Q + j * 128), col_swapped(OFF_Q + j * 128)], epi=epi_rope, j=j, dst=cx.qT))
            for j in range(8):
                jobs.append(dict(cols=[col(OFF_K + j * 128), col_swapped(OFF_K + j * 128)], epi=epi_rope, j=j, dst=cx.kT))

            def epi_gate(jb, ti, t0, n, pss):
                (p, Tp), = pss
                o, To = sb.next()
                S.op("act", I_act(o[:, 0:n], p[:, 0:n], AF.Sigmoid, bias=vec_ap(cx, "b_gate", jb["j"])),
                     reads=[Tp, cx.T_vec], writes=[To])
                r0 = jb["j"] * 128
                S.dma("sp", cx.gates[r0:r0 + 128, t0:t0 + n], o[:, 0:n], reads=[To])

            for j in range(48):
                jobs.append(dict(cols=[col(OFF_GATE + j * 128)], epi=epi_gate, j=j))

            run_jobs(cx, jobs, act, Tact, tiles, D, NB=6, tag="D")

            wv = SB(es, nc, "D_wv", [128, 16, 1024], BF16)
            Twv = Tok()
            for hh in range(2):
                S.dma("pool", wv[:, :, hh * 512:(hh + 1) * 512],
                      W[:, OFF_V + hh * 512:OFF_V + (hh + 1) * 512].rearrange("(kc p) n -> p kc n", p=128), writes=[Twv])
            for tb in range(sn // 128):
                for hh in range(2):
                    ps, Tp = next_ps(cx)
                    for kc in range(16):
                        S.op("pe", I_mm(ps[:], act[:, kc, tb * 128:(tb + 1) * 128], wv[:, kc, hh * 512:(hh + 1) * 512],
                                        kc == 0, kc == 15), reads=[Tact, Twv], writes=[Tp])
                    o, To = sb.next()
                    if hh == 0:
                        S.op("act", I_act(o[:], ps[:], AF.Identity), reads=[Tp], writes=[To])
                    else:
                        S.op("dve", I_copy(o[:], ps[:]), reads=[Tp], writes=[To])
                    tg = s0 + tb * 128
                    S.dma("sp", cx.V[tg:tg + 128, hh * 512:(hh + 1) * 512], o[:], reads=[To])


def dma_mid_split(S, q, out, in_, nmid, per, reads=(), writes=()):
    for a in range(0, nmid, per):
        b = min(nmid, a + per)
        S.dma(q, out[:, a:b, :], in_[:, a:b, :], reads=reads, writes=writes)


def split_tiles(s0, sn, maxn=512):
    out = []
    t = s0
    end = s0 + sn
    while t < end:
        lim = CTX if t < CTX else end
        n = min(maxn, lim - t, end - t)
        out.append((t, n))
        t += n
    return out


def load_T(cx, es, dram2d, R, C, name):
    nc, S = cx.nc, cx.S
    dst = SB(es, nc, name + "_T", [128, (C // 128) * R], F32)
    Tdst = Tok()
    with ExitStack() as es2:
        raw = SB(es2, nc, name + "_raw", [R, C], F32)
        Traw = Tok()
        S.dma("sp", raw[:], dram2d, writes=[Traw])
        nch = C // 128
        per = 512 // R
        for g in range(0, nch, per):
            ps, Tp = next_ps(cx)
            k = min(per, nch - g)
            for j in range(k):
                ch = g + j
                S.op("pe", I_tr(ps[:, j * R:(j + 1) * R], raw[:, ch * 128:(ch + 1) * 128], cx.ident[0:R, 0:R]),
                     reads=[Traw, cx.T_const], writes=[Tp])
            S.op("dve", I_copy(dst[:, g * R:(g + k) * R], ps[:, 0:k * R]), reads=[Tp], writes=[Tdst])
        S.barrier()
    return dst, Tdst


def phase_E(cx, l):
    nc, S = cx.nc, cx.S
    segs = [(0, CTX), (CTX, SEQ)] if l < DEPTH - 1 else [(CTX, SEQ)]
    with ExitStack() as es:
        cw, Tcw = load_T(cx, es, cx.I["conv_a_w"][l], 31, 512, "E_cw")
        conv = SB(es, nc, "E_conv", [128, 4, TA], F32)
        Tconv = [Tok() for _ in range(4)]
        y = [SB(es, nc, f"E_y{i}", [128, TA], F32) for i in range(2)]
        Ty = [Tok(), Tok()]
        acc2 = SB(es, nc, "E_acc2", [128, TA], F32)
        Tacc = Tok()
        for cc in range(4):
            b = cc % 2
            S.dma("sp", y[b][:], cx.yA[cc * 128:(cc + 1) * 128, :], writes=[Ty[b]])
            for (g0, gl) in segs:
                S.op("dve", I_ts(conv[:, cc, g0:g0 + gl], y[b][:, g0:g0 + gl], cw[:, cc * 31 + 15:cc * 31 + 16],
                                 vec_ap(cx, "conv_a_b", cc), ALU.mult, ALU.add),
                     reads=[Ty[b], Tcw, cx.T_vec], writes=[Tconv[cc]])
                for k in range(15):
                    o = 15 - k
                    S.op("dve", I_stt(conv[:, cc, g0 + o:g0 + gl], y[b][:, g0:g0 + gl - o], cw[:, cc * 31 + k:cc * 31 + k + 1],
                                      conv[:, cc, g0 + o:g0 + gl], ALU.mult, ALU.add),
                         reads=[Ty[b], Tcw], writes=[Tconv[cc]])
                for k in range(16, 31):
                    o = k - 15
                    S.op("dve", I_stt(conv[:, cc, g0:g0 + gl - o], y[b][:, g0 + o:g0 + gl], cw[:, cc * 31 + k:cc * 31 + k + 1],
                                      conv[:, cc, g0:g0 + gl - o], ALU.mult, ALU.add),
                         reads=[Ty[b], Tcw], writes=[Tconv[cc]])
        sq = SB(es, nc, "E_sq", [128, 4, 512], F32)
        Tsq = Tok()
        mt = SB(es, nc, "E_m", [128, 512], F32)
        vt = SB(es, nc, "E_v", [128, 512], F32)
        Tm, Tv = Tok(), Tok()
        d = [SB(es, nc, f"E_d{i}", [128, 512], F32) for i in range(2)]
        Td = [Tok(), Tok()]
        so = Stage(nc, es, "E_so", BF16, 4)
        tiles = [t for t in TILES if (l < DEPTH - 1 or t[0] >= CTX)]
        for (t0, n) in tiles:
            ps1, Tp1 = next_ps(cx)
            for cc in range(4):
                S.op("pe", I_mm(ps1[:, 0:n], cx.ones_f[:], conv[:, cc, t0:t0 + n], cc == 0, cc == 3),
                     reads=[Tconv[cc], cx.T_const], writes=[Tp1])
            for cc in range(4):
                S.op("act", I_act(sq[:, cc, 0:n], conv[:, cc, t0:t0 + n], AF.Square), reads=[Tconv[cc]], writes=[Tsq])
            ps2, Tp2 = next_ps(cx)
            for cc in range(4):
                S.op("pe", I_mm(ps2[:, 0:n], cx.ones_f[:], sq[:, cc, 0:n], cc == 0, cc == 3), reads=[Tsq, cx.T_const],
                     writes=[Tp2])
            S.op("dve", I_ts(mt[:, 0:n], ps1[:, 0:n], 1.0 / 512, None, ALU.mult), reads=[Tp1], writes=[Tm])
            S.op("dve", I_tt(vt[:, 0:n], mt[:, 0:n], mt[:, 0:n], ALU.mult), reads=[Tm], writes=[Tv])
            S.op("dve", I_stt(vt[:, 0:n], ps2[:, 0:n], 1.0 / 512, vt[:, 0:n], ALU.mult, ALU.subtract), reads=[Tp2, Tv],
                 writes=[Tv])
            S.op("act", I_act(vt[:, 0:n], vt[:, 0:n], AF.Sqrt, bias=EPS, scale=1.0), reads=[Tv], writes=[Tv])
            S.op("dve", I_recip(vt[:, 0:n], vt[:, 0:n]), reads=[Tv], writes=[Tv])
            for cc in range(4):
                b = cc % 2
                S.op("dve", I_tt(d[b][:, 0:n], conv[:, cc, t0:t0 + n], mt[:, 0:n], ALU.subtract), reads=[Tconv[cc], Tm],
                     writes=[Td[b]])
                S.op("dve", I_tt(d[b][:, 0:n], d[b][:, 0:n], vt[:, 0:n], ALU.mult), reads=[Tv], writes=[Td[b]])
                o, To = so.next()
                S.op("act", I_act(o[:, 0:n], d[b][:, 0:n], AF.Silu, bias=vec_ap(cx, "ln_a_b", cc),
                                  scale=vec_ap(cx, "ln_a_g", cc)), reads=[Td[b], cx.T_vec], writes=[To])
                S.dma("sp", cx.zcat[cc * 128:(cc + 1) * 128, t0:t0 + n], o[:, 0:n], reads=[To])


def phase_H(cx, l):
    nc, S = cx.nc, cx.S
    lam_init = 0.8 - 0.6 * math.exp(-0.3 * l)
    Vr = cx.V.rearrange("(kb p) c -> p kb c", p=128)
    NKB = TA // 128
    with ExitStack() as es:
        dl = SB(es, nc, "H_dl", [128, 256], F32)
        lam = SB(es, nc, "H_lam", [128, 8], F32)
        gs = SB(es, nc, "H_gs", [128, 1024], F32)
        Tl, Tgs = Tok(), Tok()
        S.dma("sp", dl[:], cx.I["diff_lambda"][l].to_broadcast([128, 256]), writes=[Tl])
        S.dma("sp", gs[:], cx.I["subln_g"][l].to_broadcast([128, 1024]), writes=[Tgs])
        S.op("dve", I_tt(dl[:, 0:64], dl[:, 0:64], dl[:, 64:128], ALU.mult), reads=[Tl], writes=[Tl])
        S.op("dve", I_tt(dl[:, 128:192], dl[:, 128:192], dl[:, 192:256], ALU.mult), reads=[Tl], writes=[Tl])
        S.op("dve", I_red(lam[:, 0:1], dl[:, 0:64], "sum"), reads=[Tl], writes=[Tl])
        S.op("dve", I_red(lam[:, 1:2], dl[:, 128:192], "sum"), reads=[Tl], writes=[Tl])
        S.op("act", I_act(lam[:, 2:4], lam[:, 0:2], AF.Exp), reads=[Tl], writes=[Tl])
        S.op("dve", I_tt(lam[:, 4:5], lam[:, 3:4], lam[:, 2:3], ALU.subtract), reads=[Tl], writes=[Tl])
        S.op("dve", I_ts(lam[:, 5:6], lam[:, 4:5], -lam_init, None, ALU.add), reads=[Tl], writes=[Tl])
        neglam = lam[:, 5:6]
        S.op("dve", I_ts(gs[:], gs[:], 1.0 - lam_init, None, ALU.mult), reads=[Tgs], writes=[Tgs])

        kT = SB(es, nc, "H_kT", [128, TA], BF16)
        qT = SB(es, nc, "H_qT", [128, TA], BF16)
        vh = SB(es, nc, "H_v", [128, NKB, 128], BF16)
        oT = SB(es, nc, "H_oT", [128, TA], BF16)
        Tk, Tq, Tv, ToT = Tok(), Tok(), Tok(), Tok()
        Sb = [SB(es, nc, f"H_S{m}", [128, TA], F32) for m in range(2)]
        TS_ = [Tok(), Tok()]
        Eb = [SB(es, nc, f"H_E{m}", [128, TA], BF16) for m in range(2)]
        TE = [Tok(), Tok()]
        tmpf = SB(es, nc, "H_tmp", [128, TA], F32)
        Ttmp = Tok()
        Ab = SB(es, nc, "H_A", [128, TA], BF16)
        TA_ = Tok()
        AT = SB(es, nc, "H_AT", [128, NKB, 128], BF16)
        TAT = Tok()
        st = SB(es, nc, "H_st", [128, 32], F32)
        Tst = Tok()
        on = SB(es, nc, "H_on", [128, 128], BF16)
        Ton = Tok()
        sqj = SB(es, nc, "H_sqj", [128, 128], F32)
        Tsqj = Tok()
        qblocks = list(range(NKB)) if l < DEPTH - 1 else list(range(2, NKB))
        qblocks = HOPT.get('qblocks', qblocks)
        stage = HOPT.get('stage', 9)
        for h in range(HOPT.get('heads', 8)):
            S.dma("sp", kT[:], cx.kT[h * 128:(h + 1) * 128, :], writes=[Tk])
            S.dma("sp", qT[:], cx.qT[h * 128:(h + 1) * 128, :], writes=[Tq])
            dma_mid_split(S, "sp", vh, Vr[:, :, h * 128:(h + 1) * 128], NKB, 12, writes=[Tv])
            for qb in qblocks:
                q0 = qb * 128
                nk = CTX if qb < 2 else TA
                nkb = nk // 128
                chunks = split_tiles(0, nk)
                for m in HOPT.get('ms', range(2)):
                    for ci, (k0, n) in enumerate(chunks):
                        ps, Tp = next_ps(cx)
                        S.op("pe", I_mm(ps[:, 0:n], qT[m * 64:(m + 1) * 64, q0:q0 + 128], kT[m * 64:(m + 1) * 64, k0:k0 + n],
                                        True, True), reads=[Tq, Tk], writes=[Tp])
                        if HOPT.get('sub', 3) >= 2:
                            S.op("dve", I_red(st[:, m * 9 + ci:m * 9 + ci + 1], ps[:, 0:n], "max"), reads=[Tp], writes=[Tst])
                        if HOPT.get('sub', 3) >= 3:
                            S.op("act", I_act(Sb[m][:, k0:k0 + n], ps[:, 0:n], AF.Identity), reads=[Tp], writes=[TS_[m]])
                    nch = len(chunks)
                    if stage < 2:
                        continue
                    S.op("dve", I_red(st[:, 18 + m:19 + m], st[:, m * 9:m * 9 + nch], "max"), reads=[Tst], writes=[Tst])
                    S.op("dve", I_ts(st[:, 20 + m:21 + m], st[:, 18 + m:19 + m], -0.125, None, ALU.mult), reads=[Tst],
                         writes=[Tst])
                    S.op("act", I_act(Eb[m][:, 0:nk], Sb[m][:, 0:nk], AF.Exp, bias=st[:, 20 + m:21 + m], scale=0.125,
                                      accum_out=st[:, 22 + m:23 + m]), reads=[TS_[m], Tst], writes=[TE[m], Tst])
                if stage < 3:
                    continue
                S.op("dve", I_recip(st[:, 24:26], st[:, 22:24]), reads=[Tst], writes=[Tst])
                S.op("dve", I_tt(st[:, 25:26], st[:, 25:26], neglam, ALU.mult), reads=[Tst, Tl], writes=[Tst])
                S.op("dve", I_ts(tmpf[:, 0:nk], Eb[1][:, 0:nk], st[:, 25:26], None, ALU.mult), reads=[TE[1], Tst],
                     writes=[Ttmp])
                S.op("dve", I_stt(Ab[:, 0:nk], Eb[0][:, 0:nk], st[:, 24:25], tmpf[:, 0:nk], ALU.mult, ALU.add),
                     reads=[TE[0], Ttmp, Tst], writes=[TA_])
                if stage < 4:
                    continue
                for g in range(0, nkb, 8):
                    ps, Tp = next_ps(cx)
                    psb = ps[:].bitcast(BF16)
                    k = min(8, nkb - g)
                    for j in range(k):
                        kb = g + j
                        S.op("pe", I_tr(psb[:, j * 128:(j + 1) * 128], Ab[:, kb * 128:(kb + 1) * 128], cx.identb[:]),
                             reads=[TA_, cx.T_const], writes=[Tp])
                    dst = AT[:, g:g + k, :]
                    src = psb[:, 0:k * 128].rearrange("p (a b) -> p a b", a=k)
                    if (g // 8) % 2 == 0:
                        S.op("act", I_act(dst, src, AF.Identity), reads=[Tp], writes=[TAT])
                    else:
                        S.op("dve", I_copy(dst, src), reads=[Tp], writes=[TAT])
                if stage < 5:
                    continue
                pso, Tpo = next_ps(cx)
                for kb in range(nkb):
                    S.op("pe", I_mm(pso[:, 0:128], AT[:, kb, :], vh[:, kb, :], kb == 0, kb == nkb - 1), reads=[TAT, Tv],
                         writes=[Tpo])
                if stage < 6:
                    continue
                S.op("act", I_act(sqj[:], pso[:, 0:128], AF.Square, accum_out=st[:, 26:27]), reads=[Tpo],
                     writes=[Tsqj, Tst])
                S.op("act", I_act(st[:, 27:28], st[:, 26:27], AF.Sqrt, bias=EPS, scale=1.0 / 128), reads=[Tst], writes=[Tst])
                S.op("dve", I_recip(st[:, 27:28], st[:, 27:28]), reads=[Tst], writes=[Tst])
                S.op("dve", I_stt(on[:], pso[:, 0:128], st[:, 27:28], gs[:, h * 128:(h + 1) * 128], ALU.mult, ALU.mult),
                     reads=[Tpo, Tst, Tgs], writes=[Ton])
                pst, Tpt = next_ps(cx)
                pstb = pst[:].bitcast(BF16)
                S.op("pe", I_tr(pstb[:, 0:128], on[:], cx.identb[:]), reads=[Ton, cx.T_const], writes=[Tpt])
                S.op("act", I_act(oT[:, q0:q0 + 128], pstb[:, 0:128], AF.Identity), reads=[Tpt], writes=[ToT])
            S.dma("sp", cx.zcat[1024 + h * 128:1024 + (h + 1) * 128, :], oT[:], reads=[ToT])


def wrap_sin(cx, buf, n, Tb, tmp, Ttmp):
    S = cx.S
    PI = math.pi
    S.op("dve", I_ts(tmp[0:64, 0:n], buf, -PI, 2 * PI, ALU.is_lt, ALU.mult), reads=[Tb], writes=[Ttmp])
    S.op("dve", I_tt(buf, buf, tmp[0:64, 0:n], ALU.add), reads=[Ttmp], writes=[Tb])
    S.op("dve", I_ts(tmp[0:64, 0:n], buf, PI, -2 * PI, ALU.is_gt, ALU.mult), reads=[Tb], writes=[Ttmp])
    S.op("dve", I_tt(buf, buf, tmp[0:64, 0:n], ALU.add), reads=[Ttmp], writes=[Tb])
    S.op("act", I_act(buf, buf, AF.Sin), reads=[Tb], writes=[Tb])


def phase_F(cx, l):
    nc, S = cx.nc, cx.S
    I = cx.I
    seqs = [dict(t0=CTX, L=SEQ, LB=32, NFB=33, f0=0, TC=I["TC"], TS=I["TS"], wf0=0),
            dict(t0=0, L=CTX, LB=2, NFB=3, f0=NF, TC=I["TCc"], TS=I["TSc"], wf0=33)]
    if l == DEPTH - 1:
        seqs = seqs[:1]
    with ExitStack() as es:
        sw, Tsw = load_T(cx, es, I["short_b_w"][l], 3, 1536, "F_sw")
        wfs = SB(es, nc, "F_wf", [128, 72], F32)
        Twf = Tok()
        S.dma("sp", wfs[:], I["hy_wf"][:, :], writes=[Twf])
        es0 = ExitStack()
        u = [SB(es0, nc, f"F_u{i}", [128, TA], F32) for i in range(2)]
        o = [SB(es0, nc, f"F_o{i}", [128, TA], F32) for i in range(2)]
        Tu, To = [Tok(), Tok()], [Tok(), Tok()]
        for jc in range(12):
            b = jc % 2
            S.dma("sp", u[b][:], cx.bT[jc * 128:(jc + 1) * 128, :], writes=[Tu[b]])
            for sq in seqs:
                a, e_ = sq["t0"], sq["t0"] + sq["L"]
                S.op("dve", I_ts(o[b][:, a:e_], u[b][:, a:e_], sw[:, jc * 3 + 1:jc * 3 + 2], vec_ap(cx, "short_b_b", jc),
                                 ALU.mult, ALU.add), reads=[Tu[b], Tsw, cx.T_vec], writes=[To[b]])
                S.op("dve", I_stt(o[b][:, a + 1:e_], u[b][:, a:e_ - 1], sw[:, jc * 3:jc * 3 + 1], o[b][:, a + 1:e_],
                                  ALU.mult, ALU.add), reads=[Tu[b], Tsw], writes=[To[b]])
                S.op("dve", I_stt(o[b][:, a:e_ - 1], u[b][:, a + 1:e_], sw[:, jc * 3 + 2:jc * 3 + 3], o[b][:, a:e_ - 1],
                                   ALU.mult, ALU.add), reads=[Tu[b], Tsw], writes=[To[b]])
                S.dma("sp", cx.ubuf[jc * 128:(jc + 1) * 128, a:e_], o[b][:, a:e_], reads=[To[b]])
        S.barrier()
        es0.close()
        for sq in seqs:
            t0, L, LB, NFB, f0 = sq["t0"], sq["L"], sq["LB"], sq["NFB"], sq["f0"]
            S.barrier()
            with ExitStack() as es1:
                w1 = SB(es1, nc, "F_w1", [17, 64], F32)
                w2 = SB(es1, nc, "F_w2", [64, 64], F32)
                w3 = SB(es1, nc, "F_w3", [64, 2048], F32)
                feats = SB(es1, nc, "F_feats", [17, SEQ], F32)
                h1 = SB(es1, nc, "F_h1", [64, SEQ], F32)
                h2 = SB(es1, nc, "F_h2", [64, SEQ], F32)
                tmpw = SB(es1, nc, "F_tmpw", [64, 512], F32)
                Tw_, Tf, Th1, Th2, Ttw = Tok(), Tok(), Tok(), Tok(), Tok()
                S.dma("sp", w1[:], I["filt_w1"][l], writes=[Tw_])
                S.dma("sp", w2[:], I["filt_w2"][l], writes=[Tw_])
                S.dma("sp", w3[:], I["filt_w3"][l], writes=[Tw_])
                S.dma("sp", feats[:, 0:L], I["hy_feats"][:, t0:t0 + L], writes=[Tf])
                b1 = vec_ap(cx, "filt_b1", 0)[0:64, :]
                b2 = vec_ap(cx, "filt_b2", 0)[0:64, :]
                for c0 in range(0, L, 512):
                    n = min(512, L - c0)
                    ps, Tp = next_ps(cx)
                    S.op("pe", I_mm(ps[0:64, 0:n], w1[:, :], feats[:, c0:c0 + n], True, True), reads=[Tw_, Tf], writes=[Tp])
                    S.op("dve", I_ts(h1[:, c0:c0 + n], ps[0:64, 0:n], b1, None, ALU.add), reads=[Tp, cx.T_vec], writes=[Th1])
                    wrap_sin(cx, h1[:, c0:c0 + n], n, Th1, tmpw, Ttw)
                for c0 in range(0, L, 512):
                    n = min(512, L - c0)
                    ps, Tp = next_ps(cx)
                    S.op("pe", I_mm(ps[0:64, 0:n], w2[:, :], h1[:, c0:c0 + n], True, True), reads=[Tw_, Th1], writes=[Tp])
                    S.op("dve", I_ts(h2[:, c0:c0 + n], ps[0:64, 0:n], b2, None, ALU.add), reads=[Tp, cx.T_vec], writes=[Th2])
                    wrap_sin(cx, h2[:, c0:c0 + n], n, Th2, tmpw, Ttw)
                dec = [SB(es1, nc, f"F_dec{i}", [128, 1024], F32) for i in range(2)]
                Tdec = [Tok(), Tok()]
                hf = [SB(es1, nc, f"F_hf{i}", [128, 512], F32) for i in range(2)]
                hb = [SB(es1, nc, f"F_hb{i}", [128, 512], F32) for i in range(2)]
                ab = [SB(es1, nc, f"F_ab{i}", [128, 512], F32) for i in range(2)]
                ab2 = [SB(es1, nc, f"F_ab2{i}", [128, 512], F32) for i in range(2)]
                hpm_t = [SB(es1, nc, f"F_hpm{i}", [128, 1024], BF16) for i in range(2)]
                Thf, Thb, Tab, Tab2, Thpm = [Tok(), Tok()], [Tok(), Tok()], [Tok(), Tok()], [Tok(), Tok()], [Tok(), Tok()]
                rn = SB(es1, nc, "F_rn", [128, 1024], F32)
                Trn = Tok()
                psn = []
                for _ in range(2):
                    p_, T_ = next_ps(cx)
                    psn.append((p_, T_))
                for p_, _ in psn:
                    cx.ps_reserved.add(cx.ps.index(p_))
                it = 0
                for tb in range(LB):
                    db = tb % 2
                    r0 = t0 + tb * 128
                    S.dma("sp", dec[db][:, 0:512], I["hy_decay"][r0:r0 + 128, :], writes=[Tdec[db]])
                    S.dma("sp", dec[db][:, 512:1024], I["hy_decayb"][r0:r0 + 128, :], writes=[Tdec[db]])
                    for o_ in range(2):
                        k = it % 2
                        it += 1
                        psf, Tpf = next_ps(cx)
                        psb, Tpb = next_ps(cx)
                        S.op("pe", I_mm(psf[:], h2[:, tb * 128:(tb + 1) * 128], w3[:, (o_ * 2) * 512:(o_ * 2 + 1) * 512], True, True),
                             reads=[Th2, Tw_], writes=[Tpf])
                        S.op("pe", I_mm(psb[:], h2[:, tb * 128:(tb + 1) * 128], w3[:, (o_ * 2 + 1) * 512:(o_ * 2 + 2) * 512], True, True),
                             reads=[Th2, Tw_], writes=[Tpb])
                        S.op("dve", I_tt(hf[k][:], psf[:], dec[db][:, 0:512], ALU.mult), reads=[Tpf, Tdec[db]], writes=[Thf[k]])
                        S.op("dve", I_tt(hb[k][:], psb[:], dec[db][:, 512:1024], ALU.mult), reads=[Tpb, Tdec[db]], writes=[Thb[k]])
                        S.op("act", I_act(ab[k][:], hf[k][:], AF.Abs), reads=[Thf[k]], writes=[Tab[k]])
                        S.op("act", I_act(ab2[k][:], hb[k][:], AF.Abs), reads=[Thb[k]], writes=[Tab2[k]])
                        S.op("pool", I_tt(ab[k][:], ab[k][:], ab2[k][:], ALU.add), reads=[Tab2[k]], writes=[Tab[k]])
                        S.op("pe", I_mm(psn[o_][0][:], cx.ones_f[:], ab[k][:], tb == 0, tb == LB - 1), reads=[Tab[k], cx.T_const],
                             writes=[psn[o_][1]])
                        S.op("dve", I_tt(hpm_t[k][:, 0:512], hf[k][:], hb[k][:], ALU.add), reads=[Thf[k], Thb[k]], writes=[Thpm[k]])
                        S.op("pool", I_tt(hpm_t[k][:, 512:1024], hf[k][:], hb[k][:], ALU.subtract), reads=[Thf[k], Thb[k]],
                             writes=[Thpm[k]])
                        S.dma("sp", cx.hpm[r0:r0 + 128, o_ * 512:(o_ + 1) * 512], hpm_t[k][:, 0:512], reads=[Thpm[k]])
                        S.dma("sp", cx.hpm[r0:r0 + 128, 1024 + o_ * 512:1024 + (o_ + 1) * 512], hpm_t[k][:, 512:1024],
                              reads=[Thpm[k]])
                for o_ in range(2):
                    S.op("dve", I_recip(rn[:, o_ * 512:(o_ + 1) * 512], psn[o_][0][:]), reads=[psn[o_][1]], writes=[Trn])
                cx.ps_reserved.clear()
                S.barrier()
                X = SB(es1, nc, "F_X", [128, 32, 512], BF16)
                TX = Tok()
                tab = [SB(es1, nc, f"F_tab{i}", [128, 32, 128], BF16) for i in range(2)]
                Ttab = [Tok(), Tok()]
                so = Stage(nc, es1, "F_so", F32, 3)
                it = 0
                for pm in range(2):
                    table = sq["TC"] if pm == 0 else sq["TS"]
                    for o_ in range(2):
                        cbase = pm * 1024 + o_ * 512
                        dma_mid_split(S, "sp", X, cx.hpm[t0:t0 + L, cbase:cbase + 512].rearrange("(tb p) c -> p tb c", p=128),
                                      LB, 8, writes=[TX])
                        for fb in range(NFB):
                            k = it % 2
                            it += 1
                            dma_mid_split(S, "sp", tab[k], table[0:L, fb * 128:(fb + 1) * 128].rearrange("(db p) f -> p db f", p=128),
                                          LB, 8, writes=[Ttab[k]])
                            ps, Tp = next_ps(cx)
                            for db in range(LB):
                                S.op("pe", I_mm(ps[:], tab[k][:, db, :], X[:, db, :], db == 0, db == LB - 1), reads=[Ttab[k], TX],
                                     writes=[Tp])
                            wcol = sq["wf0"] + fb + (36 if pm == 1 else 0)
                            ot, Tot = so.next()
                            S.op("dve", I_stt(ot[:], ps[:], wfs[:, wcol:wcol + 1], rn[:, o_ * 512:(o_ + 1) * 512], ALU.mult, ALU.mult),
                                 reads=[Tp, Twf, Trn], writes=[Tot])
                            S.dma("sp", cx.Hd[f0 + fb * 128:f0 + (fb + 1) * 128, cbase:cbase + 512], ot[:], reads=[Tot])
        for sq in seqs:
            t0, L, LB, NFB, f0 = sq["t0"], sq["L"], sq["LB"], sq["NFB"], sq["f0"]
            S.barrier()
            with ExitStack() as es2:
                zt = SB(es2, nc, "F_zt", [128, 32, 512], BF16)
                Y = SB(es2, nc, "F_Y", [128, 33, 1024], BF16)
                Tzt, TY = Tok(), Tok()
                ld = Stage(nc, es2, "F_ld", F32, 4)
                for cc in range(4):
                    for c0 in range(0, L, 512):
                        n = min(512, L - c0)
                        vt, Tvt = ld.next()
                        S.dma("sp", vt[:, 0:n], cx.ubuf[cc * 128:(cc + 1) * 128, t0 + c0:t0 + c0 + n], writes=[Tvt])
                        ps, Tp = next_ps(cx)
                        nb = n // 128
                        for j in range(nb):
                            S.op("pe", I_tr(ps[:, j * 128:(j + 1) * 128], vt[:, j * 128:(j + 1) * 128], cx.ident[:]),
                                 reads=[Tvt, cx.T_const], writes=[Tp])
                        tb0 = c0 // 128
                        S.op("act", I_act(zt[:, tb0:tb0 + nb, cc * 128:(cc + 1) * 128],
                                          ps[:, 0:nb * 128].rearrange("p (a b) -> p a b", a=nb), AF.Identity), reads=[Tp], writes=[Tzt])
                Tz1 = {}
                for o_ in range(2):
                    with ExitStack() as es3:
                        tc_ = [SB(es3, nc, f"F_tc{i}", [128, 32, 128], BF16) for i in range(2)]
                        ts_ = [SB(es3, nc, f"F_ts{i}", [128, 32, 128], BF16) for i in range(2)]
                        Hh = [SB(es3, nc, f"F_Hh{i}", [128, 1024], F32) for i in range(2)]
                        Ttc, Tts, THh = [Tok(), Tok()], [Tok(), Tok()], [Tok(), Tok()]
                        pa = Stage(nc, es3, "F_pa", F32, 4)
                        for fb in range(NFB):
                            k = fb % 2
                            dma_mid_split(S, "sp", tc_[k], sq["TC"][0:L, fb * 128:(fb + 1) * 128].rearrange("(db p) f -> p db f", p=128),
                                          LB, 8, writes=[Ttc[k]])
                            dma_mid_split(S, "sp", ts_[k], sq["TS"][0:L, fb * 128:(fb + 1) * 128].rearrange("(db p) f -> p db f", p=128),
                                          LB, 8, writes=[Tts[k]])
                            S.dma("sp", Hh[k][:, 0:512], cx.Hd[f0 + fb * 128:f0 + (fb + 1) * 128, o_ * 512:(o_ + 1) * 512],
                                  writes=[THh[k]])
                            S.dma("sp", Hh[k][:, 512:1024],
                                  cx.Hd[f0 + fb * 128:f0 + (fb + 1) * 128, 1024 + o_ * 512:1024 + (o_ + 1) * 512], writes=[THh[k]])
                            pr, Tpr = next_ps(cx)
                            pi, Tpi = next_ps(cx)
                            for db in range(LB):
                                S.op("pe", I_mm(pr[:], tc_[k][:, db, :], zt[:, db, :], db == 0, db == LB - 1), reads=[Ttc[k], Tzt],
                                     writes=[Tpr])
                            for db in range(LB):
                                S.op("pe", I_mm(pi[:], ts_[k][:, db, :], zt[:, db, :], db == 0, db == LB - 1), reads=[Tts[k], Tzt],
                                     writes=[Tpi])
                            a, Ta = pa.next()
                            b, Tb = pa.next()
                            S.op("dve", I_tt(a[:], pr[:], Hh[k][:, 0:512], ALU.mult), reads=[Tpr, THh[k]], writes=[Ta])
                            S.op("dve", I_tt(b[:], pi[:], Hh[k][:, 512:1024], ALU.mult), reads=[Tpi, THh[k]], writes=[Tb])
                            S.op("pool", I_tt(Y[:, fb, 0:512], a[:], b[:], ALU.add), reads=[Ta, Tb], writes=[TY])
                            c, Tc = pa.next()
                            d, Td = pa.next()
                            S.op("dve", I_tt(c[:], pi[:], Hh[k][:, 0:512], ALU.mult), reads=[Tpi, THh[k]], writes=[Tc])
                            S.op("dve", I_tt(d[:], pr[:], Hh[k][:, 512:1024], ALU.mult), reads=[Tpr, THh[k]], writes=[Td])
                            S.op("pool", I_tt(Y[:, fb, 512:1024], c[:], d[:], ALU.subtract), reads=[Tc, Td], writes=[TY])
                    S.barrier()
                    with ExitStack() as es3:
                        tci = [SB(es3, nc, f"F_tci{i}", [128, 33, 256], BF16) for i in range(2)]
                        tsi = [SB(es3, nc, f"F_tsi{i}", [128, 33, 256], BF16) for i in range(2)]
                        Ttci, Ttsi = [Tok(), Tok()], [Tok(), Tok()]
                        ld2 = Stage(nc, es3, "F_ld2", F32, 6)
                        sob = Stage(nc, es3, "F_sob", BF16, 3)
                        for ti, tt0 in enumerate(range(0, L, 256)):
                            tn = 256
                            k = ti % 2
                            dma_mid_split(S, "sp", tci[k], sq["TC"][0:NFB * 128, tt0:tt0 + tn].rearrange("(fb p) t -> p fb t", p=128),
                                          NFB, 8, writes=[Ttci[k]])
                            dma_mid_split(S, "sp", tsi[k], sq["TS"][0:NFB * 128, tt0:tt0 + tn].rearrange("(fb p) t -> p fb t", p=128),
                                          NFB, 8, writes=[Ttsi[k]])
                            for cc in range(4):
                                ps, Tp = next_ps(cx)
                                for fb in range(NFB):
                                    S.op("pe", I_mm(ps[:, 0:tn], Y[:, fb, cc * 128:(cc + 1) * 128], tci[k][:, fb, :], fb == 0, False),
                                         reads=[TY, Ttci[k]], writes=[Tp])
                                    S.op("pe", I_mm(ps[:, 0:tn], Y[:, fb, 512 + cc * 128:512 + (cc + 1) * 128], tsi[k][:, fb, :], False,
                                                    fb == NFB - 1), reads=[TY, Ttsi[k]], writes=[Tp])
                                zp, Tzp = ld2.next()
                                gt, Tgt = ld2.next()
                                g0 = t0 + tt0
                                if o_ == 0:
                                    S.dma("sp", zp[:, 0:tn], cx.ubuf[cc * 128:(cc + 1) * 128, g0:g0 + tn], writes=[Tzp])
                                else:
                                    S.dma("sp", zp[:, 0:tn], cx.z1[cc * 128:(cc + 1) * 128, g0:g0 + tn], reads=[Tz1[(cc, tt0)]],
                                          writes=[Tzp])
                                gr = (1 + o_) * 512 + cc * 128
                                S.dma("sp", gt[:, 0:tn], cx.ubuf[gr:gr + 128, g0:g0 + tn], writes=[Tgt])
                                S.op("dve", I_stt(zp[:, 0:tn], zp[:, 0:tn], vec_ap(cx, "hyena_skip", o_ * 4 + cc), ps[:, 0:tn],
                                                  ALU.mult, ALU.add), reads=[Tp, cx.T_vec], writes=[Tzp])
                                if o_ == 0:
                                    S.op("dve", I_tt(zp[:, 0:tn], zp[:, 0:tn], gt[:, 0:tn], ALU.mult), reads=[Tgt], writes=[Tzp])
                                    Tz1[(cc, tt0)] = Tok()
                                    S.dma("sp", cx.z1[cc * 128:(cc + 1) * 128, g0:g0 + tn], zp[:, 0:tn], reads=[Tzp],
                                          writes=[Tz1[(cc, tt0)]])
                                    pt, Tpt = next_ps(cx)
                                    for j in range(2):
                                        S.op("pe", I_tr(pt[:, j * 128:(j + 1) * 128], zp[:, j * 128:(j + 1) * 128], cx.ident[:]),
                                             reads=[Tzp, cx.T_const], writes=[Tpt])
                                    tb0 = tt0 // 128
                                    S.op("act", I_act(zt[:, tb0:tb0 + 2, cc * 128:(cc + 1) * 128],
                                                      pt[:, 0:256].rearrange("p (a b) -> p a b", a=2), AF.Identity), reads=[Tpt],
                                         writes=[Tzt])
                                else:
                                    ob, Tob = sob.next()
                                    S.op("dve", I_tt(ob[:, 0:tn], zp[:, 0:tn], gt[:, 0:tn], ALU.mult), reads=[Tgt, Tzp], writes=[Tob])
                                    S.dma("sp", cx.zcat[512 + cc * 128:512 + (cc + 1) * 128, g0:g0 + tn], ob[:, 0:tn], reads=[Tob])
                    S.barrier()


def run_jobs2(cx, jobs, act, Tact, tiles, KCmax, NB=6, tag="J"):
    nc, S = cx.nc, cx.S
    with ExitStack() as es:
        wb = [SB(es, nc, f"{tag}_w{i}", [128, KCmax, 128], BF16) for i in range(NB)]
        Tw = [Tok() for _ in range(NB)]
        flat = []
        for ji, jb in enumerate(jobs):
            for ci in range(len(jb["cols"])):
                flat.append((ji, ci))
        slot = {}

        def issue(fi):
            ji, ci = flat[fi]
            s = fi % NB
            slot[(ji, ci)] = s
            c = jobs[ji]["cols"][ci]
            load_w_chunk(cx, wb[s], Tw[s], c["segs"], c["KC"] * 128)

        nxt = 0
        fpos = 0
        for ji, jb in enumerate(jobs):
            ncol = len(jb["cols"])
            while nxt < len(flat) and nxt < fpos + NB:
                issue(nxt)
                nxt += 1
            for ti, (c0, n, t0) in enumerate(tiles):
                pss = []
                for ci in range(ncol):
                    s = slot[(ji, ci)]
                    c = jb["cols"][ci]
                    ps, Tp = next_ps(cx)
                    KC = c["KC"]
                    for kc in range(KC):
                        S.op("pe", I_mm(ps[:, 0:n], wb[s][:, kc, :], act[:, c["kc0"] + kc, c0:c0 + n], kc == 0, kc == KC - 1),
                             reads=[Tw[s], Tact], writes=[Tp])
                    pss.append((ps, Tp))
                jb["epi"](jb, ti, t0, n, pss)
            fpos += ncol
        S.barrier()


def phase_G(cx, l):
    nc, S = cx.nc, cx.S
    zr = cx.zcat.rearrange("(c p) t -> p c t", p=128)
    halves = [(0, 2304), (2304, 2048)] if l < DEPTH - 1 else [(CTX, 2048), (CTX + 2048, 2048)]
    Wa, Wb, Wc, Wo = cx.I["w_a_out"][l], cx.I["w_b_out"][l], cx.I["w_c_out"][l], cx.I["w_o"][l]
    for (s0, sn) in halves:
        S.barrier()
        with ExitStack() as es:
            act = SB(es, nc, "G_act", [128, 16, 2304], BF16)
            mT = SB(es, nc, "G_m", [128, 16, 2304], BF16)
            Tact, TmT = Tok(), Tok()
            for c in range(16):
                S.dma("sp", act[:, c, 0:sn], zr[:, c, s0:s0 + sn], writes=[Tact])
            tiles = [(t0 - s0, n, t0) for (t0, n) in split_tiles(s0, sn)]
            gt = [SB(es, nc, f"G_g{i}", [128, 3, 512], BF16) for i in range(3)]
            Tg = [Tok() for _ in range(3)]
            gi = [0]
            tf = Stage(nc, es, "G_tf", F32, 4)
            jobs = []

            def epi_merge(jb, ti, t0, n, pss):
                fo = jb["j"]
                i = gi[0]
                gi[0] = (i + 1) % 3
                for br in range(3):
                    r0 = br * 2048 + fo * 128
                    S.dma("sp", gt[i][:, br, 0:n], cx.gates[r0:r0 + 128, t0:t0 + n], writes=[Tg[i]])
                a, Ta = tf.next()
                b, Tb = tf.next()
                S.op("dve", I_tt(a[:, 0:n], pss[0][0][:, 0:n], gt[i][:, 0, 0:n], ALU.mult), reads=[pss[0][1], Tg[i]], writes=[Ta])
                S.op("dve", I_tt(b[:, 0:n], pss[1][0][:, 0:n], gt[i][:, 1, 0:n], ALU.mult), reads=[pss[1][1], Tg[i]], writes=[Tb])
                S.op("pool", I_tt(a[:, 0:n], a[:, 0:n], b[:, 0:n], ALU.add), reads=[Tb], writes=[Ta])
                S.op("dve", I_tt(b[:, 0:n], pss[2][0][:, 0:n], gt[i][:, 2, 0:n], ALU.mult), reads=[pss[2][1], Tg[i]], writes=[Tb])
                c0 = t0 - s0
                S.op("dve", I_tt(mT[:, fo, c0:c0 + n], a[:, 0:n], b[:, 0:n], ALU.add), reads=[Ta, Tb], writes=[TmT])

            for fo in range(16):
                cols = [dict(segs=[(0, Wa[:, fo * 128:(fo + 1) * 128])], kc0=0, KC=4),
                        dict(segs=[(0, Wb[:, fo * 128:(fo + 1) * 128])], kc0=4, KC=4),
                        dict(segs=[(0, Wc[:, fo * 128:(fo + 1) * 128])], kc0=8, KC=8)]
                jobs.append(dict(cols=cols, epi=epi_merge, j=fo))
            run_jobs2(cx, jobs, act, Tact, tiles, 8, NB=6, tag="G1")

            xt = [SB(es, nc, f"G_x{i}", [128, 512], F32) for i in range(3)]
            Tx = [Tok() for _ in range(3)]
            xi = [0]

            def epi_res(jb, ti, t0, n, pss):
                fo = jb["j"]
                grp = 1 if t0 < CTX else 0
                i = xi[0]
                xi[0] = (i + 1) % 3
                S.dma("sp", xt[i][:, 0:n], cx.xT[fo * 128:(fo + 1) * 128, t0:t0 + n], writes=[Tx[i]])
                S.op("dve", I_stt(xt[i][:, 0:n], pss[0][0][:, 0:n], mod_ap(cx, 2, fo, grp), xt[i][:, 0:n], ALU.mult, ALU.add),
                     reads=[pss[0][1], cx.T_mod], writes=[Tx[i]])
                S.dma("sp", cx.xT[fo * 128:(fo + 1) * 128, t0:t0 + n], xt[i][:, 0:n], reads=[Tx[i]])

            jobs = [dict(cols=[dict(segs=[(0, Wo[:, fo * 128:(fo + 1) * 128])], kc0=0, KC=16)], epi=epi_res, j=fo)
                    for fo in range(16)]
            run_jobs2(cx, jobs, mT, TmT, tiles, 16, NB=4, tag="G2")


def phase_J(cx, l):
    nc, S = cx.nc, cx.S
    hT_r = cx.hT.rearrange("(c p) t -> p c t", p=128)
    Wu, Wd = cx.I["w_up"][l], cx.I["w_down"][l]
    with_ctx = l < DEPTH - 1
    halves = [(0, 2304), (2304, 2048)] if with_ctx else [(CTX, 2048), (CTX + 2048, 2048)]
    with ExitStack() as es0:
        fw, Tfw = load_T(cx, es0, cx.I["conv_f_w"][l], 3, 2 * DFF, "J_fw")
        for (s0, sn) in halves:
            S.barrier()
            with ExitStack() as es:
                lo = s0 - 1 if s0 > CTX else s0
                hi = s0 + sn + 1 if s0 + sn < TA else s0 + sn
                W = hi - lo
                act = SB(es, nc, "J_act", [128, 16, 2306], BF16)
                Tact = Tok()
                for c in range(16):
                    S.dma("sp", act[:, c, 0:W], hT_r[:, c, lo:hi], writes=[Tact])
                tl = []
                if lo < s0:
                    tl.append((lo, 1))
                tl += split_tiles(s0, sn)
                if hi > s0 + sn:
                    tl.append((s0 + sn, 1))
                tiles = [(t0 - lo, n, t0) for (t0, n) in tl]
                ua = [SB(es, nc, f"J_ua{i}", [128, 2306], F32) for i in range(2)]
                uv = [SB(es, nc, f"J_uv{i}", [128, 2306], F32) for i in range(2)]
                ca = SB(es, nc, "J_ca", [128, 2306], F32)
                cv = SB(es, nc, "J_cv", [128, 2306], F32)
                fo_ = [SB(es, nc, f"J_f{i}", [128, 2306], BF16) for i in range(2)]
                Tua, Tuv = [Tok(), Tok()], [Tok(), Tok()]
                Tca, Tcv = Tok(), Tok()
                Tfo = [Tok(), Tok()]
                segs = []
                if s0 < CTX:
                    segs.append((0, CTX, False, False))
                    segs.append((CTX, s0 + sn, False, hi > s0 + sn))
                else:
                    segs.append((s0, s0 + sn, lo < s0, hi > s0 + sn))

                def conv3(eng, dst, Tdst, src, Tsrc, ch):
                    for (a, b, hl, hr) in segs:
                        ca0, cb0 = a - lo, b - lo
                        w0 = fw[:, ch * 3:ch * 3 + 1]
                        w1 = fw[:, ch * 3 + 1:ch * 3 + 2]
                        w2 = fw[:, ch * 3 + 2:ch * 3 + 3]
                        S.op(eng, I_ts(dst[:, ca0:cb0], src[:, ca0:cb0], w1, vec_ap(cx, "conv_f_b", ch), ALU.mult, ALU.add),
                             reads=[Tsrc, Tfw, cx.T_vec], writes=[Tdst])
                        ol = ca0 if hl else ca0 + 1
                        S.op(eng, I_stt(dst[:, ol:cb0], src[:, ol - 1:cb0 - 1], w0, dst[:, ol:cb0], ALU.mult, ALU.add),
                             reads=[Tsrc, Tfw], writes=[Tdst])
                        oh = cb0 if hr else cb0 - 1
                        S.op(eng, I_stt(dst[:, ca0:oh], src[:, ca0 + 1:oh + 1], w2, dst[:, ca0:oh], ALU.mult, ALU.add),
                             reads=[Tsrc, Tfw], writes=[Tdst])

                jobs = []
                ntile = len(tiles)

                def epi_up(jb, ti, t0, n, pss):
                    j = jb["j"]
                    b = j % 2
                    c0 = t0 - lo
                    S.op("act", I_act(ua[b][:, c0:c0 + n], pss[0][0][:, 0:n], AF.Identity), reads=[pss[0][1]], writes=[Tua[b]])
                    S.op("act", I_act(uv[b][:, c0:c0 + n], pss[1][0][:, 0:n], AF.Identity), reads=[pss[1][1]], writes=[Tuv[b]])
                    if ti == ntile - 1:
                        conv3("dve", ca, Tca, ua[b], Tua[b], j)
                        conv3("dve", cv, Tcv, uv[b], Tuv[b], 44 + j)
                        a0, b0 = s0 - lo, s0 - lo + sn
                        S.op("act", I_act(ca[:, a0:b0], ca[:, a0:b0], AF.Silu), reads=[Tca], writes=[Tca])
                        S.op("dve", I_tt(fo_[b][:, a0:b0], ca[:, a0:b0], cv[:, a0:b0], ALU.mult), reads=[Tca, Tcv],
                             writes=[Tfo[b]])
                        S.dma("sp", cx.fT[j * 128:(j + 1) * 128, s0:s0 + sn], fo_[b][:, a0:b0], reads=[Tfo[b]])

                for j in range(44):
                    cols = [dict(segs=[(0, Wu[:, j * 128:(j + 1) * 128])], kc0=0, KC=16),
                            dict(segs=[(0, Wu[:, (44 + j) * 128:(45 + j) * 128])], kc0=0, KC=16)]
                    jobs.append(dict(cols=cols, epi=epi_up, j=j))
                run_jobs2(cx, jobs, act, Tact, tiles, 16, NB=6, tag="J1")
    fr = cx.fT.rearrange("(c p) t -> p c t", p=128)
    quarters = [(i * 1088, 1088) for i in range(4)] if with_ctx else [(CTX + i * 1024, 1024) for i in range(4)]
    for (s0, sn) in quarters:
        S.barrier()
        with ExitStack() as es:
            act = SB(es, nc, "J2_act", [128, 44, 1088], BF16)
            Tact = Tok()
            for c in range(44):
                S.dma("sp", act[:, c, 0:sn], fr[:, c, s0:s0 + sn], writes=[Tact])
            tiles = [(t0 - s0, n, t0) for (t0, n) in split_tiles(s0, sn)]
            xt = [SB(es, nc, f"J2_x{i}", [128, 512], F32) for i in range(3)]
            Tx = [Tok() for _ in range(3)]
            xi = [0]

            def epi_res(jb, ti, t0, n, pss):
                fo = jb["j"]
                grp = 1 if t0 < CTX else 0
                i = xi[0]
                xi[0] = (i + 1) % 3
                S.dma("sp", xt[i][:, 0:n], cx.xT[fo * 128:(fo + 1) * 128, t0:t0 + n], writes=[Tx[i]])
                S.op("dve", I_stt(xt[i][:, 0:n], pss[0][0][:, 0:n], mod_ap(cx, 5, fo, grp), xt[i][:, 0:n], ALU.mult, ALU.add),
                     reads=[pss[0][1], cx.T_mod], writes=[Tx[i]])
                S.dma("sp", cx.xT[fo * 128:(fo + 1) * 128, t0:t0 + n], xt[i][:, 0:n], reads=[Tx[i]])

            jobs = [dict(cols=[dict(segs=[(0, Wd[:, fo * 128:(fo + 1) * 128])], kc0=0, KC=44)], epi=epi_res, j=fo)
                    for fo in range(16)]
            run_jobs2(cx, jobs, act, Tact, tiles, 44, NB=3, tag="J2")


def phase_final(cx):
    nc, S = cx.nc, cx.S
    xT_r = cx.xT.rearrange("(c p) t -> p c t", p=128)
    with ExitStack() as es:
        gb = SB(es, nc, "F_gb", [128, D], F32)
        Tgb = Tok()
        S.dma("sp", gb[:], cx.I["final_g"].to_broadcast([128, D]), writes=[Tgb])
        xb = [SB(es, nc, f"F_x{i}", [128, 16, 128], F32) for i in range(2)]
        ob = [SB(es, nc, f"F_o{i}", [128, D], F32) for i in range(2)]
        Tx, To = [Tok(), Tok()], [Tok(), Tok()]
        st = SB(es, nc, "F_st", [128, 8], F32)
        Tst = Tok()
        junk = SB(es, nc, "F_junk", [128, 512], F32)
        Tj = Tok()
        for tb in range(SEQ // 128):
            b = tb % 2
            t0 = CTX + tb * 128
            S.dma("sp", xb[b][:], xT_r[:, :, t0:t0 + 128], writes=[Tx[b]])
            pss = []
            for g4 in range(4):
                ps, Tp = next_ps(cx)
                for j in range(4):
                    c = g4 * 4 + j
                    S.op("pe", I_tr(ps[:, j * 128:(j + 1) * 128], xb[b][:, c, :], cx.ident[:]), reads=[Tx[b], cx.T_const],
                         writes=[Tp])
                S.op("act", I_act(junk[:], ps[:], AF.Square, accum_out=st[:, g4:g4 + 1]), reads=[Tp], writes=[Tj, Tst])
                pss.append((ps, Tp))
            S.op("dve", I_red(st[:, 4:5], st[:, 0:4], "sum"), reads=[Tst], writes=[Tst])
            S.op("act", I_act(st[:, 5:6], st[:, 4:5], AF.Sqrt, bias=EPS, scale=1.0 / D), reads=[Tst], writes=[Tst])
            S.op("dve", I_recip(st[:, 5:6], st[:, 5:6]), reads=[Tst], writes=[Tst])
            for g4 in range(4):
                ps, Tp = pss[g4]
                S.op("dve", I_stt(ob[b][:, g4 * 512:(g4 + 1) * 512], ps[:], st[:, 5:6], gb[:, g4 * 512:(g4 + 1) * 512],
                                  ALU.mult, ALU.mult), reads=[Tp, Tst, Tgb], writes=[To[b]])
            S.dma("sp", cx.out[tb * 128:(tb + 1) * 128, :], ob[b][:], reads=[To[b]])


def host_consts():
    c = {}
    c["ident"] = np.eye(128, dtype=np.float32)
    rows = SEQ // 64
    row = np.repeat(np.arange(rows), 64).astype(np.float32)
    col = np.tile(np.arange(64), rows).astype(np.float32)
    nf = 16
    inv = (10000.0 ** (-np.arange(nf, dtype=np.float32) / nf)).astype(np.float32)
    ang = np.concatenate([row[:, None] * inv, col[:, None] * inv], -1)
    cos = np.cos(ang).astype(np.float32).T
    sin = np.sin(ang).astype(np.float32).T
    rc = np.ones((128, TA), np.float32)
    rs = np.zeros((128, TA), np.float32)
    for m in range(2):
        rc[m * 64:m * 64 + 32, CTX:] = cos
        rc[m * 64 + 32:m * 64 + 64, CTX:] = cos
        rs[m * 64:m * 64 + 32, CTX:] = -sin
        rs[m * 64 + 32:m * 64 + 64, CTX:] = sin
    c["ropec"] = rc
    c["ropes"] = rs
    feats = np.zeros((17, TA), np.float32)
    decay = np.zeros((TA, 512), np.float32)
    decayb = np.zeros((TA, 512), np.float32)
    deltas = np.abs(np.linspace(math.log(1e-2) / 1.5, math.log(1e-2) / 0.3, 512, dtype=np.float32))
    for (t0, L) in ((0, CTX), (CTX, SEQ)):
        pos = np.arange(L, dtype=np.float32)
        t = pos / max(L - 1, 1)
        bands = np.linspace(1e-4, 7, 8, dtype=np.float32)
        ang = (2.0 * math.pi * pos / L)[:, None] * bands[None]
        f = np.concatenate([t[:, None], np.cos(ang), -np.sin(ang)], -1).astype(np.float32)
        feats[:, t0:t0 + L] = f.T
        dk = np.exp(-t[:, None] * deltas[None]).astype(np.float32)
        decay[t0:t0 + L] = dk
        decayb[t0:t0 + L] = dk
        decayb[t0] = 0.0
    c["hy_feats"] = feats
    c["hy_decay"] = decay
    c["hy_decayb"] = decayb

    def dft_tables(N, npad):
        half = N // 2
        a = np.arange(npad, dtype=np.int64)
        prod = (a[:, None] * a[None, :]) % N
        angm = prod.astype(np.float64) * (2.0 * math.pi / N)
        valid = (a[:, None] <= half) & (a[None, :] <= half)
        tc = np.where(valid, np.cos(angm), 0.0).astype(np.float32).astype(ml_dtypes.bfloat16)
        ts = np.where(valid, np.sin(angm), 0.0).astype(np.float32).astype(ml_dtypes.bfloat16)
        return tc, ts

    c["TC"], c["TS"] = dft_tables(2 * SEQ, NF)
    c["TCc"], c["TSc"] = dft_tables(2 * CTX, NFC)
    wf = np.zeros((128, 72), np.float32)
    for (col0, N, nfb) in ((0, 2 * SEQ, 33), (33, 2 * CTX, 3)):
        for fb in range(nfb):
            f = fb * 128 + np.arange(128)
            w = np.where(f > N // 2, 0.0, np.where((f == 0) | (f == N // 2), 1.0 / N, 2.0 / N)).astype(np.float32)
            wf[:, col0 + fb] = w
            wf[:, 36 + col0 + fb] = -w
    c["hy_wf"] = wf
    return c


def pack_vecs(inp):
    v = np.zeros((DEPTH, NVR, 12288), np.float32)
    for l in range(DEPTH):
        for r, name in enumerate(VEC_ROWS):
            a = np.asarray(inp[name][l], np.float32).reshape(-1)
            v[l, r, :a.size] = a
    return v


def make_in_maps(inp, ncores=8):
    consts = host_consts()
    vecs = pack_vecs(inp)
    shared = {k: np.ascontiguousarray(np.asarray(inp[k], np.float32)) for k in
              ["w_ada", "w_in", "w_a_out", "w_b_out", "w_c_out", "w_o", "w_up", "w_down", "filt_w1", "filt_w2",
               "filt_w3", "conv_a_w", "short_b_w", "conv_f_w"]}
    shared["vecs"] = vecs
    shared["diff_lambda"] = np.asarray(inp["diff_lambda"], np.float32).reshape(DEPTH, 1, 256)
    shared["subln_g"] = np.asarray(inp["subln_g"], np.float32).reshape(DEPTH, 1, 1024)
    shared["final_g"] = np.asarray(inp["final_g"], np.float32).reshape(1, D)
    shared.update(consts)
    maps = []
    for core in range(ncores):
        b = core % 4
        m = dict(shared)
        m["x"] = np.ascontiguousarray(np.asarray(inp["x"][b], np.float32))
        m["ctx"] = np.ascontiguousarray(np.asarray(inp["ctx"][b], np.float32))
        m["c2"] = np.stack([np.asarray(inp["c"][b], np.float32), np.asarray(inp["c_ctx"], np.float32)], 0)
        maps.append(m)
    return maps


NCORES = 4


def kernel(**inputs):
    nc, cx = build()
    maps = make_in_maps(inputs, ncores=NCORES)
    res = run_bass_kernel_spmd(nc, maps, core_ids=list(range(NCORES)))
    out = np.stack([np.asarray(res.results[b]["out"]) for b in range(4)], 0)
    return out.astype(np.float32)
```

```python
import math
from contextlib import ExitStack
import numpy as np
import ml_dtypes
import concourse.bass as bass
import concourse.mybir as mybir
from concourse.bass_utils import run_bass_kernel_spmd

F32 = mybir.dt.float32
BF16 = mybir.dt.bfloat16
AF = mybir.ActivationFunctionType
ALU = mybir.AluOpType
AX = mybir.AxisListType

D = 2048
SEQ = 4096
CTX = 256
TA = CTX + SEQ
DEPTH = 2
NIN = 11776
DFF = 5632
EPS = 1e-6
OFF_A, OFF_G, OFF_B, OFF_Q, OFF_K, OFF_V, OFF_GATE = 0, 512, 1024, 2560, 3584, 4608, 5632
TILES = [(0, 256)] + [(256 + 512 * i, 512) for i in range(8)]
NF = 4224
NFC = 384

ENGS = ("pe", "act", "dve", "pool", "sp")


class Tok:
    __slots__ = ("w", "r", "excl")

    def __init__(self, excl=False):
        self.w = []
        self.r = []
        self.excl = excl


class Sched:
    LIMIT = 32000

    def __init__(self, nc, es, n_dma_sems=8):
        self.nc = nc
        self.es = es
        self.lists = {e: [] for e in ENGS}
        self.count = {e: 0 for e in ENGS}
        self.gen = {e: 0 for e in ENGS}
        self.known = {e: {} for e in ENGS}
        self.semobj = {}
        self.nsem = 0
        for e in ENGS:
            self.semobj[("e", e, 0)] = self._newsem()
        self.dkeys, self.dnext, self.dval, self.dgen = {}, {}, {}, {}
        for q in ("sp", "pool", "act"):
            self.dkeys[q] = []
            for i in range(n_dma_sems):
                key = ("d", q, i, 0)
                self.semobj[key] = self._newsem()
                self.dkeys[q].append(key)
            self.dnext[q] = 0
            self.dval[q] = [0] * n_dma_sems
            self.dgen[q] = [0] * n_dma_sems
        self.final = {}

    def _newsem(self):
        self.nsem += 1
        return self.es.enter_context(self.nc.semaphore(f"sm{self.nsem}"))

    def _wait(self, eng, ev):
        if ev is None:
            return
        key, val = ev
        if key[0] == "e" and key[1] == eng and eng == "pe":
            return
        kn = self.known[eng]
        if kn.get(key, 0) >= val:
            return
        kn[key] = val
        self.lists[eng].append(("wait", key, val))

    def _deps(self, eng, reads, writes, acc=False):
        for t in reads:
            for ev in t.w:
                self._wait(eng, ev)
        for t in writes:
            if not acc:
                for ev in t.w:
                    self._wait(eng, ev)
            for ev in t.r:
                self._wait(eng, ev)

    def _record(self, ev, reads, writes, acc=False):
        for t in reads:
            t.r.append(ev)
        for t in writes:
            if acc and not t.r:
                t.w.append(ev)
            else:
                t.w = [ev]
                t.r = []

    def op(self, eng, fn, reads=(), writes=()):
        if any(t.excl for t in reads):
            writes = list(writes) + [t for t in reads if t.excl and t not in writes]
            reads = [t for t in reads if not t.excl]
        self._deps(eng, reads, writes)
        if self.count[eng] >= self.LIMIT:
            self.final[("e", eng, self.gen[eng])] = self.count[eng]
            self.gen[eng] += 1
            self.count[eng] = 0
            self.semobj[("e", eng, self.gen[eng])] = self._newsem()
        self.count[eng] += 1
        key = ("e", eng, self.gen[eng])
        ev = (key, self.count[eng])
        self.lists[eng].append(("op", fn, key))
        self._record(ev, reads, writes)
        return ev

    def dma(self, q, out, in_, reads=(), writes=(), acc=False):
        self._deps(q, reads, writes, acc)
        i = self.dnext[q]
        self.dnext[q] = (i + 1) % len(self.dkeys[q])
        key = self.dkeys[q][i]
        if self.dval[q][i] > 0:
            self._wait(q, (key, self.dval[q][i]))
        if self.dval[q][i] >= self.LIMIT:
            self.final[key] = self.dval[q][i]
            self.dgen[q][i] += 1
            key = ("d", q, i, self.dgen[q][i])
            self.semobj[key] = self._newsem()
            self.dkeys[q][i] = key
            self.dval[q][i] = 0
        self.dval[q][i] += 16
        ev = (key, self.dval[q][i])
        self.lists[q].append(("dma", out, in_, key))
        self._record(ev, reads, writes, acc)
        return ev

    def _all_events(self):
        evs = [(("e", e, self.gen[e]), self.count[e]) for e in ENGS if self.count[e] > 0]
        for q in self.dkeys:
            for i, v in enumerate(self.dval[q]):
                if v > 0:
                    evs.append((self.dkeys[q][i], v))
        evs += list(self.final.items())
        return evs

    def barrier(self, engines=ENGS):
        evs = self._all_events()
        for e in engines:
            kn = self.known[e]
            for key, val in evs:
                if kn.get(key, 0) < val:
                    kn[key] = val
                    self.lists[e].append(("wait", key, val))

    def emit(self, block):
        lists, semobj = self.lists, self.semobj

        def replay(engname, e):
            for item in lists[engname]:
                k = item[0]
                if k == "wait":
                    e.wait_ge(semobj[item[1]], item[2])
                elif k == "op":
                    item[1](e).then_inc(semobj[item[2]], 1)
                else:
                    e.dma_start(out=item[1], in_=item[2]).then_inc(semobj[item[3]], 16)

        @block.tensor
        def _(e):
            replay("pe", e)

        @block.scalar
        def _(e):
            replay("act", e)

        @block.vector
        def _(e):
            replay("dve", e)

        @block.gpsimd
        def _(e):
            replay("pool", e)

        @block.sync
        def _(e):
            replay("sp", e)


def I_mm(out, lhsT, rhs, start, stop):
    return lambda e: e.matmul(out, lhsT, rhs, start=start, stop=stop)


def I_tr(out, in_, ident):
    return lambda e: e.transpose(out, in_, ident)


def I_act(out, in_, func, bias=None, scale=None, accum_out=None):
    kw = {}
    if bias is not None:
        kw["bias"] = bias
    if scale is not None:
        kw["scale"] = scale
    if accum_out is not None:
        kw["accum_out"] = accum_out
    return lambda e: e.activation(out=out, in_=in_, func=func, **kw)


def I_copy(out, in_):
    return lambda e: e.tensor_copy(out=out, in_=in_)


def I_tt(out, a, b, op):
    return lambda e: e.tensor_tensor(out=out, in0=a, in1=b, op=op)


def I_ts(out, a, s1, s2, op0, op1=None):
    if op1 is None:
        return lambda e: e.tensor_scalar(out=out, in0=a, scalar1=s1, scalar2=None, op0=op0)
    return lambda e: e.tensor_scalar(out=out, in0=a, scalar1=s1, scalar2=s2, op0=op0, op1=op1)


def I_stt(out, a, s, b, op0, op1):
    return lambda e: e.scalar_tensor_tensor(out=out, in0=a, scalar=s, in1=b, op0=op0, op1=op1)


def I_memset(out, v):
    return lambda e: e.memset(out, v)


def I_recip(out, in_):
    return lambda e: e.reciprocal(out=out, in_=in_)


def I_red(out, in_, op):
    if op == "max":
        return lambda e: e.reduce_max(out=out, in_=in_, axis=AX.X)
    return lambda e: e.reduce_sum(out=out, in_=in_, axis=AX.X)


_UID = [0]


def SB(es, nc, name, shape, dt):
    _UID[0] += 1
    return es.enter_context(nc.sbuf_tensor(f"{name}_u{_UID[0]}", shape, dt))


class Cx:
    pass


HOPT = {}
USE_H2 = True


def build(dbg=(), stop_after=None, n_layers=DEPTH, only=None, feed=(), hopt=None):
    global HOPT
    HOPT = hopt or {}
    nc = bass.Bass("TRN2", target_bir_lowering=False)
    cx = Cx()
    cx.nc = nc
    cx.dbg_out = {}

    def din(name, shape, dt=F32):
        return nc.dram_tensor(name, list(shape), dt, kind="ExternalInput").ap()

    def scratch(name, shape, dt):
        if name in feed:
            return din(name, shape, dt)
        if name in dbg:
            ap = nc.dram_tensor(name, list(shape), dt, kind="ExternalOutput").ap()
            cx.dbg_out[name] = ap
            return ap
        return nc.dram_tensor(name, list(shape), dt).ap()

    specs = {
        "x": ([SEQ, D], F32), "ctx": ([CTX, D], F32), "c2": ([2, D], F32),
        "w_ada": ([DEPTH, D, 6 * D], F32), "w_in": ([DEPTH, D, NIN], F32),
        "w_a_out": ([DEPTH, 512, D], F32), "w_b_out": ([DEPTH, 512, D], F32), "w_c_out": ([DEPTH, 1024, D], F32),
        "w_o": ([DEPTH, D, D], F32), "w_up": ([DEPTH, D, 2 * DFF], F32), "w_down": ([DEPTH, DFF, D], F32),
        "filt_w1": ([DEPTH, 17, 64], F32), "filt_w2": ([DEPTH, 64, 64], F32), "filt_w3": ([DEPTH, 64, 2048], F32),
        "vecs": ([DEPTH, NVR, 12288], F32), "conv_a_w": ([DEPTH, 31, 512], F32), "short_b_w": ([DEPTH, 3, 1536], F32),
        "conv_f_w": ([DEPTH, 3, 2 * DFF], F32), "diff_lambda": ([DEPTH, 1, 256], F32), "subln_g": ([DEPTH, 1, 1024], F32),
        "final_g": ([1, D], F32), "ident": ([128, 128], F32), "ropec": ([128, TA], F32), "ropes": ([128, TA], F32),
        "hy_feats": ([17, TA], F32), "hy_decay": ([TA, 512], F32), "hy_decayb": ([TA, 512], F32), "hy_wf": ([128, 72], F32),
        "TC": ([NF, NF], BF16), "TS": ([NF, NF], BF16), "TCc": ([NFC, NFC], BF16), "TSc": ([NFC, NFC], BF16),
        "TCf": ([33, 128, 33, 128], BF16), "TSf": ([33, 128, 33, 128], BF16),
        "TCcf": ([3, 128, 3, 128], BF16), "TScf": ([3, 128, 3, 128], BF16),
    }

    class LazyI(dict):
        def __missing__(self, name):
            shape, dt = specs[name]
            ap = din(name, shape, dt)
            self[name] = ap
            return ap

    I = LazyI()
    if only is None:
        for name in specs:
            I[name]
    out = nc.dram_tensor("out", [SEQ, D], F32, kind="ExternalOutput").ap()
    cx.I = I
    cx.out = out

    cx.xT = scratch("xT", [D, TA], F32)
    cx.hT = scratch("hT", [D, TA], BF16)
    cx.yA = scratch("yA", [512, TA], F32)
    cx.bT = scratch("bT", [1536, TA], F32)
    cx.qT = scratch("qT", [1024, TA], BF16)
    cx.kT = scratch("kT", [1024, TA], BF16)
    cx.V = scratch("V", [TA, 1024], BF16)
    cx.gates = scratch("gates", [6144, TA], BF16)
    cx.zcat = scratch("zcat", [2048, TA], BF16)
    cx.fT = scratch("fT", [DFF, TA], BF16)
    cx.modd = scratch("modd", [128, 96 * 2], F32)
    cx.ubuf = scratch("ubuf", [1536, TA], F32)
    cx.hpm = scratch("hpm", [TA, 2048], BF16)
    cx.Hd = scratch("Hd", [NF + NFC, 2048], F32)
    cx.z1 = scratch("z1", [512, TA], F32)

    with ExitStack() as es:
        S = Sched(nc, es)
        cx.S = S
        cx.ident = nc.alloc_sbuf_tensor("sb_ident", [128, 128], F32)
        cx.identb = nc.alloc_sbuf_tensor("sb_identb", [128, 128], BF16)
        cx.ones_b = nc.alloc_sbuf_tensor("sb_ones_b", [128, 128], BF16)
        cx.ones_f = nc.alloc_sbuf_tensor("sb_ones_f", [128, 128], F32)
        cx.T_const = Tok()
        cx.epsc = nc.alloc_sbuf_tensor("sb_epsc", [128, 1], F32)
        cx.mod = nc.alloc_sbuf_tensor("sb_mod", [128, 6 * 16 * 2], F32)
        cx.modA = nc.alloc_sbuf_tensor("sb_modA", [128, 2 * 16 * 2], F32)
        cx.T_mod = Tok()
        cx.vec = nc.alloc_sbuf_tensor("sb_vec", [128, 96 * NVR], F32)
        cx.T_vec = Tok()
        cx.ps = [nc.alloc_psum_tensor(f"ps{i}", [128, 512], F32) for i in range(8)]
        cx.T_ps = [Tok(excl=True) for _ in range(8)]
        cx.ps_next = 0
        cx.ps_reserved = set()

        S.dma("sp", cx.ident[:], I["ident"][:, :], writes=[cx.T_const])
        S.op("dve", I_copy(cx.identb[:], cx.ident[:]), reads=[cx.T_const], writes=[cx.T_const])
        S.op("dve", I_memset(cx.ones_b[:], 1.0), writes=[cx.T_const])
        S.op("dve", I_memset(cx.ones_f[:], 1.0), writes=[cx.T_const])
        S.op("dve", I_memset(cx.epsc[:], EPS), writes=[cx.T_const])

        phases = [("A", phase_A, None)]
        for l in range(n_layers):
            phases += [("vec%d" % l, phase_vec, l), ("B%d" % l, phase_B, l), ("C%d" % l, phase_norm, (l, 0)),
                       ("D%d" % l, phase_D, l), ("E%d" % l, phase_E, l), ("F%d" % l, phase_F, l), ("H%d" % l, phase_H2 if USE_H2 else phase_H, l),
                       ("G%d" % l, phase_G, l), ("I%d" % l, phase_norm, (l, 1)), ("J%d" % l, phase_J, l)]
        phases.append(("final", phase_final, None))
        if only is not None:
            phases = [p for p in phases if p[0] in only]
        for name, fn, arg in phases:
            S.barrier()
            if arg is None:
                fn(cx)
            else:
                fn(cx, arg)
            if stop_after == name:
                break
        S.barrier()
        with nc.Block() as block:
            S.emit(block)
    return nc, cx


def next_ps(cx):
    while True:
        i = cx.ps_next
        cx.ps_next = (i + 1) % 8
        if i not in cx.ps_reserved:
            return cx.ps[i], cx.T_ps[i]


VEC_ROWS = ["b_ada", "norm1_g", "norm2_g", "b_gate", "conv_a_b", "ln_a_g", "ln_a_b", "short_b_b",
            "hyena_skip", "conv_f_b", "filt_b1", "filt_b2"]
NVR = len(VEC_ROWS)


def vec_ap(cx, row, chunk):
    r = VEC_ROWS.index(row)
    o = chunk * NVR + r
    return cx.vec[:, o:o + 1]


def phase_A(cx):
    nc, S = cx.nc, cx.S
    xT_r = cx.xT.rearrange("(c p) t -> p c t", p=128)
    with ExitStack() as es:
        xin = [SB(es, nc, f"A_xin{i}", [128, D], F32) for i in range(2)]
        xo = [SB(es, nc, f"A_xo{i}", [128, D], F32) for i in range(2)]
        Tin = [Tok(), Tok()]
        To = [Tok(), Tok()]
        for bi in range(TA // 128):
            b = bi % 2
            src = cx.I["ctx"][bi * 128:(bi + 1) * 128, :] if bi < 2 else cx.I["x"][(bi - 2) * 128:(bi - 1) * 128, :]
            S.dma("sp", xin[b][:], src, writes=[Tin[b]])
            for g4 in range(4):
                ps, Tp = next_ps(cx)
                for j in range(4):
                    c = g4 * 4 + j
                    S.op("pe", I_tr(ps[:, j * 128:(j + 1) * 128], xin[b][:, c * 128:(c + 1) * 128], cx.ident[:]),
                         reads=[Tin[b], cx.T_const], writes=[Tp])
                eng = "dve" if g4 % 2 == 0 else "act"
                if eng == "dve":
                    S.op("dve", I_copy(xo[b][:, g4 * 512:(g4 + 1) * 512], ps[:]), reads=[Tp], writes=[To[b]])
                else:
                    S.op("act", I_act(xo[b][:, g4 * 512:(g4 + 1) * 512], ps[:], AF.Identity), reads=[Tp], writes=[To[b]])
            S.dma("pool", xT_r[:, :, bi * 128:(bi + 1) * 128], xo[b][:].rearrange("p (c t) -> p c t", c=16),
                  reads=[To[b]])


def phase_vec(cx, l):
    nc, S = cx.nc, cx.S
    with ExitStack() as es:
        raw = SB(es, nc, "V_raw", [NVR, 12288], F32)
        Traw = Tok()
        S.dma("sp", raw[:], cx.I["vecs"][l], writes=[Traw])
        for g in range(96 // 16):
            ps, Tp = next_ps(cx)
            for j in range(16):
                ch = g * 16 + j
                S.op("pe", I_tr(ps[:, j * NVR:(j + 1) * NVR], raw[:, ch * 128:(ch + 1) * 128], cx.ident[0:NVR, 0:NVR]),
                     reads=[Traw, cx.T_const], writes=[Tp])
            S.op("dve", I_copy(cx.vec[:, g * 16 * NVR:(g + 1) * 16 * NVR], ps[:, 0:16 * NVR]), reads=[Tp],
                 writes=[cx.T_vec])


def load_w_chunk(cx, wbuf, Tw, segs, K):
    S = cx.S
    KC = K // 128
    for (dc, ap) in segs:
        n = ap.shape[1]
        S.dma("pool", wbuf[:, 0:KC, dc:dc + n], ap.rearrange("(kc p) n -> p kc n", p=128), writes=[Tw], acc=True)


def phase_B(cx, l):
    nc, S = cx.nc, cx.S
    with ExitStack() as es:
        craw = SB(es, nc, "B_craw", [2, D], F32)
        cs = SB(es, nc, "B_cs", [128, 32], BF16)
        csf = SB(es, nc, "B_csf", [128, 32], F32)
        NB = 4
        wb = [SB(es, nc, f"B_w{i}", [128, 16, 128], BF16) for i in range(NB)]
        Tw = [Tok() for _ in range(NB)]
        Tc = Tok()
        S.dma("sp", craw[:], cx.I["c2"][:, :], writes=[Tc])
        for g in range(2):
            ps, Tp = next_ps(cx)
            for j in range(8):
                ch = g * 8 + j
                S.op("pe", I_tr(ps[:, j * 2:(j + 1) * 2], craw[:, ch * 128:(ch + 1) * 128], cx.ident[0:2, 0:2]),
                     reads=[Tc, cx.T_const], writes=[Tp])
            S.op("act", I_act(csf[:, g * 16:(g + 1) * 16], ps[:, 0:16], AF.Silu), reads=[Tp], writes=[Tc])
        S.op("dve", I_copy(cs[:], csf[:]), reads=[Tc], writes=[Tc])
        wsrc = cx.I["w_ada"][l]
        nj = 96

        def issue_load(j):
            load_w_chunk(cx, wb[j % NB], Tw[j % NB], [(0, wsrc[:, j * 128:(j + 1) * 128])], D)

        for j in range(min(NB - 1, nj)):
            issue_load(j)
        ps, Tp = None, None
        for j in range(nj):
            if j + NB - 1 < nj:
                issue_load(j + NB - 1)
            ps, Tp = next_ps(cx)
            o = 0
            for kc in range(16):
                S.op("pe", I_mm(ps[:, o:o + 2], wb[j % NB][:, kc, :], cs[:, kc * 2:(kc + 1) * 2], kc == 0, kc == 15),
                     reads=[Tw[j % NB], Tc], writes=[Tp])
            S.op("act", I_act(cx.mod[:, j * 2:(j + 1) * 2], ps[:, o:o + 2], AF.Identity, bias=vec_ap(cx, "b_ada", j)),
                 reads=[Tp, cx.T_vec], writes=[cx.T_mod])
        for sub in range(2):
            gname = "norm1_g" if sub == 0 else "norm2_g"
            which_scale = 1 + 3 * sub
            for c in range(16):
                S.op("dve", I_ts(cx.modA[:, (sub * 16 + c) * 2:(sub * 16 + c) * 2 + 2],
                                 cx.mod[:, (which_scale * 16 + c) * 2:(which_scale * 16 + c) * 2 + 2],
                                 1.0, vec_ap(cx, gname, c), ALU.add, ALU.mult),
                     reads=[cx.T_mod, cx.T_vec], writes=[cx.T_mod])
        if "modd" in cx.dbg_out:
            S.dma("sp", cx.modd[:, :], cx.mod[:], reads=[cx.T_mod])


def mod_ap(cx, which, c, grp):
    o = (which * 16 + c) * 2 + grp
    return cx.mod[:, o:o + 1]


def modA_ap(cx, sub, c, grp):
    o = (sub * 16 + c) * 2 + grp
    return cx.modA[:, o:o + 1]


def phase_norm(cx, arg):
    l, sub = arg
    nc, S = cx.nc, cx.S
    xT_r = cx.xT.rearrange("(c p) t -> p c t", p=128)
    hT_r = cx.hT.rearrange("(c p) t -> p c t", p=128)
    with ExitStack() as es:
        xb = [SB(es, nc, f"N_x{i}", [128, 16, 512], F32) for i in range(2)]
        hb = [SB(es, nc, f"N_h{i}", [128, 16, 512], BF16) for i in range(2)]
        sq = SB(es, nc, "N_sq", [128, 16, 512], BF16)
        rs = SB(es, nc, "N_rs", [128, 512], F32)
        tmp = [SB(es, nc, f"N_tmp{i}", [128, 512], F32) for i in range(2)]
        Tx = [Tok(), Tok()]
        Th = [Tok(), Tok()]
        Tsq, Trs = Tok(), Tok()
        Ttmp = [Tok(), Tok()]
        for ti, (t0, n) in enumerate(TILES):
            b = ti % 2
            grp = 1 if t0 < CTX else 0
            S.dma("sp", xb[b][:, :, 0:n], xT_r[:, :, t0:t0 + n], writes=[Tx[b]])
            for c in range(16):
                S.op("act", I_act(sq[:, c, 0:n], xb[b][:, c, 0:n], AF.Square), reads=[Tx[b]], writes=[Tsq])
            ps, Tp = next_ps(cx)
            for c in range(16):
                S.op("pe", I_mm(ps[:, 0:n], cx.ones_b[:], sq[:, c, 0:n], c == 0, c == 15), reads=[Tsq, cx.T_const],
                     writes=[Tp])
            S.op("act", I_act(rs[:, 0:n], ps[:, 0:n], AF.Sqrt, bias=EPS, scale=1.0 / D), reads=[Tp], writes=[Trs])
            S.op("dve", I_recip(rs[:, 0:n], rs[:, 0:n]), reads=[Trs], writes=[Trs])
            for c in range(16):
                tb = c % 2
                S.op("dve", I_tt(tmp[tb][:, 0:n], xb[b][:, c, 0:n], rs[:, 0:n], ALU.mult), reads=[Tx[b], Trs],
                     writes=[Ttmp[tb]])
                S.op("act", I_act(hb[b][:, c, 0:n], tmp[tb][:, 0:n], AF.Identity,
                                  bias=mod_ap(cx, 3 * sub, c, grp), scale=modA_ap(cx, sub, c, grp)),
                     reads=[Ttmp[tb], cx.T_mod], writes=[Th[b]])
            S.dma("act", hT_r[:, :, t0:t0 + n], hb[b][:, :, 0:n], reads=[Th[b]])


def run_jobs(cx, jobs, act, Tact, tiles, K, NB=6, tag="J"):
    nc, S = cx.nc, cx.S
    KC = K // 128
    with ExitStack() as es:
        wb = [SB(es, nc, f"{tag}_w{i}", [128, KC, 128], BF16) for i in range(NB)]
        Tw = [Tok() for _ in range(NB)]
        flat = []
        for ji, jb in enumerate(jobs):
            for ci in range(len(jb["cols"])):
                flat.append((ji, ci))
        slot = {}

        def issue(fi):
            ji, ci = flat[fi]
            s = fi % NB
            slot[(ji, ci)] = s
            load_w_chunk(cx, wb[s], Tw[s], jobs[ji]["cols"][ci], K)

        nxt = 0
        fpos = 0
        for ji, jb in enumerate(jobs):
            ncol = len(jb["cols"])
            assert ncol <= NB
            while nxt < len(flat) and nxt < fpos + NB:
                issue(nxt)
                nxt += 1
            for ti, (c0, n, t0) in enumerate(tiles):
                pss = []
                for ci in range(ncol):
                    s = slot[(ji, ci)]
                    ps, Tp = next_ps(cx)
                    for kc in range(KC):
                        S.op("pe", I_mm(ps[:, 0:n], wb[s][:, kc, :], act[:, kc, c0:c0 + n], kc == 0, kc == KC - 1),
                             reads=[Tw[s], Tact], writes=[Tp])
                    pss.append((ps, Tp))
                jb["epi"](jb, ti, t0, n, pss)
            fpos += ncol
        S.barrier()


class Stage:
    def __init__(self, nc, es, name, dt, n):
        self.t = [SB(es, nc, f"{name}{i}", [128, 512], dt) for i in range(n)]
        self.T = [Tok() for _ in range(n)]
        self.i = 0

    def next(self):
        i = self.i
        self.i = (i + 1) % len(self.t)
        return self.t[i], self.T[i]


def phase_D(cx, l):
    nc, S = cx.nc, cx.S
    W = cx.I["w_in"][l]
    hT_r = cx.hT.rearrange("(c p) t -> p c t", p=128)
    halves = [(0, 2304), (2304, 2048)]
    for (s0, sn) in halves:
        S.barrier()
        with ExitStack() as es:
            act = SB(es, nc, "D_act", [128, 16, 2304], BF16)
            Tact = Tok()
            rc = SB(es, nc, "D_rc", [128, 2304], F32)
            rsn = SB(es, nc, "D_rs", [128, 2304], F32)
            Trope = Tok()
            for c in range(16):
                S.dma("sp", act[:, c, 0:sn], hT_r[:, c, s0:s0 + sn], writes=[Tact], acc=True)
            S.dma("sp", rc[:, 0:sn], cx.I["ropec"][:, s0:s0 + sn], writes=[Trope], acc=True)
            S.dma("sp", rsn[:, 0:sn], cx.I["ropes"][:, s0:s0 + sn], writes=[Trope], acc=True)
            tiles = [(t0 - s0, n, t0) for (t0, n) in TILES if s0 <= t0 < s0 + sn]
            sf = Stage(nc, es, "D_sf", F32, 4)
            sb = Stage(nc, es, "D_sb", BF16, 4)
            st = Stage(nc, es, "D_st", F32, 4)

            def col(c0):
                return [(0, W[:, c0:c0 + 128])]

            def col_swapped(c0):
                return [(0, W[:, c0 + 32:c0 + 64]), (32, W[:, c0:c0 + 32]),
                        (64, W[:, c0 + 96:c0 + 128]), (96, W[:, c0 + 64:c0 + 96])]

            jobs = []

            def epi_glu(jb, ti, t0, n, pss):
                (pa, Ta), (pg, Tg) = pss
                t1, T1 = st.next()
                S.op("act", I_act(t1[:, 0:n], pg[:, 0:n], AF.Sigmoid), reads=[Tg], writes=[T1])
                o, To = sf.next()
                S.op("dve", I_tt(o[:, 0:n], pa[:, 0:n], t1[:, 0:n], ALU.mult), reads=[Ta, T1], writes=[To])
                r0 = jb["j"] * 128
                S.dma("pool", cx.yA[r0:r0 + 128, t0:t0 + n], o[:, 0:n], reads=[To])

            for j in range(4):
                jobs.append(dict(cols=[col(OFF_A + j * 128), col(OFF_G + j * 128)], epi=epi_glu, j=j))

            def epi_b(jb, ti, t0, n, pss):
                (p, Tp), = pss
                o, To = sf.next()
                S.op("act", I_act(o[:, 0:n], p[:, 0:n], AF.Identity), reads=[Tp], writes=[To])
                r0 = jb["j"] * 128
                S.dma("pool", cx.bT[r0:r0 + 128, t0:t0 + n], o[:, 0:n], reads=[To])

            for j in range(12):
                jobs.append(dict(cols=[col(OFF_B + j * 128)], epi=epi_b, j=j))

            def epi_rope(jb, ti, t0, n, pss):
                (p, Tp), (psw, Tsw) = pss
                c0 = t0 - s0
                t1, T1 = st.next()
                t2, T2 = st.next()
                S.op("dve", I_tt(t1[:, 0:n], p[:, 0:n], rc[:, c0:c0 + n], ALU.mult), reads=[Tp, Trope], writes=[T1])
                S.op("dve", I_tt(t2[:, 0:n], psw[:, 0:n], rsn[:, c0:c0 + n], ALU.mult), reads=[Tsw, Trope], writes=[T2])
                o, To = sb.next()
                S.op("dve", I_tt(o[:, 0:n], t1[:, 0:n], t2[:, 0:n], ALU.add), reads=[T1, T2], writes=[To])
                r0 = jb["j"] * 128
                S.dma("pool", jb["dst"][r0:r0 + 128, t0:t0 + n], o[:, 0:n], reads=[To])

            for j in range(8):
                jobs.append(dict(cols=[col(OFF_Q + j * 128), col_swapped(OFF_Q + j * 128)], epi=epi_rope, j=j, dst=cx.qT))
            for j in range(8):
                jobs.append(dict(cols=[col(OFF_K + j * 128), col_swapped(OFF_K + j * 128)], epi=epi_rope, j=j, dst=cx.kT))

            def epi_gate(jb, ti, t0, n, pss):
                (p, Tp), = pss
                o, To = sb.next()
                S.op("act", I_act(o[:, 0:n], p[:, 0:n], AF.Sigmoid, bias=vec_ap(cx, "b_gate", jb["j"])),
                     reads=[Tp, cx.T_vec], writes=[To])
                r0 = jb["j"] * 128
                S.dma("pool", cx.gates[r0:r0 + 128, t0:t0 + n], o[:, 0:n], reads=[To])

            for j in range(48):
                jobs.append(dict(cols=[col(OFF_GATE + j * 128)], epi=epi_gate, j=j))

            run_jobs(cx, jobs, act, Tact, tiles, D, NB=6, tag="D")

            wv = SB(es, nc, "D_wv", [128, 16, 1024], BF16)
            Twv = Tok()
            for hh in range(2):
                S.dma("pool", wv[:, :, hh * 512:(hh + 1) * 512],
                      W[:, OFF_V + hh * 512:OFF_V + (hh + 1) * 512].rearrange("(kc p) n -> p kc n", p=128), writes=[Twv], acc=True)
            for tb in range(sn // 128):
                for hh in range(2):
                    ps, Tp = next_ps(cx)
                    for kc in range(16):
                        S.op("pe", I_mm(ps[:], act[:, kc, tb * 128:(tb + 1) * 128], wv[:, kc, hh * 512:(hh + 1) * 512],
                                        kc == 0, kc == 15), reads=[Tact, Twv], writes=[Tp])
                    o, To = sb.next()
                    if hh == 0:
                        S.op("act", I_act(o[:], ps[:], AF.Identity), reads=[Tp], writes=[To])
                    else:
                        S.op("dve", I_copy(o[:], ps[:]), reads=[Tp], writes=[To])
                    tg = s0 + tb * 128
                    S.dma("pool", cx.V[tg:tg + 128, hh * 512:(hh + 1) * 512], o[:], reads=[To])


def dma_mid_split(S, q, out, in_, nmid, per, reads=(), writes=()):
    for a in range(0, nmid, per):
        b = min(nmid, a + per)
        S.dma(q, out[:, a:b, :], in_[:, a:b, :], reads=reads, writes=writes, acc=True)


def split_tiles(s0, sn, maxn=512):
    out = []
    t = s0
    end = s0 + sn
    while t < end:
        lim = CTX if t < CTX else end
        n = min(maxn, lim - t, end - t)
        out.append((t, n))
        t += n
    return out


def load_T(cx, es, dram2d, R, C, name):
    nc, S = cx.nc, cx.S
    dst = SB(es, nc, name + "_T", [128, (C // 128) * R], F32)
    Tdst = Tok()
    with ExitStack() as es2:
        raw = SB(es2, nc, name + "_raw", [R, C], F32)
        Traw = Tok()
        S.dma("sp", raw[:], dram2d, writes=[Traw])
        nch = C // 128
        per = 512 // R
        for g in range(0, nch, per):
            ps, Tp = next_ps(cx)
            k = min(per, nch - g)
            for j in range(k):
                ch = g + j
                S.op("pe", I_tr(ps[:, j * R:(j + 1) * R], raw[:, ch * 128:(ch + 1) * 128], cx.ident[0:R, 0:R]),
                     reads=[Traw, cx.T_const], writes=[Tp])
            S.op("dve", I_copy(dst[:, g * R:(g + k) * R], ps[:, 0:k * R]), reads=[Tp], writes=[Tdst])
        S.barrier()
    return dst, Tdst


def phase_E(cx, l):
    nc, S = cx.nc, cx.S
    segs = [(0, CTX), (CTX, SEQ)] if l < DEPTH - 1 else [(CTX, SEQ)]
    with ExitStack() as es:
        cw, Tcw = load_T(cx, es, cx.I["conv_a_w"][l], 31, 512, "E_cw")
        conv = SB(es, nc, "E_conv", [128, 4, TA], F32)
        Tconv = [Tok() for _ in range(4)]
        y = [SB(es, nc, f"E_y{i}", [128, TA], F32) for i in range(2)]
        Ty = [Tok(), Tok()]
        acc2 = SB(es, nc, "E_acc2", [128, TA], F32)
        Tacc = Tok()
        for cc in range(4):
            b = cc % 2
            S.dma("sp", y[b][:], cx.yA[cc * 128:(cc + 1) * 128, :], writes=[Ty[b]])
            for (g0, gl) in segs:
                S.op("dve", I_ts(conv[:, cc, g0:g0 + gl], y[b][:, g0:g0 + gl], cw[:, cc * 31 + 15:cc * 31 + 16],
                                 vec_ap(cx, "conv_a_b", cc), ALU.mult, ALU.add),
                     reads=[Ty[b], Tcw, cx.T_vec], writes=[Tconv[cc]])
                for k in range(15):
                    o = 15 - k
                    S.op("dve", I_stt(conv[:, cc, g0 + o:g0 + gl], y[b][:, g0:g0 + gl - o], cw[:, cc * 31 + k:cc * 31 + k + 1],
                                      conv[:, cc, g0 + o:g0 + gl], ALU.mult, ALU.add),
                         reads=[Ty[b], Tcw], writes=[Tconv[cc]])
                for k in range(16, 31):
                    o = k - 15
                    S.op("dve", I_stt(conv[:, cc, g0:g0 + gl - o], y[b][:, g0 + o:g0 + gl], cw[:, cc * 31 + k:cc * 31 + k + 1],
                                      conv[:, cc, g0:g0 + gl - o], ALU.mult, ALU.add),
                         reads=[Ty[b], Tcw], writes=[Tconv[cc]])
        sq = SB(es, nc, "E_sq", [128, 4, 512], F32)
        Tsq = Tok()
        mt = SB(es, nc, "E_m", [128, 512], F32)
        vt = SB(es, nc, "E_v", [128, 512], F32)
        Tm, Tv = Tok(), Tok()
        d = [SB(es, nc, f"E_d{i}", [128, 512], F32) for i in range(2)]
        Td = [Tok(), Tok()]
        so = Stage(nc, es, "E_so", BF16, 4)
        tiles = [t for t in TILES if (l < DEPTH - 1 or t[0] >= CTX)]
        for (t0, n) in tiles:
            ps1, Tp1 = next_ps(cx)
            for cc in range(4):
                S.op("pe", I_mm(ps1[:, 0:n], cx.ones_f[:], conv[:, cc, t0:t0 + n], cc == 0, cc == 3),
                     reads=[Tconv[cc], cx.T_const], writes=[Tp1])
            for cc in range(4):
                S.op("act", I_act(sq[:, cc, 0:n], conv[:, cc, t0:t0 + n], AF.Square), reads=[Tconv[cc]], writes=[Tsq])
            ps2, Tp2 = next_ps(cx)
            for cc in range(4):
                S.op("pe", I_mm(ps2[:, 0:n], cx.ones_f[:], sq[:, cc, 0:n], cc == 0, cc == 3), reads=[Tsq, cx.T_const],
                     writes=[Tp2])
            S.op("dve", I_ts(mt[:, 0:n], ps1[:, 0:n], 1.0 / 512, None, ALU.mult), reads=[Tp1], writes=[Tm])
            S.op("dve", I_tt(vt[:, 0:n], mt[:, 0:n], mt[:, 0:n], ALU.mult), reads=[Tm], writes=[Tv])
            S.op("dve", I_stt(vt[:, 0:n], ps2[:, 0:n], 1.0 / 512, vt[:, 0:n], ALU.mult, ALU.subtract), reads=[Tp2, Tv],
                 writes=[Tv])
            S.op("act", I_act(vt[:, 0:n], vt[:, 0:n], AF.Sqrt, bias=EPS, scale=1.0), reads=[Tv], writes=[Tv])
            S.op("dve", I_recip(vt[:, 0:n], vt[:, 0:n]), reads=[Tv], writes=[Tv])
            for cc in range(4):
                b = cc % 2
                S.op("dve", I_tt(d[b][:, 0:n], conv[:, cc, t0:t0 + n], mt[:, 0:n], ALU.subtract), reads=[Tconv[cc], Tm],
                     writes=[Td[b]])
                S.op("dve", I_tt(d[b][:, 0:n], d[b][:, 0:n], vt[:, 0:n], ALU.mult), reads=[Tv], writes=[Td[b]])
                o, To = so.next()
                S.op("act", I_act(o[:, 0:n], d[b][:, 0:n], AF.Silu, bias=vec_ap(cx, "ln_a_b", cc),
                                  scale=vec_ap(cx, "ln_a_g", cc)), reads=[Td[b], cx.T_vec], writes=[To])
                S.dma("act", cx.zcat[cc * 128:(cc + 1) * 128, t0:t0 + n], o[:, 0:n], reads=[To])


def phase_H(cx, l):
    nc, S = cx.nc, cx.S
    lam_init = 0.8 - 0.6 * math.exp(-0.3 * l)
    Vr = cx.V.rearrange("(kb p) c -> p kb c", p=128)
    NKB = TA // 128
    with ExitStack() as es:
        dl = SB(es, nc, "H_dl", [128, 256], F32)
        lam = SB(es, nc, "H_lam", [128, 8], F32)
        gs = SB(es, nc, "H_gs", [128, 1024], F32)
        Tl, Tgs = Tok(), Tok()
        S.dma("sp", dl[:], cx.I["diff_lambda"][l].to_broadcast([128, 256]), writes=[Tl])
        S.dma("sp", gs[:], cx.I["subln_g"][l].to_broadcast([128, 1024]), writes=[Tgs])
        S.op("dve", I_tt(dl[:, 0:64], dl[:, 0:64], dl[:, 64:128], ALU.mult), reads=[Tl], writes=[Tl])
        S.op("dve", I_tt(dl[:, 128:192], dl[:, 128:192], dl[:, 192:256], ALU.mult), reads=[Tl], writes=[Tl])
        S.op("dve", I_red(lam[:, 0:1], dl[:, 0:64], "sum"), reads=[Tl], writes=[Tl])
        S.op("dve", I_red(lam[:, 1:2], dl[:, 128:192], "sum"), reads=[Tl], writes=[Tl])
        S.op("act", I_act(lam[:, 2:4], lam[:, 0:2], AF.Exp), reads=[Tl], writes=[Tl])
        S.op("dve", I_tt(lam[:, 4:5], lam[:, 3:4], lam[:, 2:3], ALU.subtract), reads=[Tl], writes=[Tl])
        S.op("dve", I_ts(lam[:, 5:6], lam[:, 4:5], -lam_init, None, ALU.add), reads=[Tl], writes=[Tl])
        neglam = lam[:, 5:6]
        S.op("dve", I_ts(gs[:], gs[:], 1.0 - lam_init, None, ALU.mult), reads=[Tgs], writes=[Tgs])

        kT = SB(es, nc, "H_kT", [128, TA], BF16)
        qT = SB(es, nc, "H_qT", [128, TA], BF16)
        vh = SB(es, nc, "H_v", [128, NKB, 128], BF16)
        oT = SB(es, nc, "H_oT", [128, TA], BF16)
        Tk, Tq, Tv, ToT = Tok(), Tok(), Tok(), Tok()
        Sb = [SB(es, nc, f"H_S{m}", [128, TA], F32) for m in range(2)]
        TS_ = [Tok(), Tok()]
        Eb = [SB(es, nc, f"H_E{m}", [128, TA], BF16) for m in range(2)]
        TE = [Tok(), Tok()]
        tmpf = SB(es, nc, "H_tmp", [128, TA], F32)
        Ttmp = Tok()
        Ab = SB(es, nc, "H_A", [128, TA], BF16)
        TA_ = Tok()
        AT = SB(es, nc, "H_AT", [128, NKB, 128], BF16)
        TAT = Tok()
        st = SB(es, nc, "H_st", [128, 32], F32)
        Tst = Tok()
        on = SB(es, nc, "H_on", [128, 128], BF16)
        Ton = Tok()
        sqj = SB(es, nc, "H_sqj", [128, 128], F32)
        Tsqj = Tok()
        qblocks = list(range(NKB)) if l < DEPTH - 1 else list(range(2, NKB))
        qblocks = HOPT.get('qblocks', qblocks)
        stage = HOPT.get('stage', 9)
        for h in range(HOPT.get('heads', 8)):
            S.dma("sp", kT[:], cx.kT[h * 128:(h + 1) * 128, :], writes=[Tk])
            S.dma("sp", qT[:], cx.qT[h * 128:(h + 1) * 128, :], writes=[Tq])
            dma_mid_split(S, "sp", vh, Vr[:, :, h * 128:(h + 1) * 128], NKB, 12, writes=[Tv])
            for qb in qblocks:
                q0 = qb * 128
                nk = CTX if qb < 2 else TA
                nkb = nk // 128
                chunks = split_tiles(0, nk)
                for m in HOPT.get('ms', range(2)):
                    for ci, (k0, n) in enumerate(chunks):
                        ps, Tp = next_ps(cx)
                        S.op("pe", I_mm(ps[:, 0:n], qT[m * 64:(m + 1) * 64, q0:q0 + 128], kT[m * 64:(m + 1) * 64, k0:k0 + n],
                                        True, True), reads=[Tq, Tk], writes=[Tp])
                        if HOPT.get('sub', 3) >= 2:
                            S.op("dve", I_red(st[:, m * 9 + ci:m * 9 + ci + 1], ps[:, 0:n], "max"), reads=[Tp], writes=[Tst])
                        if HOPT.get('sub', 3) >= 3:
                            S.op("act", I_act(Sb[m][:, k0:k0 + n], ps[:, 0:n], AF.Identity), reads=[Tp], writes=[TS_[m]])
                    nch = len(chunks)
                    if stage < 2:
                        continue
                    S.op("dve", I_red(st[:, 18 + m:19 + m], st[:, m * 9:m * 9 + nch], "max"), reads=[Tst], writes=[Tst])
                    S.op("dve", I_ts(st[:, 20 + m:21 + m], st[:, 18 + m:19 + m], -0.125, None, ALU.mult), reads=[Tst],
                         writes=[Tst])
                    S.op("act", I_act(Eb[m][:, 0:nk], Sb[m][:, 0:nk], AF.Exp, bias=st[:, 20 + m:21 + m], scale=0.125,
                                      accum_out=st[:, 22 + m:23 + m]), reads=[TS_[m], Tst], writes=[TE[m], Tst])
                if stage < 3:
                    continue
                S.op("dve", I_recip(st[:, 24:26], st[:, 22:24]), reads=[Tst], writes=[Tst])
                S.op("dve", I_tt(st[:, 25:26], st[:, 25:26], neglam, ALU.mult), reads=[Tst, Tl], writes=[Tst])
                S.op("dve", I_ts(tmpf[:, 0:nk], Eb[1][:, 0:nk], st[:, 25:26], None, ALU.mult), reads=[TE[1], Tst],
                     writes=[Ttmp])
                S.op("dve", I_stt(Ab[:, 0:nk], Eb[0][:, 0:nk], st[:, 24:25], tmpf[:, 0:nk], ALU.mult, ALU.add),
                     reads=[TE[0], Ttmp, Tst], writes=[TA_])
                if stage < 4:
                    continue
                for g in range(0, nkb, 8):
                    ps, Tp = next_ps(cx)
                    psb = ps[:].bitcast(BF16)
                    k = min(8, nkb - g)
                    for j in range(k):
                        kb = g + j
                        S.op("pe", I_tr(psb[:, j * 128:(j + 1) * 128], Ab[:, kb * 128:(kb + 1) * 128], cx.identb[:]),
                             reads=[TA_, cx.T_const], writes=[Tp])
                    dst = AT[:, g:g + k, :]
                    src = psb[:, 0:k * 128].rearrange("p (a b) -> p a b", a=k)
                    if (g // 8) % 2 == 0:
                        S.op("act", I_act(dst, src, AF.Identity), reads=[Tp], writes=[TAT])
                    else:
                        S.op("dve", I_copy(dst, src), reads=[Tp], writes=[TAT])
                if stage < 5:
                    continue
                pso, Tpo = next_ps(cx)
                for kb in range(nkb):
                    S.op("pe", I_mm(pso[:, 0:128], AT[:, kb, :], vh[:, kb, :], kb == 0, kb == nkb - 1), reads=[TAT, Tv],
                         writes=[Tpo])
                if stage < 6:
                    continue
                S.op("act", I_act(sqj[:], pso[:, 0:128], AF.Square, accum_out=st[:, 26:27]), reads=[Tpo],
                     writes=[Tsqj, Tst])
                S.op("act", I_act(st[:, 27:28], st[:, 26:27], AF.Sqrt, bias=EPS, scale=1.0 / 128), reads=[Tst], writes=[Tst])
                S.op("dve", I_recip(st[:, 27:28], st[:, 27:28]), reads=[Tst], writes=[Tst])
                S.op("dve", I_stt(on[:], pso[:, 0:128], st[:, 27:28], gs[:, h * 128:(h + 1) * 128], ALU.mult, ALU.mult),
                     reads=[Tpo, Tst, Tgs], writes=[Ton])
                pst, Tpt = next_ps(cx)
                pstb = pst[:].bitcast(BF16)
                S.op("pe", I_tr(pstb[:, 0:128], on[:], cx.identb[:]), reads=[Ton, cx.T_const], writes=[Tpt])
                S.op("act", I_act(oT[:, q0:q0 + 128], pstb[:, 0:128], AF.Identity), reads=[Tpt], writes=[ToT])
            S.dma("sp", cx.zcat[1024 + h * 128:1024 + (h + 1) * 128, :], oT[:], reads=[ToT])


def phase_H2(cx, l):
    nc, S = cx.nc, cx.S
    lam_init = 0.8 - 0.6 * math.exp(-0.3 * l)
    Vr = cx.V.rearrange("(kb p) c -> p kb c", p=128)
    NKB = TA // 128
    with ExitStack() as es:
        dl = SB(es, nc, "H_dl", [128, 256], F32)
        lam = SB(es, nc, "H_lam", [128, 8], F32)
        gs = SB(es, nc, "H_gs", [128, 1024], F32)
        Tl, Tgs = Tok(), Tok()
        S.dma("sp", dl[:], cx.I["diff_lambda"][l].to_broadcast([128, 256]), writes=[Tl])
        S.dma("sp", gs[:], cx.I["subln_g"][l].to_broadcast([128, 1024]), writes=[Tgs])
        S.op("dve", I_tt(dl[:, 0:64], dl[:, 0:64], dl[:, 64:128], ALU.mult), reads=[Tl], writes=[Tl])
        S.op("dve", I_tt(dl[:, 128:192], dl[:, 128:192], dl[:, 192:256], ALU.mult), reads=[Tl], writes=[Tl])
        S.op("dve", I_red(lam[:, 0:1], dl[:, 0:64], "sum"), reads=[Tl], writes=[Tl])
        S.op("dve", I_red(lam[:, 1:2], dl[:, 128:192], "sum"), reads=[Tl], writes=[Tl])
        S.op("act", I_act(lam[:, 2:4], lam[:, 0:2], AF.Exp), reads=[Tl], writes=[Tl])
        S.op("dve", I_tt(lam[:, 4:5], lam[:, 3:4], lam[:, 2:3], ALU.subtract), reads=[Tl], writes=[Tl])
        S.op("dve", I_ts(lam[:, 5:6], lam[:, 4:5], -lam_init, None, ALU.add), reads=[Tl], writes=[Tl])
        neglam = lam[:, 5:6]
        S.op("dve", I_ts(gs[:], gs[:], 1.0 - lam_init, None, ALU.mult), reads=[Tgs], writes=[Tgs])

        kT = SB(es, nc, "H_kT", [128, TA], BF16)
        qT = SB(es, nc, "H_qT", [128, TA], BF16)
        vh = SB(es, nc, "H_v", [128, NKB, 128], BF16)
        oT = SB(es, nc, "H_oT", [128, TA], BF16)
        Tk, Tq, Tv, ToT = Tok(), Tok(), Tok(), Tok()
        sqb = SB(es, nc, "H_sqb", [128, TA], F32)
        Tsqb = Tok()
        nb = SB(es, nc, "H_nb", [128, 68], F32)
        Tnb = Tok()
        Eb = [[SB(es, nc, f"H_E{m}_{i}", [128, TA], BF16) for m in range(2)] for i in range(2)]
        TE = [[Tok(), Tok()] for i in range(2)]
        Ab = [SB(es, nc, f"H_A{i}", [128, TA], BF16) for i in range(2)]
        TA_ = [Tok(), Tok()]
        AT = [SB(es, nc, f"H_AT{i}", [128, NKB, 128], BF16) for i in range(2)]
        TAT = [Tok(), Tok()]
        st = [SB(es, nc, f"H_st{i}", [128, 40], F32) for i in range(2)]
        Tst = [Tok(), Tok()]
        Tst2 = [Tok(), Tok()]
        kst = SB(es, nc, "H_kst", [128, 24], F32)
        Tkst = Tok()
        on = [SB(es, nc, f"H_on{i}", [128, 128], BF16) for i in range(2)]
        Ton = [Tok(), Tok()]
        sqj = SB(es, nc, "H_sqj", [128, 128], F32)
        Tsqj = Tok()
        qblocks = list(range(NKB)) if l < DEPTH - 1 else list(range(2, NKB))
        qblocks = HOPT.get('qblocks', qblocks)
        allchunks = split_tiles(0, TA)
        it = 0
        for h in range(HOPT.get('heads', 8)):
            S.dma("sp", kT[:], cx.kT[h * 128:(h + 1) * 128, :], writes=[Tk])
            S.dma("sp", qT[:], cx.qT[h * 128:(h + 1) * 128, :], writes=[Tq])
            dma_mid_split(S, "sp", vh, Vr[:, :, h * 128:(h + 1) * 128], NKB, 12, writes=[Tv])
            S.op("act", I_act(sqb[:], kT[:], AF.Square), reads=[Tk], writes=[Tsqb])
            for m in range(2):
                for ci, (k0, n) in enumerate(allchunks):
                    ps, Tp = next_ps(cx)
                    S.op("pe", I_mm(ps[:, 0:n], cx.ones_f[m * 64:(m + 1) * 64, :], sqb[m * 64:(m + 1) * 64, k0:k0 + n], True, True),
                         reads=[Tsqb, cx.T_const], writes=[Tp])
                    S.op("dve", I_red(kst[:, m * 9 + ci:m * 9 + ci + 1], ps[:, 0:n], "max"), reads=[Tp], writes=[Tkst])
                S.op("dve", I_red(kst[:, 18 + m:19 + m], kst[:, m * 9:m * 9 + 9], "max"), reads=[Tkst], writes=[Tkst])
            S.op("act", I_act(sqb[:], qT[:], AF.Square), reads=[Tq], writes=[Tsqb])
            psq, Tpq = next_ps(cx)
            for m in range(2):
                for qb in range(NKB):
                    S.op("pe", I_mm(psq[:, m * NKB + qb:m * NKB + qb + 1], sqb[m * 64:(m + 1) * 64, qb * 128:(qb + 1) * 128],
                                    cx.ones_f[m * 64:(m + 1) * 64, 0:1], True, True), reads=[Tsqb, cx.T_const], writes=[Tpq])
            for m in range(2):
                S.op("dve", I_ts(nb[:, m * NKB:(m + 1) * NKB], psq[:, m * NKB:(m + 1) * NKB], kst[:, 18 + m:19 + m], None, ALU.mult),
                     reads=[Tpq, Tkst], writes=[Tnb])
            S.op("act", I_act(nb[:], nb[:], AF.Sqrt), reads=[Tnb], writes=[Tnb])
            S.op("dve", I_ts(nb[:], nb[:], -0.125 * 1.002, None, ALU.mult), reads=[Tnb], writes=[Tnb])
            def stage1(qb, b):
                q0 = qb * 128
                nk = CTX if qb < 2 else TA
                chunks = split_tiles(0, nk)
                nch = len(chunks)
                for m in range(2):
                    for ci, (k0, n) in enumerate(chunks):
                        ps, Tp = next_ps(cx)
                        S.op("pe", I_mm(ps[:, 0:n], qT[m * 64:(m + 1) * 64, q0:q0 + 128], kT[m * 64:(m + 1) * 64, k0:k0 + n],
                                        True, True), reads=[Tq, Tk], writes=[Tp])
                        S.op("act", I_act(Eb[b][m][:, k0:k0 + n], ps[:, 0:n], AF.Exp, bias=nb[:, m * NKB + qb:m * NKB + qb + 1],
                                          scale=0.125, accum_out=st[b][:, m * 9 + ci:m * 9 + ci + 1]),
                             reads=[Tp, Tnb], writes=[TE[b][m], Tst[b]])

            def stage2(qb, b):
                q0 = qb * 128
                nk = CTX if qb < 2 else TA
                nkb = nk // 128
                nch = len(split_tiles(0, nk))
                for m in range(2):
                    S.op("dve", I_red(st[b][:, 22 + m:23 + m], st[b][:, m * 9:m * 9 + nch], "sum"), reads=[Tst[b]], writes=[Tst2[b]])
                S.op("dve", I_recip(st[b][:, 24:26], st[b][:, 22:24]), reads=[Tst2[b]], writes=[Tst2[b]])
                S.op("dve", I_tt(st[b][:, 28:29], st[b][:, 22:23], st[b][:, 25:26], ALU.mult), reads=[Tst2[b]], writes=[Tst2[b]])
                S.op("dve", I_tt(st[b][:, 29:30], st[b][:, 28:29], neglam, ALU.mult), reads=[Tst2[b], Tl], writes=[Tst2[b]])
                S.op("dve", I_stt(Ab[b][:, 0:nk], Eb[b][1][:, 0:nk], st[b][:, 29:30], Eb[b][0][:, 0:nk], ALU.mult, ALU.add),
                     reads=[TE[b][0], TE[b][1], Tst2[b]], writes=[TA_[b]])
                for g in range(0, nkb, 8):
                    ps, Tp = next_ps(cx)
                    psb = ps[:].bitcast(BF16)
                    k = min(8, nkb - g)
                    for j in range(k):
                        kb = g + j
                        S.op("pe", I_tr(psb[:, j * 128:(j + 1) * 128], Ab[b][:, kb * 128:(kb + 1) * 128], cx.identb[:]),
                             reads=[TA_[b], cx.T_const], writes=[Tp])
                    dst = AT[b][:, g:g + k, :]
                    src = psb[:, 0:k * 128].rearrange("p (a b) -> p a b", a=k)
                    S.op("dve", I_copy(dst, src), reads=[Tp], writes=[TAT[b]])
                pso, Tpo = next_ps(cx)
                for kb in range(nkb):
                    S.op("pe", I_mm(pso[:, 0:128], AT[b][:, kb, :], vh[:, kb, :], kb == 0, kb == nkb - 1), reads=[TAT[b], Tv],
                         writes=[Tpo])
                S.op("act", I_act(sqj[:], pso[:, 0:128], AF.Square, scale=st[b][:, 24:25], accum_out=st[b][:, 26:27]),
                     reads=[Tpo, Tst2[b]], writes=[Tsqj, Tst2[b]])
                S.op("act", I_act(st[b][:, 27:28], st[b][:, 26:27], AF.Ln, bias=cx.epsc[:, 0:1], scale=1.0 / 128), reads=[Tst2[b], cx.T_const],
                     writes=[Tst2[b]])
                S.op("act", I_act(st[b][:, 27:28], st[b][:, 27:28], AF.Exp, scale=-0.5), reads=[Tst2[b]], writes=[Tst2[b]])
                S.op("dve", I_tt(st[b][:, 30:31], st[b][:, 27:28], st[b][:, 24:25], ALU.mult), reads=[Tst2[b]], writes=[Tst2[b]])
                S.op("dve", I_stt(on[b][:], pso[:, 0:128], st[b][:, 30:31], gs[:, h * 128:(h + 1) * 128], ALU.mult, ALU.mult),
                     reads=[Tpo, Tst2[b], Tgs], writes=[Ton[b]])
                pst, Tpt = next_ps(cx)
                pstb = pst[:].bitcast(BF16)
                S.op("pe", I_tr(pstb[:, 0:128], on[b][:], cx.identb[:]), reads=[Ton[b], cx.T_const], writes=[Tpt])
                S.op("dve", I_copy(oT[:, q0:q0 + 128], pstb[:, 0:128]), reads=[Tpt], writes=[ToT])

            seq = [(qb, (it + i) % 2) for i, qb in enumerate(qblocks)]
            it += len(qblocks)
            for i, (qb, b) in enumerate(seq):
                if i == 0:
                    stage1(qb, b)
                if i + 1 < len(seq):
                    stage1(*seq[i + 1])
                stage2(qb, b)
            S.dma("sp", cx.zcat[1024 + h * 128:1024 + (h + 1) * 128, :], oT[:], reads=[ToT])


def wrap_sin(cx, buf, n, Tb, tmp, Ttmp):
    S = cx.S
    PI = math.pi
    S.op("dve", I_ts(tmp[0:64, 0:n], buf, -PI, 2 * PI, ALU.is_lt, ALU.mult), reads=[Tb], writes=[Ttmp])
    S.op("dve", I_tt(buf, buf, tmp[0:64, 0:n], ALU.add), reads=[Ttmp], writes=[Tb])
    S.op("dve", I_ts(tmp[0:64, 0:n], buf, PI, -2 * PI, ALU.is_gt, ALU.mult), reads=[Tb], writes=[Ttmp])
    S.op("dve", I_tt(buf, buf, tmp[0:64, 0:n], ALU.add), reads=[Ttmp], writes=[Tb])
    S.op("act", I_act(buf, buf, AF.Sin), reads=[Tb], writes=[Tb])


def phase_F(cx, l):
    nc, S = cx.nc, cx.S
    I = cx.I
    seqs = [dict(t0=CTX, L=SEQ, LB=32, NFB=33, f0=0, TC=I["TC"], TS=I["TS"], TCf=I["TCf"], TSf=I["TSf"], wf0=0),
            dict(t0=0, L=CTX, LB=2, NFB=3, f0=NF, TC=I["TCc"], TS=I["TSc"], TCf=I["TCcf"], TSf=I["TScf"], wf0=33)]
    if l == DEPTH - 1:
        seqs = seqs[:1]
    with ExitStack() as es:
        sw, Tsw = load_T(cx, es, I["short_b_w"][l], 3, 1536, "F_sw")
        wfs = SB(es, nc, "F_wf", [128, 72], F32)
        Twf = Tok()
        S.dma("sp", wfs[:], I["hy_wf"][:, :], writes=[Twf])
        es0 = ExitStack()
        u = [SB(es0, nc, f"F_u{i}", [128, TA], F32) for i in range(2)]
        o = [SB(es0, nc, f"F_o{i}", [128, TA], F32) for i in range(2)]
        Tu, To = [Tok(), Tok()], [Tok(), Tok()]
        for jc in range(12):
            b = jc % 2
            S.dma("sp", u[b][:], cx.bT[jc * 128:(jc + 1) * 128, :], writes=[Tu[b]])
            for sq in seqs:
                a, e_ = sq["t0"], sq["t0"] + sq["L"]
                S.op("dve", I_ts(o[b][:, a:e_], u[b][:, a:e_], sw[:, jc * 3 + 1:jc * 3 + 2], vec_ap(cx, "short_b_b", jc),
                                 ALU.mult, ALU.add), reads=[Tu[b], Tsw, cx.T_vec], writes=[To[b]])
                S.op("dve", I_stt(o[b][:, a + 1:e_], u[b][:, a:e_ - 1], sw[:, jc * 3:jc * 3 + 1], o[b][:, a + 1:e_],
                                  ALU.mult, ALU.add), reads=[Tu[b], Tsw], writes=[To[b]])
                S.op("dve", I_stt(o[b][:, a:e_ - 1], u[b][:, a + 1:e_], sw[:, jc * 3 + 2:jc * 3 + 3], o[b][:, a:e_ - 1],
                                   ALU.mult, ALU.add), reads=[Tu[b], Tsw], writes=[To[b]])
                S.dma("act", cx.ubuf[jc * 128:(jc + 1) * 128, a:e_], o[b][:, a:e_], reads=[To[b]])
        S.barrier()
        es0.close()
        for sq in seqs:
            t0, L, LB, NFB, f0 = sq["t0"], sq["L"], sq["LB"], sq["NFB"], sq["f0"]
            S.barrier()
            with ExitStack() as es1:
                w1 = SB(es1, nc, "F_w1", [17, 64], F32)
                w2 = SB(es1, nc, "F_w2", [64, 64], F32)
                w3 = SB(es1, nc, "F_w3", [64, 2048], F32)
                feats = SB(es1, nc, "F_feats", [17, SEQ], F32)
                h1 = SB(es1, nc, "F_h1", [64, SEQ], F32)
                h2 = SB(es1, nc, "F_h2", [64, SEQ], F32)
                tmpw = SB(es1, nc, "F_tmpw", [64, 512], F32)
                Tw_, Tf, Th1, Th2, Ttw = Tok(), Tok(), Tok(), Tok(), Tok()
                S.dma("sp", w1[:], I["filt_w1"][l], writes=[Tw_])
                S.dma("sp", w2[:], I["filt_w2"][l], writes=[Tw_])
                S.dma("sp", w3[:], I["filt_w3"][l], writes=[Tw_])
                S.dma("sp", feats[:, 0:L], I["hy_feats"][:, t0:t0 + L], writes=[Tf])
                b1 = vec_ap(cx, "filt_b1", 0)[0:64, :]
                b2 = vec_ap(cx, "filt_b2", 0)[0:64, :]
                for c0 in range(0, L, 512):
                    n = min(512, L - c0)
                    ps, Tp = next_ps(cx)
                    S.op("pe", I_mm(ps[0:64, 0:n], w1[:, :], feats[:, c0:c0 + n], True, True), reads=[Tw_, Tf], writes=[Tp])
                    S.op("dve", I_ts(h1[:, c0:c0 + n], ps[0:64, 0:n], b1, None, ALU.add), reads=[Tp, cx.T_vec], writes=[Th1])
                    wrap_sin(cx, h1[:, c0:c0 + n], n, Th1, tmpw, Ttw)
                for c0 in range(0, L, 512):
                    n = min(512, L - c0)
                    ps, Tp = next_ps(cx)
                    S.op("pe", I_mm(ps[0:64, 0:n], w2[:, :], h1[:, c0:c0 + n], True, True), reads=[Tw_, Th1], writes=[Tp])
                    S.op("dve", I_ts(h2[:, c0:c0 + n], ps[0:64, 0:n], b2, None, ALU.add), reads=[Tp, cx.T_vec], writes=[Th2])
                    wrap_sin(cx, h2[:, c0:c0 + n], n, Th2, tmpw, Ttw)
                dec = [SB(es1, nc, f"F_dec{i}", [128, 1024], F32) for i in range(2)]
                Tdec = [Tok(), Tok()]
                hf = [SB(es1, nc, f"F_hf{i}", [128, 512], F32) for i in range(2)]
                hb = [SB(es1, nc, f"F_hb{i}", [128, 512], F32) for i in range(2)]
                ab = [SB(es1, nc, f"F_ab{i}", [128, 512], F32) for i in range(2)]
                ab2 = [SB(es1, nc, f"F_ab2{i}", [128, 512], F32) for i in range(2)]
                hpm_t = [SB(es1, nc, f"F_hpm{i}", [128, 1024], BF16) for i in range(2)]
                Thf, Thb, Tab, Tab2, Thpm = [Tok(), Tok()], [Tok(), Tok()], [Tok(), Tok()], [Tok(), Tok()], [Tok(), Tok()]
                rn = SB(es1, nc, "F_rn", [128, 1024], F32)
                Trn = Tok()
                psn = []
                for _ in range(2):
                    p_, T_ = next_ps(cx)
                    psn.append((p_, T_))
                for p_, _ in psn:
                    cx.ps_reserved.add(cx.ps.index(p_))
                it = 0
                for tb in range(LB):
                    db = tb % 2
                    r0 = t0 + tb * 128
                    S.dma("sp", dec[db][:, 0:512], I["hy_decay"][r0:r0 + 128, :], writes=[Tdec[db]], acc=True)
                    S.dma("sp", dec[db][:, 512:1024], I["hy_decayb"][r0:r0 + 128, :], writes=[Tdec[db]], acc=True)
                    for o_ in range(2):
                        k = it % 2
                        it += 1
                        psf, Tpf = next_ps(cx)
                        psb, Tpb = next_ps(cx)
                        S.op("pe", I_mm(psf[:], h2[:, tb * 128:(tb + 1) * 128], w3[:, (o_ * 2) * 512:(o_ * 2 + 1) * 512], True, True),
                             reads=[Th2, Tw_], writes=[Tpf])
                        S.op("pe", I_mm(psb[:], h2[:, tb * 128:(tb + 1) * 128], w3[:, (o_ * 2 + 1) * 512:(o_ * 2 + 2) * 512], True, True),
                             reads=[Th2, Tw_], writes=[Tpb])
                        S.op("dve", I_tt(hf[k][:], psf[:], dec[db][:, 0:512], ALU.mult), reads=[Tpf, Tdec[db]], writes=[Thf[k]])
                        S.op("dve", I_tt(hb[k][:], psb[:], dec[db][:, 512:1024], ALU.mult), reads=[Tpb, Tdec[db]], writes=[Thb[k]])
                        S.op("act", I_act(ab[k][:], hf[k][:], AF.Abs), reads=[Thf[k]], writes=[Tab[k]])
                        S.op("act", I_act(ab2[k][:], hb[k][:], AF.Abs), reads=[Thb[k]], writes=[Tab2[k]])
                        S.op("pool", I_tt(ab[k][:], ab[k][:], ab2[k][:], ALU.add), reads=[Tab2[k]], writes=[Tab[k]])
                        S.op("pe", I_mm(psn[o_][0][:], cx.ones_f[:], ab[k][:], tb == 0, tb == LB - 1), reads=[Tab[k], cx.T_const],
                             writes=[psn[o_][1]])
                        S.op("dve", I_tt(hpm_t[k][:, 0:512], hf[k][:], hb[k][:], ALU.add), reads=[Thf[k], Thb[k]], writes=[Thpm[k]])
                        S.op("pool", I_tt(hpm_t[k][:, 512:1024], hf[k][:], hb[k][:], ALU.subtract), reads=[Thf[k], Thb[k]],
                             writes=[Thpm[k]])
                        S.dma("act", cx.hpm[r0:r0 + 128, o_ * 512:(o_ + 1) * 512], hpm_t[k][:, 0:512], reads=[Thpm[k]])
                        S.dma("act", cx.hpm[r0:r0 + 128, 1024 + o_ * 512:1024 + (o_ + 1) * 512], hpm_t[k][:, 512:1024],
                              reads=[Thpm[k]])
                for o_ in range(2):
                    S.op("dve", I_recip(rn[:, o_ * 512:(o_ + 1) * 512], psn[o_][0][:]), reads=[psn[o_][1]], writes=[Trn])
                cx.ps_reserved.clear()
                S.barrier()
                X = SB(es1, nc, "F_X", [128, 32, 512], BF16)
                TX = Tok()
                tab = [SB(es1, nc, f"F_tab{i}", [128, 32, 128], BF16) for i in range(2)]
                Ttab = [Tok(), Tok()]
                so = Stage(nc, es1, "F_so", F32, 3)
                it = 0
                for pm in range(2):
                    tableF = sq["TCf"] if pm == 0 else sq["TSf"]
                    for o_ in range(2):
                        cbase = pm * 1024 + o_ * 512
                        dma_mid_split(S, "sp", X, cx.hpm[t0:t0 + L, cbase:cbase + 512].rearrange("(tb p) c -> p tb c", p=128),
                                      LB, 8, writes=[TX])
                        for fb in range(NFB):
                            k = it % 2
                            it += 1
                            S.dma("sp", tab[k][:, 0:LB, :], tableF[fb][:, 0:LB, :], writes=[Ttab[k]])
                            ps, Tp = next_ps(cx)
                            for db in range(LB):
                                S.op("pe", I_mm(ps[:], tab[k][:, db, :], X[:, db, :], db == 0, db == LB - 1), reads=[Ttab[k], TX],
                                     writes=[Tp])
                            wcol = sq["wf0"] + fb + (36 if pm == 1 else 0)
                            ot, Tot = so.next()
                            S.op("dve", I_stt(ot[:], ps[:], wfs[:, wcol:wcol + 1], rn[:, o_ * 512:(o_ + 1) * 512], ALU.mult, ALU.mult),
                                 reads=[Tp, Twf, Trn], writes=[Tot])
                            S.dma("act", cx.Hd[f0 + fb * 128:f0 + (fb + 1) * 128, cbase:cbase + 512], ot[:], reads=[Tot])
        for sq in seqs:
            t0, L, LB, NFB, f0 = sq["t0"], sq["L"], sq["LB"], sq["NFB"], sq["f0"]
            S.barrier()
            with ExitStack() as es2:
                zt = SB(es2, nc, "F_zt", [128, 32, 512], BF16)
                Y = SB(es2, nc, "F_Y", [128, 33, 1024], BF16)
                Tzt, TY = Tok(), Tok()
                ld = Stage(nc, es2, "F_ld", F32, 4)
                for cc in range(4):
                    for c0 in range(0, L, 512):
                        n = min(512, L - c0)
                        vt, Tvt = ld.next()
                        S.dma("sp", vt[:, 0:n], cx.ubuf[cc * 128:(cc + 1) * 128, t0 + c0:t0 + c0 + n], writes=[Tvt])
                        ps, Tp = next_ps(cx)
                        nb = n // 128
                        for j in range(nb):
                            S.op("pe", I_tr(ps[:, j * 128:(j + 1) * 128], vt[:, j * 128:(j + 1) * 128], cx.ident[:]),
                                 reads=[Tvt, cx.T_const], writes=[Tp])
                        tb0 = c0 // 128
                        S.op("act", I_act(zt[:, tb0:tb0 + nb, cc * 128:(cc + 1) * 128],
                                          ps[:, 0:nb * 128].rearrange("p (a b) -> p a b", a=nb), AF.Identity), reads=[Tp], writes=[Tzt])
                Tz1 = {}
                for o_ in range(2):
                    with ExitStack() as es3:
                        tc_ = [SB(es3, nc, f"F_tc{i}", [128, 32, 128], BF16) for i in range(2)]
                        ts_ = [SB(es3, nc, f"F_ts{i}", [128, 32, 128], BF16) for i in range(2)]
                        Hh = [SB(es3, nc, f"F_Hh{i}", [128, 1024], F32) for i in range(2)]
                        Ttc, Tts, THh = [Tok(), Tok()], [Tok(), Tok()], [Tok(), Tok()]
                        pa = Stage(nc, es3, "F_pa", F32, 4)
                        for fb in range(NFB):
                            k = fb % 2
                            S.dma("sp", tc_[k][:, 0:LB, :], sq["TCf"][fb][:, 0:LB, :], writes=[Ttc[k]])
                            S.dma("sp", ts_[k][:, 0:LB, :], sq["TSf"][fb][:, 0:LB, :], writes=[Tts[k]])
                            S.dma("sp", Hh[k][:, 0:512], cx.Hd[f0 + fb * 128:f0 + (fb + 1) * 128, o_ * 512:(o_ + 1) * 512],
                                  writes=[THh[k]], acc=True)
                            S.dma("sp", Hh[k][:, 512:1024],
                                  cx.Hd[f0 + fb * 128:f0 + (fb + 1) * 128, 1024 + o_ * 512:1024 + (o_ + 1) * 512], writes=[THh[k]],
                                  acc=True)
                            pr, Tpr = next_ps(cx)
                            pi, Tpi = next_ps(cx)
                            for db in range(LB):
                                S.op("pe", I_mm(pr[:], tc_[k][:, db, :], zt[:, db, :], db == 0, db == LB - 1), reads=[Ttc[k], Tzt],
                                     writes=[Tpr])
                            for db in range(LB):
                                S.op("pe", I_mm(pi[:], ts_[k][:, db, :], zt[:, db, :], db == 0, db == LB - 1), reads=[Tts[k], Tzt],
                                     writes=[Tpi])
                            a, Ta = pa.next()
                            b, Tb = pa.next()
                            S.op("dve", I_tt(a[:], pr[:], Hh[k][:, 0:512], ALU.mult), reads=[Tpr, THh[k]], writes=[Ta])
                            S.op("dve", I_tt(b[:], pi[:], Hh[k][:, 512:1024], ALU.mult), reads=[Tpi, THh[k]], writes=[Tb])
                            S.op("pool", I_tt(Y[:, fb, 0:512], a[:], b[:], ALU.add), reads=[Ta, Tb], writes=[TY])
                            c, Tc = pa.next()
                            d, Td = pa.next()
                            S.op("dve", I_tt(c[:], pi[:], Hh[k][:, 0:512], ALU.mult), reads=[Tpi, THh[k]], writes=[Tc])
                            S.op("dve", I_tt(d[:], pr[:], Hh[k][:, 512:1024], ALU.mult), reads=[Tpr, THh[k]], writes=[Td])
                            S.op("pool", I_tt(Y[:, fb, 512:1024], c[:], d[:], ALU.subtract), reads=[Tc, Td], writes=[TY])
                    S.barrier()
                    with ExitStack() as es3:
                        tci = [SB(es3, nc, f"F_tci{i}", [128, 33, 256], BF16) for i in range(2)]
                        tsi = [SB(es3, nc, f"F_tsi{i}", [128, 33, 256], BF16) for i in range(2)]
                        Ttci, Ttsi = [Tok(), Tok()], [Tok(), Tok()]
                        ld2 = Stage(nc, es3, "F_ld2", F32, 6)
                        sob = Stage(nc, es3, "F_sob", BF16, 3)
                        for ti, tt0 in enumerate(range(0, L, 256)):
                            tn = 256
                            k = ti % 2
                            dma_mid_split(S, "sp", tci[k], sq["TC"][0:NFB * 128, tt0:tt0 + tn].rearrange("(fb p) t -> p fb t", p=128),
                                          NFB, 8, writes=[Ttci[k]])
                            dma_mid_split(S, "sp", tsi[k], sq["TS"][0:NFB * 128, tt0:tt0 + tn].rearrange("(fb p) t -> p fb t", p=128),
                                          NFB, 8, writes=[Ttsi[k]])
                            for cc in range(4):
                                ps, Tp = next_ps(cx)
                                for fb in range(NFB):
                                    S.op("pe", I_mm(ps[:, 0:tn], Y[:, fb, cc * 128:(cc + 1) * 128], tci[k][:, fb, :], fb == 0, False),
                                         reads=[TY, Ttci[k]], writes=[Tp])
                                    S.op("pe", I_mm(ps[:, 0:tn], Y[:, fb, 512 + cc * 128:512 + (cc + 1) * 128], tsi[k][:, fb, :], False,
                                                    fb == NFB - 1), reads=[TY, Ttsi[k]], writes=[Tp])
                                zp, Tzp = ld2.next()
                                gt, Tgt = ld2.next()
                                g0 = t0 + tt0
                                if o_ == 0:
                                    S.dma("sp", zp[:, 0:tn], cx.ubuf[cc * 128:(cc + 1) * 128, g0:g0 + tn], writes=[Tzp])
                                else:
                                    S.dma("sp", zp[:, 0:tn], cx.z1[cc * 128:(cc + 1) * 128, g0:g0 + tn], reads=[Tz1[(cc, tt0)]],
                                          writes=[Tzp])
                                gr = (1 + o_) * 512 + cc * 128
                                S.dma("sp", gt[:, 0:tn], cx.ubuf[gr:gr + 128, g0:g0 + tn], writes=[Tgt])
                                S.op("dve", I_stt(zp[:, 0:tn], zp[:, 0:tn], vec_ap(cx, "hyena_skip", o_ * 4 + cc), ps[:, 0:tn],
                                                  ALU.mult, ALU.add), reads=[Tp, cx.T_vec], writes=[Tzp])
                                if o_ == 0:
                                    S.op("dve", I_tt(zp[:, 0:tn], zp[:, 0:tn], gt[:, 0:tn], ALU.mult), reads=[Tgt], writes=[Tzp])
                                    Tz1[(cc, tt0)] = Tok()
                                    S.dma("act", cx.z1[cc * 128:(cc + 1) * 128, g0:g0 + tn], zp[:, 0:tn], reads=[Tzp],
                                          writes=[Tz1[(cc, tt0)]])
                                    pt, Tpt = next_ps(cx)
                                    for j in range(2):
                                        S.op("pe", I_tr(pt[:, j * 128:(j + 1) * 128], zp[:, j * 128:(j + 1) * 128], cx.ident[:]),
                                             reads=[Tzp, cx.T_const], writes=[Tpt])
                                    tb0 = tt0 // 128
                                    S.op("act", I_act(zt[:, tb0:tb0 + 2, cc * 128:(cc + 1) * 128],
                                                      pt[:, 0:256].rearrange("p (a b) -> p a b", a=2), AF.Identity), reads=[Tpt],
                                         writes=[Tzt])
                                else:
                                    ob, Tob = sob.next()
                                    S.op("dve", I_tt(ob[:, 0:tn], zp[:, 0:tn], gt[:, 0:tn], ALU.mult), reads=[Tgt, Tzp], writes=[Tob])
                                    S.dma("act", cx.zcat[512 + cc * 128:512 + (cc + 1) * 128, g0:g0 + tn], ob[:, 0:tn], reads=[Tob])
                    S.barrier()


def run_jobs2(cx, jobs, act, Tact, tiles, KCmax, NB=6, tag="J"):
    nc, S = cx.nc, cx.S
    with ExitStack() as es:
        wb = [SB(es, nc, f"{tag}_w{i}", [128, KCmax, 128], BF16) for i in range(NB)]
        Tw = [Tok() for _ in range(NB)]
        flat = []
        for ji, jb in enumerate(jobs):
            for ci in range(len(jb["cols"])):
                flat.append((ji, ci))
        slot = {}

        def issue(fi):
            ji, ci = flat[fi]
            s = fi % NB
            slot[(ji, ci)] = s
            c = jobs[ji]["cols"][ci]
            load_w_chunk(cx, wb[s], Tw[s], c["segs"], c["KC"] * 128)

        nxt = 0
        fpos = 0
        for ji, jb in enumerate(jobs):
            ncol = len(jb["cols"])
            while nxt < len(flat) and nxt < fpos + NB:
                issue(nxt)
                nxt += 1
            for ti, (c0, n, t0) in enumerate(tiles):
                pss = []
                for ci in range(ncol):
                    s = slot[(ji, ci)]
                    c = jb["cols"][ci]
                    ps, Tp = next_ps(cx)
                    KC = c["KC"]
                    for kc in range(KC):
                        S.op("pe", I_mm(ps[:, 0:n], wb[s][:, kc, :], act[:, c["kc0"] + kc, c0:c0 + n], kc == 0, kc == KC - 1),
                             reads=[Tw[s], Tact], writes=[Tp])
                    pss.append((ps, Tp))
                jb["epi"](jb, ti, t0, n, pss)
            fpos += ncol
        S.barrier()


def phase_G(cx, l):
    nc, S = cx.nc, cx.S
    zr = cx.zcat.rearrange("(c p) t -> p c t", p=128)
    halves = [(0, 2304), (2304, 2048)] if l < DEPTH - 1 else [(CTX, 2048), (CTX + 2048, 2048)]
    Wa, Wb, Wc, Wo = cx.I["w_a_out"][l], cx.I["w_b_out"][l], cx.I["w_c_out"][l], cx.I["w_o"][l]
    for (s0, sn) in halves:
        S.barrier()
        with ExitStack() as es:
            act = SB(es, nc, "G_act", [128, 16, 2304], BF16)
            mT = SB(es, nc, "G_m", [128, 16, 2304], BF16)
            Tact, TmT = Tok(), Tok()
            for c in range(16):
                S.dma("sp", act[:, c, 0:sn], zr[:, c, s0:s0 + sn], writes=[Tact], acc=True)
            tiles = [(t0 - s0, n, t0) for (t0, n) in split_tiles(s0, sn)]
            gt = [SB(es, nc, f"G_g{i}", [128, 3, 512], BF16) for i in range(3)]
            Tg = [Tok() for _ in range(3)]
            gi = [0]
            tf = Stage(nc, es, "G_tf", F32, 4)
            jobs = []

            def epi_merge(jb, ti, t0, n, pss):
                fo = jb["j"]
                i = gi[0]
                gi[0] = (i + 1) % 3
                for br in range(3):
                    r0 = br * 2048 + fo * 128
                    S.dma("sp", gt[i][:, br, 0:n], cx.gates[r0:r0 + 128, t0:t0 + n], writes=[Tg[i]], acc=True)
                a, Ta = tf.next()
                b, Tb = tf.next()
                S.op("dve", I_tt(a[:, 0:n], pss[0][0][:, 0:n], gt[i][:, 0, 0:n], ALU.mult), reads=[pss[0][1], Tg[i]], writes=[Ta])
                S.op("dve", I_tt(b[:, 0:n], pss[1][0][:, 0:n], gt[i][:, 1, 0:n], ALU.mult), reads=[pss[1][1], Tg[i]], writes=[Tb])
                S.op("pool", I_tt(a[:, 0:n], a[:, 0:n], b[:, 0:n], ALU.add), reads=[Tb], writes=[Ta])
                S.op("dve", I_tt(b[:, 0:n], pss[2][0][:, 0:n], gt[i][:, 2, 0:n], ALU.mult), reads=[pss[2][1], Tg[i]], writes=[Tb])
                c0 = t0 - s0
                S.op("dve", I_tt(mT[:, fo, c0:c0 + n], a[:, 0:n], b[:, 0:n], ALU.add), reads=[Ta, Tb], writes=[TmT])

            for fo in range(16):
                cols = [dict(segs=[(0, Wa[:, fo * 128:(fo + 1) * 128])], kc0=0, KC=4),
                        dict(segs=[(0, Wb[:, fo * 128:(fo + 1) * 128])], kc0=4, KC=4),
                        dict(segs=[(0, Wc[:, fo * 128:(fo + 1) * 128])], kc0=8, KC=8)]
                jobs.append(dict(cols=cols, epi=epi_merge, j=fo))
            run_jobs2(cx, jobs, act, Tact, tiles, 8, NB=6, tag="G1")

            xt = [SB(es, nc, f"G_x{i}", [128, 512], F32) for i in range(3)]
            Tx = [Tok() for _ in range(3)]
            xi = [0]

            def epi_res(jb, ti, t0, n, pss):
                fo = jb["j"]
                grp = 1 if t0 < CTX else 0
                i = xi[0]
                xi[0] = (i + 1) % 3
                S.dma("sp", xt[i][:, 0:n], cx.xT[fo * 128:(fo + 1) * 128, t0:t0 + n], writes=[Tx[i]])
                S.op("dve", I_stt(xt[i][:, 0:n], pss[0][0][:, 0:n], mod_ap(cx, 2, fo, grp), xt[i][:, 0:n], ALU.mult, ALU.add),
                     reads=[pss[0][1], cx.T_mod], writes=[Tx[i]])
                S.dma("act", cx.xT[fo * 128:(fo + 1) * 128, t0:t0 + n], xt[i][:, 0:n], reads=[Tx[i]])

            jobs = [dict(cols=[dict(segs=[(0, Wo[:, fo * 128:(fo + 1) * 128])], kc0=0, KC=16)], epi=epi_res, j=fo)
                    for fo in range(16)]
            run_jobs2(cx, jobs, mT, TmT, tiles, 16, NB=4, tag="G2")


def phase_J(cx, l):
    nc, S = cx.nc, cx.S
    hT_r = cx.hT.rearrange("(c p) t -> p c t", p=128)
    Wu, Wd = cx.I["w_up"][l], cx.I["w_down"][l]
    with_ctx = l < DEPTH - 1
    halves = [(0, 2304), (2304, 2048)] if with_ctx else [(CTX, 2048), (CTX + 2048, 2048)]
    with ExitStack() as es0:
        fw, Tfw = load_T(cx, es0, cx.I["conv_f_w"][l], 3, 2 * DFF, "J_fw")
        for (s0, sn) in halves:
            S.barrier()
            with ExitStack() as es:
                lo = s0 - 1 if s0 > CTX else s0
                hi = s0 + sn + 1 if s0 + sn < TA else s0 + sn
                W = hi - lo
                act = SB(es, nc, "J_act", [128, 16, 2306], BF16)
                Tact = Tok()
                for c in range(16):
                    S.dma("sp", act[:, c, 0:W], hT_r[:, c, lo:hi], writes=[Tact], acc=True)
                tl = []
                if lo < s0:
                    tl.append((lo, 1))
                tl += split_tiles(s0, sn)
                if hi > s0 + sn:
                    tl.append((s0 + sn, 1))
                tiles = [(t0 - lo, n, t0) for (t0, n) in tl]
                ua = [SB(es, nc, f"J_ua{i}", [128, 2306], F32) for i in range(2)]
                uv = [SB(es, nc, f"J_uv{i}", [128, 2306], F32) for i in range(2)]
                ca = SB(es, nc, "J_ca", [128, 2306], F32)
                cv = SB(es, nc, "J_cv", [128, 2306], F32)
                fo_ = [SB(es, nc, f"J_f{i}", [128, 2306], BF16) for i in range(2)]
                Tua, Tuv = [Tok(), Tok()], [Tok(), Tok()]
                Tca, Tcv = Tok(), Tok()
                Tfo = [Tok(), Tok()]
                segs = []
                if s0 < CTX:
                    segs.append((0, CTX, False, False))
                    segs.append((CTX, s0 + sn, False, hi > s0 + sn))
                else:
                    segs.append((s0, s0 + sn, lo < s0, hi > s0 + sn))

                def conv3(eng, dst, Tdst, src, Tsrc, ch):
                    for (a, b, hl, hr) in segs:
                        ca0, cb0 = a - lo, b - lo
                        w0 = fw[:, ch * 3:ch * 3 + 1]
                        w1 = fw[:, ch * 3 + 1:ch * 3 + 2]
                        w2 = fw[:, ch * 3 + 2:ch * 3 + 3]
                        S.op(eng, I_ts(dst[:, ca0:cb0], src[:, ca0:cb0], w1, vec_ap(cx, "conv_f_b", ch), ALU.mult, ALU.add),
                             reads=[Tsrc, Tfw, cx.T_vec], writes=[Tdst])
                        ol = ca0 if hl else ca0 + 1
                        S.op(eng, I_stt(dst[:, ol:cb0], src[:, ol - 1:cb0 - 1], w0, dst[:, ol:cb0], ALU.mult, ALU.add),
                             reads=[Tsrc, Tfw], writes=[Tdst])
                        oh = cb0 if hr else cb0 - 1
                        S.op(eng, I_stt(dst[:, ca0:oh], src[:, ca0 + 1:oh + 1], w2, dst[:, ca0:oh], ALU.mult, ALU.add),
                             reads=[Tsrc, Tfw], writes=[Tdst])

                jobs = []
                ntile = len(tiles)

                def epi_up(jb, ti, t0, n, pss):
                    j = jb["j"]
                    b = j % 2
                    c0 = t0 - lo
                    S.op("act", I_act(ua[b][:, c0:c0 + n], pss[0][0][:, 0:n], AF.Identity), reads=[pss[0][1]], writes=[Tua[b]])
                    S.op("act", I_act(uv[b][:, c0:c0 + n], pss[1][0][:, 0:n], AF.Identity), reads=[pss[1][1]], writes=[Tuv[b]])
                    if ti == ntile - 1:
                        conv3("dve", ca, Tca, ua[b], Tua[b], j)
                        conv3("dve", cv, Tcv, uv[b], Tuv[b], 44 + j)
                        a0, b0 = s0 - lo, s0 - lo + sn
                        S.op("act", I_act(ca[:, a0:b0], ca[:, a0:b0], AF.Silu), reads=[Tca], writes=[Tca])
                        S.op("dve", I_tt(fo_[b][:, a0:b0], ca[:, a0:b0], cv[:, a0:b0], ALU.mult), reads=[Tca, Tcv],
                             writes=[Tfo[b]])
                        S.dma("act", cx.fT[j * 128:(j + 1) * 128, s0:s0 + sn], fo_[b][:, a0:b0], reads=[Tfo[b]])

                for j in range(44):
                    cols = [dict(segs=[(0, Wu[:, j * 128:(j + 1) * 128])], kc0=0, KC=16),
                            dict(segs=[(0, Wu[:, (44 + j) * 128:(45 + j) * 128])], kc0=0, KC=16)]
                    jobs.append(dict(cols=cols, epi=epi_up, j=j))
                run_jobs2(cx, jobs, act, Tact, tiles, 16, NB=6, tag="J1")
    fr = cx.fT.rearrange("(c p) t -> p c t", p=128)
    quarters = [(i * 1088, 1088) for i in range(4)] if with_ctx else [(CTX + i * 1024, 1024) for i in range(4)]
    for (s0, sn) in quarters:
        S.barrier()
        with ExitStack() as es:
            act = SB(es, nc, "J2_act", [128, 44, 1088], BF16)
            Tact = Tok()
            for c in range(44):
                S.dma("sp", act[:, c, 0:sn], fr[:, c, s0:s0 + sn], writes=[Tact], acc=True)
            tiles = [(t0 - s0, n, t0) for (t0, n) in split_tiles(s0, sn)]
            xt = [SB(es, nc, f"J2_x{i}", [128, 512], F32) for i in range(3)]
            Tx = [Tok() for _ in range(3)]
            xi = [0]

            def epi_res(jb, ti, t0, n, pss):
                fo = jb["j"]
                grp = 1 if t0 < CTX else 0
                i = xi[0]
                xi[0] = (i + 1) % 3
                S.dma("sp", xt[i][:, 0:n], cx.xT[fo * 128:(fo + 1) * 128, t0:t0 + n], writes=[Tx[i]])
                S.op("dve", I_stt(xt[i][:, 0:n], pss[0][0][:, 0:n], mod_ap(cx, 5, fo, grp), xt[i][:, 0:n], ALU.mult, ALU.add),
                     reads=[pss[0][1], cx.T_mod], writes=[Tx[i]])
                S.dma("act", cx.xT[fo * 128:(fo + 1) * 128, t0:t0 + n], xt[i][:, 0:n], reads=[Tx[i]])

            jobs = [dict(cols=[dict(segs=[(0, Wd[:, fo * 128:(fo + 1) * 128])], kc0=0, KC=44)], epi=epi_res, j=fo)
                    for fo in range(16)]
            run_jobs2(cx, jobs, act, Tact, tiles, 44, NB=3, tag="J2")


def phase_final(cx):
    nc, S = cx.nc, cx.S
    xT_r = cx.xT.rearrange("(c p) t -> p c t", p=128)
    with ExitStack() as es:
        gb = SB(es, nc, "F_gb", [128, D], F32)
        Tgb = Tok()
        S.dma("sp", gb[:], cx.I["final_g"].to_broadcast([128, D]), writes=[Tgb])
        xb = [SB(es, nc, f"F_x{i}", [128, 16, 128], F32) for i in range(2)]
        ob = [SB(es, nc, f"F_o{i}", [128, D], F32) for i in range(2)]
        Tx, To = [Tok(), Tok()], [Tok(), Tok()]
        st = SB(es, nc, "F_st", [128, 8], F32)
        Tst = Tok()
        junk = SB(es, nc, "F_junk", [128, 512], F32)
        Tj = Tok()
        for tb in range(SEQ // 128):
            b = tb % 2
            t0 = CTX + tb * 128
            S.dma("sp", xb[b][:], xT_r[:, :, t0:t0 + 128], writes=[Tx[b]])
            pss = []
            for g4 in range(4):
                ps, Tp = next_ps(cx)
                for j in range(4):
                    c = g4 * 4 + j
                    S.op("pe", I_tr(ps[:, j * 128:(j + 1) * 128], xb[b][:, c, :], cx.ident[:]), reads=[Tx[b], cx.T_const],
                         writes=[Tp])
                S.op("act", I_act(junk[:], ps[:], AF.Square, accum_out=st[:, g4:g4 + 1]), reads=[Tp], writes=[Tj, Tst])
                pss.append((ps, Tp))
            S.op("dve", I_red(st[:, 4:5], st[:, 0:4], "sum"), reads=[Tst], writes=[Tst])
            S.op("act", I_act(st[:, 5:6], st[:, 4:5], AF.Sqrt, bias=EPS, scale=1.0 / D), reads=[Tst], writes=[Tst])
            S.op("dve", I_recip(st[:, 5:6], st[:, 5:6]), reads=[Tst], writes=[Tst])
            for g4 in range(4):
                ps, Tp = pss[g4]
                S.op("dve", I_stt(ob[b][:, g4 * 512:(g4 + 1) * 512], ps[:], st[:, 5:6], gb[:, g4 * 512:(g4 + 1) * 512],
                                  ALU.mult, ALU.mult), reads=[Tp, Tst, Tgb], writes=[To[b]])
            S.dma("pool", cx.out[tb * 128:(tb + 1) * 128, :], ob[b][:], reads=[To[b]])


def host_consts():
    c = {}
    c["ident"] = np.eye(128, dtype=np.float32)
    rows = SEQ // 64
    row = np.repeat(np.arange(rows), 64).astype(np.float32)
    col = np.tile(np.arange(64), rows).astype(np.float32)
    nf = 16
    inv = (10000.0 ** (-np.arange(nf, dtype=np.float32) / nf)).astype(np.float32)
    ang = np.concatenate([row[:, None] * inv, col[:, None] * inv], -1)
    cos = np.cos(ang).astype(np.float32).T
    sin = np.sin(ang).astype(np.float32).T
    rc = np.ones((128, TA), np.float32)
    rs = np.zeros((128, TA), np.float32)
    for m in range(2):
        rc[m * 64:m * 64 + 32, CTX:] = cos
        rc[m * 64 + 32:m * 64 + 64, CTX:] = cos
        rs[m * 64:m * 64 + 32, CTX:] = -sin
        rs[m * 64 + 32:m * 64 + 64, CTX:] = sin
    c["ropec"] = rc
    c["ropes"] = rs
    feats = np.zeros((17, TA), np.float32)
    decay = np.zeros((TA, 512), np.float32)
    decayb = np.zeros((TA, 512), np.float32)
    deltas = np.abs(np.linspace(math.log(1e-2) / 1.5, math.log(1e-2) / 0.3, 512, dtype=np.float32))
    for (t0, L) in ((0, CTX), (CTX, SEQ)):
        pos = np.arange(L, dtype=np.float32)
        t = pos / max(L - 1, 1)
        bands = np.linspace(1e-4, 7, 8, dtype=np.float32)
        ang = (2.0 * math.pi * pos / L)[:, None] * bands[None]
        f = np.concatenate([t[:, None], np.cos(ang), -np.sin(ang)], -1).astype(np.float32)
        feats[:, t0:t0 + L] = f.T
        dk = np.exp(-t[:, None] * deltas[None]).astype(np.float32)
        decay[t0:t0 + L] = dk
        decayb[t0:t0 + L] = dk
        decayb[t0] = 0.0
    c["hy_feats"] = feats
    c["hy_decay"] = decay
    c["hy_decayb"] = decayb

    def dft_tables(N, npad):
        half = N // 2
        a = np.arange(npad, dtype=np.int64)
        prod = (a[:, None] * a[None, :]) % N
        angm = prod.astype(np.float64) * (2.0 * math.pi / N)
        valid = (a[:, None] <= half) & (a[None, :] <= half)
        tc = np.where(valid, np.cos(angm), 0.0).astype(np.float32).astype(ml_dtypes.bfloat16)
        ts = np.where(valid, np.sin(angm), 0.0).astype(np.float32).astype(ml_dtypes.bfloat16)
        return tc, ts

    c["TC"], c["TS"] = dft_tables(2 * SEQ, NF)
    c["TCc"], c["TSc"] = dft_tables(2 * CTX, NFC)

    def blocked(t):
        nb = t.shape[0] // 128
        return np.ascontiguousarray(t.reshape(nb, 128, nb, 128).transpose(2, 1, 0, 3))

    c["TCf"], c["TSf"] = blocked(c["TC"]), blocked(c["TS"])
    c["TCcf"], c["TScf"] = blocked(c["TCc"]), blocked(c["TSc"])
    wf = np.zeros((128, 72), np.float32)
    for (col0, N, nfb) in ((0, 2 * SEQ, 33), (33, 2 * CTX, 3)):
        for fb in range(nfb):
            f = fb * 128 + np.arange(128)
            w = np.where(f > N // 2, 0.0, np.where((f == 0) | (f == N // 2), 1.0 / N, 2.0 / N)).astype(np.float32)
            wf[:, col0 + fb] = w
            wf[:, 36 + col0 + fb] = -w
    c["hy_wf"] = wf
    return c


def pack_vecs(inp):
    v = np.zeros((DEPTH, NVR, 12288), np.float32)
    for l in range(DEPTH):
        for r, name in enumerate(VEC_ROWS):
            a = np.asarray(inp[name][l], np.float32).reshape(-1)
            v[l, r, :a.size] = a
    return v


def make_in_maps(inp, ncores=8):
    consts = host_consts()
    vecs = pack_vecs(inp)
    shared = {k: np.ascontiguousarray(np.asarray(inp[k], np.float32)) for k in
              ["w_ada", "w_in", "w_a_out", "w_b_out", "w_c_out", "w_o", "w_up", "w_down", "filt_w1", "filt_w2",
               "filt_w3", "conv_a_w", "short_b_w", "conv_f_w"]}
    shared["vecs"] = vecs
    shared["diff_lambda"] = np.asarray(inp["diff_lambda"], np.float32).reshape(DEPTH, 1, 256)
    shared["subln_g"] = np.asarray(inp["subln_g"], np.float32).reshape(DEPTH, 1, 1024)
    shared["final_g"] = np.asarray(inp["final_g"], np.float32).reshape(1, D)
    shared.update(consts)
    maps = []
    for core in range(ncores):
        b = core % 4
        m = dict(shared)
        m["x"] = np.ascontiguousarray(np.asarray(inp["x"][b], np.float32))
        m["ctx"] = np.ascontiguousarray(np.asarray(inp["ctx"][b], np.float32))
        m["c2"] = np.stack([np.asarray(inp["c"][b], np.float32), np.asarray(inp["c_ctx"], np.float32)], 0)
        maps.append(m)
    return maps


NCORES = 4


def kernel(**inputs):
    nc, cx = build()
    maps = make_in_maps(inputs, ncores=NCORES)
    res = run_bass_kernel_spmd(nc, maps, core_ids=list(range(NCORES)))
    out = np.stack([np.asarray(res.results[b]["out"]) for b in range(4)], 0)
    return out.astype(np.float32)
```

```python
import math
from contextlib import ExitStack
import numpy as np
import ml_dtypes
import concourse.bass as bass
import concourse.mybir as mybir
from concourse.bass_utils import run_bass_kernel_spmd

F32 = mybir.dt.float32
BF16 = mybir.dt.bfloat16
AF = mybir.ActivationFunctionType
ALU = mybir.AluOpType
AX = mybir.AxisListType

D = 2048
SEQ = 4096
CTX = 256
TA = CTX + SEQ
DEPTH = 2
NIN = 11776
DFF = 5632
EPS = 1e-6
OFF_A, OFF_G, OFF_B, OFF_Q, OFF_K, OFF_V, OFF_GATE = 0, 512, 1024, 2560, 3584, 4608, 5632
TILES = [(0, 256)] + [(256 + 512 * i, 512) for i in range(8)]
NF = 4224
NFC = 384

ENGS = ("pe", "act", "dve", "pool", "sp")


class Tok:
    __slots__ = ("w", "r", "excl")

    def __init__(self, excl=False):
        self.w = []
        self.r = []
        self.excl = excl


class Sched:
    LIMIT = 32000

    def __init__(self, nc, es, n_dma_sems=8):
        self.nc = nc
        self.es = es
        self.lists = {e: [] for e in ENGS}
        self.count = {e: 0 for e in ENGS}
        self.gen = {e: 0 for e in ENGS}
        self.known = {e: {} for e in ENGS}
        self.semobj = {}
        self.nsem = 0
        for e in ENGS:
            self.semobj[("e", e, 0)] = self._newsem()
        self.dkeys, self.dnext, self.dval, self.dgen = {}, {}, {}, {}
        for q in ("sp", "pool", "act"):
            self.dkeys[q] = []
            for i in range(n_dma_sems):
                key = ("d", q, i, 0)
                self.semobj[key] = self._newsem()
                self.dkeys[q].append(key)
            self.dnext[q] = 0
            self.dval[q] = [0] * n_dma_sems
            self.dgen[q] = [0] * n_dma_sems
        self.final = {}

    def _newsem(self):
        self.nsem += 1
        return self.es.enter_context(self.nc.semaphore(f"sm{self.nsem}"))

    def _wait(self, eng, ev):
        if ev is None:
            return
        key, val = ev
        if key[0] == "e" and key[1] == eng and eng == "pe":
            return
        kn = self.known[eng]
        if kn.get(key, 0) >= val:
            return
        kn[key] = val
        self.lists[eng].append(("wait", key, val))

    def _deps(self, eng, reads, writes, acc=False):
        for t in reads:
            for ev in t.w:
                self._wait(eng, ev)
        for t in writes:
            if not acc:
                for ev in t.w:
                    self._wait(eng, ev)
            for ev in t.r:
                self._wait(eng, ev)

    def _record(self, ev, reads, writes, acc=False):
        for t in reads:
            t.r.append(ev)
        for t in writes:
            if acc and not t.r:
                t.w.append(ev)
            else:
                t.w = [ev]
                t.r = []

    def op(self, eng, fn, reads=(), writes=()):
        if any(t.excl for t in reads):
            writes = list(writes) + [t for t in reads if t.excl and t not in writes]
            reads = [t for t in reads if not t.excl]
        self._deps(eng, reads, writes)
        if self.count[eng] >= self.LIMIT:
            self.final[("e", eng, self.gen[eng])] = self.count[eng]
            self.gen[eng] += 1
            self.count[eng] = 0
            self.semobj[("e", eng, self.gen[eng])] = self._newsem()
        self.count[eng] += 1
        key = ("e", eng, self.gen[eng])
        ev = (key, self.count[eng])
        self.lists[eng].append(("op", fn, key))
        self._record(ev, reads, writes)
        return ev

    def dma(self, q, out, in_, reads=(), writes=(), acc=False):
        self._deps(q, reads, writes, acc)
        i = self.dnext[q]
        self.dnext[q] = (i + 1) % len(self.dkeys[q])
        key = self.dkeys[q][i]
        if self.dval[q][i] > 0:
            self._wait(q, (key, self.dval[q][i]))
        if self.dval[q][i] >= self.LIMIT:
            self.final[key] = self.dval[q][i]
            self.dgen[q][i] += 1
            key = ("d", q, i, self.dgen[q][i])
            self.semobj[key] = self._newsem()
            self.dkeys[q][i] = key
            self.dval[q][i] = 0
        self.dval[q][i] += 16
        ev = (key, self.dval[q][i])
        self.lists[q].append(("dma", out, in_, key))
        self._record(ev, reads, writes, acc)
        return ev

    def _all_events(self):
        evs = [(("e", e, self.gen[e]), self.count[e]) for e in ENGS if self.count[e] > 0]
        for q in self.dkeys:
            for i, v in enumerate(self.dval[q]):
                if v > 0:
                    evs.append((self.dkeys[q][i], v))
        evs += list(self.final.items())
        return evs

    def barrier(self, engines=ENGS):
        evs = self._all_events()
        for e in engines:
            kn = self.known[e]
            for key, val in evs:
                if kn.get(key, 0) < val:
                    kn[key] = val
                    self.lists[e].append(("wait", key, val))

    def emit(self, block):
        lists, semobj = self.lists, self.semobj

        def replay(engname, e):
            for item in lists[engname]:
                k = item[0]
                if k == "wait":
                    e.wait_ge(semobj[item[1]], item[2])
                elif k == "op":
                    item[1](e).then_inc(semobj[item[2]], 1)
                else:
                    e.dma_start(out=item[1], in_=item[2]).then_inc(semobj[item[3]], 16)

        @block.tensor
        def _(e):
            replay("pe", e)

        @block.scalar
        def _(e):
            replay("act", e)

        @block.vector
        def _(e):
            replay("dve", e)

        @block.gpsimd
        def _(e):
            replay("pool", e)

        @block.sync
        def _(e):
            replay("sp", e)


def I_mm(out, lhsT, rhs, start, stop):
    return lambda e: e.matmul(out, lhsT, rhs, start=start, stop=stop)


def I_tr(out, in_, ident):
    return lambda e: e.transpose(out, in_, ident)


def I_act(out, in_, func, bias=None, scale=None, accum_out=None):
    kw = {}
    if bias is not None:
        kw["bias"] = bias
    if scale is not None:
        kw["scale"] = scale
    if accum_out is not None:
        kw["accum_out"] = accum_out
    return lambda e: e.activation(out=out, in_=in_, func=func, **kw)


def I_copy(out, in_):
    return lambda e: e.tensor_copy(out=out, in_=in_)


def I_tt(out, a, b, op):
    return lambda e: e.tensor_tensor(out=out, in0=a, in1=b, op=op)


def I_ts(out, a, s1, s2, op0, op1=None):
    if op1 is None:
        return lambda e: e.tensor_scalar(out=out, in0=a, scalar1=s1, scalar2=None, op0=op0)
    return lambda e: e.tensor_scalar(out=out, in0=a, scalar1=s1, scalar2=s2, op0=op0, op1=op1)


def I_stt(out, a, s, b, op0, op1):
    return lambda e: e.scalar_tensor_tensor(out=out, in0=a, scalar=s, in1=b, op0=op0, op1=op1)


def I_memset(out, v):
    return lambda e: e.memset(out, v)


def I_recip(out, in_):
    return lambda e: e.reciprocal(out=out, in_=in_)


def I_red(out, in_, op):
    if op == "max":
        return lambda e: e.reduce_max(out=out, in_=in_, axis=AX.X)
    return lambda e: e.reduce_sum(out=out, in_=in_, axis=AX.X)


_UID = [0]


def SB(es, nc, name, shape, dt):
    _UID[0] += 1
    return es.enter_context(nc.sbuf_tensor(f"{name}_u{_UID[0]}", shape, dt))


class Cx:
    pass


HOPT = {}
USE_H2 = True


def build(dbg=(), stop_after=None, n_layers=DEPTH, only=None, feed=(), hopt=None):
    global HOPT
    HOPT = hopt or {}
    nc = bass.Bass("TRN2", target_bir_lowering=False)
    cx = Cx()
    cx.nc = nc
    cx.dbg_out = {}

    def din(name, shape, dt=F32):
        return nc.dram_tensor(name, list(shape), dt, kind="ExternalInput").ap()

    def scratch(name, shape, dt):
        if name in feed:
            return din(name, shape, dt)
        if name in dbg:
            ap = nc.dram_tensor(name, list(shape), dt, kind="ExternalOutput").ap()
            cx.dbg_out[name] = ap
            return ap
        return nc.dram_tensor(name, list(shape), dt).ap()

    specs = {
        "x": ([SEQ, D], F32), "ctx": ([CTX, D], F32), "c2": ([2, D], F32),
        "w_ada": ([DEPTH, D, 6 * D], F32), "w_in": ([DEPTH, D, NIN], F32),
        "w_a_out": ([DEPTH, 512, D], F32), "w_b_out": ([DEPTH, 512, D], F32), "w_c_out": ([DEPTH, 1024, D], F32),
        "w_o": ([DEPTH, D, D], F32), "w_up": ([DEPTH, D, 2 * DFF], F32), "w_down": ([DEPTH, DFF, D], F32),
        "filt_w1": ([DEPTH, 17, 64], F32), "filt_w2": ([DEPTH, 64, 64], F32), "filt_w3": ([DEPTH, 64, 2048], F32),
        "vecs": ([DEPTH, NVR, 12288], F32), "conv_a_w": ([DEPTH, 31, 512], F32), "short_b_w": ([DEPTH, 3, 1536], F32),
        "conv_f_w": ([DEPTH, 3, 2 * DFF], F32), "diff_lambda": ([DEPTH, 1, 256], F32), "subln_g": ([DEPTH, 1, 1024], F32),
        "final_g": ([1, D], F32), "ident": ([128, 128], F32), "ropec": ([128, TA], F32), "ropes": ([128, TA], F32),
        "hy_feats": ([17, TA], F32), "hy_decay": ([TA, 512], F32), "hy_decayb": ([TA, 512], F32), "hy_wf": ([128, 72], F32),
        "TC": ([NF, NF], BF16), "TS": ([NF, NF], BF16), "TCc": ([NFC, NFC], BF16), "TSc": ([NFC, NFC], BF16),
        "TCf": ([33, 128, 33, 128], BF16), "TSf": ([33, 128, 33, 128], BF16),
        "TCcf": ([3, 128, 3, 128], BF16), "TScf": ([3, 128, 3, 128], BF16),
    }

    class LazyI(dict):
        def __missing__(self, name):
            shape, dt = specs[name]
            ap = din(name, shape, dt)
            self[name] = ap
            return ap

    I = LazyI()
    if only is None:
        for name in specs:
            I[name]
    out = nc.dram_tensor("out", [SEQ, D], F32, kind="ExternalOutput").ap()
    cx.I = I
    cx.out = out

    cx.xT = scratch("xT", [D, TA], F32)
    cx.hT = scratch("hT", [D, TA], BF16)
    cx.yA = scratch("yA", [512, TA], F32)
    cx.bT = scratch("bT", [1536, TA], F32)
    cx.qT = scratch("qT", [1024, TA], BF16)
    cx.kT = scratch("kT", [1024, TA], BF16)
    cx.V = scratch("V", [TA, 1024], BF16)
    cx.gates = scratch("gates", [6144, TA], BF16)
    cx.zcat = scratch("zcat", [2048, TA], BF16)
    cx.fT = scratch("fT", [DFF, TA], BF16)
    cx.modd = scratch("modd", [128, 96 * 2], F32)
    cx.ubuf = scratch("ubuf", [1536, TA], F32)
    cx.hpm = scratch("hpm", [TA, 2048], BF16)
    cx.Hd = scratch("Hd", [NF + NFC, 2048], F32)
    cx.z1 = scratch("z1", [512, TA], F32)

    with ExitStack() as es:
        S = Sched(nc, es)
        cx.S = S
        cx.ident = nc.alloc_sbuf_tensor("sb_ident", [128, 128], F32)
        cx.identb = nc.alloc_sbuf_tensor("sb_identb", [128, 128], BF16)
        cx.ones_b = nc.alloc_sbuf_tensor("sb_ones_b", [128, 128], BF16)
        cx.ones_f = nc.alloc_sbuf_tensor("sb_ones_f", [128, 128], F32)
        cx.T_const = Tok()
        cx.epsc = nc.alloc_sbuf_tensor("sb_epsc", [128, 1], F32)
        cx.mod = nc.alloc_sbuf_tensor("sb_mod", [128, 6 * 16 * 2], F32)
        cx.modA = nc.alloc_sbuf_tensor("sb_modA", [128, 2 * 16 * 2], F32)
        cx.T_mod = Tok()
        cx.vec = nc.alloc_sbuf_tensor("sb_vec", [128, 96 * NVR], F32)
        cx.T_vec = Tok()
        cx.ps = [nc.alloc_psum_tensor(f"ps{i}", [128, 512], F32) for i in range(8)]
        cx.T_ps = [Tok(excl=True) for _ in range(8)]
        cx.ps_next = 0
        cx.ps_reserved = set()

        S.dma("sp", cx.ident[:], I["ident"][:, :], writes=[cx.T_const])
        S.op("dve", I_copy(cx.identb[:], cx.ident[:]), reads=[cx.T_const], writes=[cx.T_const])
        S.op("dve", I_memset(cx.ones_b[:], 1.0), writes=[cx.T_const])
        S.op("dve", I_memset(cx.ones_f[:], 1.0), writes=[cx.T_const])
        S.op("dve", I_memset(cx.epsc[:], EPS), writes=[cx.T_const])

        phases = [("A", phase_A, None)]
        for l in range(n_layers):
            phases += [("vec%d" % l, phase_vec, l), ("B%d" % l, phase_B, l), ("C%d" % l, phase_norm, (l, 0)),
                       ("D%d" % l, phase_D, l), ("E%d" % l, phase_E, l), ("F%d" % l, phase_F, l), ("H%d" % l, phase_H2 if USE_H2 else phase_H, l),
                       ("G%d" % l, phase_G, l), ("I%d" % l, phase_norm, (l, 1)), ("J%d" % l, phase_J, l)]
        phases.append(("final", phase_final, None))
        if only is not None:
            phases = [p for p in phases if p[0] in only]
        for name, fn, arg in phases:
            S.barrier()
            if arg is None:
                fn(cx)
            else:
                fn(cx, arg)
            if stop_after == name:
                break
        S.barrier()
        with nc.Block() as block:
            S.emit(block)
    return nc, cx


def next_ps(cx):
    while True:
        i = cx.ps_next
        cx.ps_next = (i + 1) % 8
        if i not in cx.ps_reserved:
            return cx.ps[i], cx.T_ps[i]


VEC_ROWS = ["b_ada", "norm1_g", "norm2_g", "b_gate", "conv_a_b", "ln_a_g", "ln_a_b", "short_b_b",
            "hyena_skip", "conv_f_b", "filt_b1", "filt_b2"]
NVR = len(VEC_ROWS)


def vec_ap(cx, row, chunk):
    r = VEC_ROWS.index(row)
    o = chunk * NVR + r
    return cx.vec[:, o:o + 1]


def phase_A(cx):
    nc, S = cx.nc, cx.S
    xT_r = cx.xT.rearrange("(c p) t -> p c t", p=128)
    with ExitStack() as es:
        xin = [SB(es, nc, f"A_xin{i}", [128, D], F32) for i in range(2)]
        xo = [SB(es, nc, f"A_xo{i}", [128, D], F32) for i in range(2)]
        Tin = [Tok(), Tok()]
        To = [Tok(), Tok()]
        for bi in range(TA // 128):
            b = bi % 2
            src = cx.I["ctx"][bi * 128:(bi + 1) * 128, :] if bi < 2 else cx.I["x"][(bi - 2) * 128:(bi - 1) * 128, :]
            S.dma("sp", xin[b][:], src, writes=[Tin[b]])
            for g4 in range(4):
                ps, Tp = next_ps(cx)
                for j in range(4):
                    c = g4 * 4 + j
                    S.op("pe", I_tr(ps[:, j * 128:(j + 1) * 128], xin[b][:, c * 128:(c + 1) * 128], cx.ident[:]),
                         reads=[Tin[b], cx.T_const], writes=[Tp])
                eng = "dve" if g4 % 2 == 0 else "act"
                if eng == "dve":
                    S.op("dve", I_copy(xo[b][:, g4 * 512:(g4 + 1) * 512], ps[:]), reads=[Tp], writes=[To[b]])
                else:
                    S.op("act", I_act(xo[b][:, g4 * 512:(g4 + 1) * 512], ps[:], AF.Identity), reads=[Tp], writes=[To[b]])
            S.dma("pool", xT_r[:, :, bi * 128:(bi + 1) * 128], xo[b][:].rearrange("p (c t) -> p c t", c=16),
                  reads=[To[b]])


def phase_vec(cx, l):
    nc, S = cx.nc, cx.S
    with ExitStack() as es:
        raw = SB(es, nc, "V_raw", [NVR, 12288], F32)
        Traw = Tok()
        S.dma("sp", raw[:], cx.I["vecs"][l], writes=[Traw])
        for g in range(96 // 16):
            ps, Tp = next_ps(cx)
            for j in range(16):
                ch = g * 16 + j
                S.op("pe", I_tr(ps[:, j * NVR:(j + 1) * NVR], raw[:, ch * 128:(ch + 1) * 128], cx.ident[0:NVR, 0:NVR]),
                     reads=[Traw, cx.T_const], writes=[Tp])
            S.op("dve", I_copy(cx.vec[:, g * 16 * NVR:(g + 1) * 16 * NVR], ps[:, 0:16 * NVR]), reads=[Tp],
                 writes=[cx.T_vec])


def load_w_chunk(cx, wbuf, Tw, segs, K):
    S = cx.S
    KC = K // 128
    for (dc, ap) in segs:
        n = ap.shape[1]
        S.dma("pool", wbuf[:, 0:KC, dc:dc + n], ap.rearrange("(kc p) n -> p kc n", p=128), writes=[Tw], acc=True)


def phase_B(cx, l):
    nc, S = cx.nc, cx.S
    with ExitStack() as es:
        craw = SB(es, nc, "B_craw", [2, D], F32)
        csf = SB(es, nc, "B_csf", [128, 32], F32)
        NB = 6
        wb = [SB(es, nc, f"B_w{i}", [128, 16, 128], F32) for i in range(NB)]
        Tw = [Tok() for _ in range(NB)]
        Tc = Tok()
        S.dma("sp", craw[:], cx.I["c2"][:, :], writes=[Tc])
        for g in range(2):
            ps, Tp = next_ps(cx)
            for j in range(8):
                ch = g * 8 + j
                S.op("pe", I_tr(ps[:, j * 2:(j + 1) * 2], craw[:, ch * 128:(ch + 1) * 128], cx.ident[0:2, 0:2]),
                     reads=[Tc, cx.T_const], writes=[Tp])
            S.op("act", I_act(csf[:, g * 16:(g + 1) * 16], ps[:, 0:16], AF.Silu), reads=[Tp], writes=[Tc])
        wsrc = cx.I["w_ada"][l]
        nj = 96

        def issue_load(j):
            q = "sp" if j % 2 == 0 else "act"
            S.dma(q, wb[j % NB][:], wsrc[:, j * 128:(j + 1) * 128].rearrange("(kc p) n -> p kc n", p=128), writes=[Tw[j % NB]])

        for j in range(min(NB - 1, nj)):
            issue_load(j)
        for j in range(nj):
            if j + NB - 1 < nj:
                issue_load(j + NB - 1)
            ps, Tp = next_ps(cx)
            o = 0
            for kc in range(16):
                S.op("pe", I_mm(ps[:, o:o + 2], wb[j % NB][:, kc, :], csf[:, kc * 2:(kc + 1) * 2], kc == 0, kc == 15),
                     reads=[Tw[j % NB], Tc], writes=[Tp])
            S.op("dve", I_ts(cx.mod[:, j * 2:(j + 1) * 2], ps[:, o:o + 2], vec_ap(cx, "b_ada", j), None, ALU.add),
                 reads=[Tp, cx.T_vec], writes=[cx.T_mod])
        for sub in range(2):
            gname = "norm1_g" if sub == 0 else "norm2_g"
            which_scale = 1 + 3 * sub
            for c in range(16):
                S.op("dve", I_ts(cx.modA[:, (sub * 16 + c) * 2:(sub * 16 + c) * 2 + 2],
                                 cx.mod[:, (which_scale * 16 + c) * 2:(which_scale * 16 + c) * 2 + 2],
                                 1.0, vec_ap(cx, gname, c), ALU.add, ALU.mult),
                     reads=[cx.T_mod, cx.T_vec], writes=[cx.T_mod])
        if "modd" in cx.dbg_out:
            S.dma("sp", cx.modd[:, :], cx.mod[:], reads=[cx.T_mod])


def mod_ap(cx, which, c, grp):
    o = (which * 16 + c) * 2 + grp
    return cx.mod[:, o:o + 1]


def modA_ap(cx, sub, c, grp):
    o = (sub * 16 + c) * 2 + grp
    return cx.modA[:, o:o + 1]


def phase_norm(cx, arg):
    l, sub = arg
    nc, S = cx.nc, cx.S
    xT_r = cx.xT.rearrange("(c p) t -> p c t", p=128)
    hT_r = cx.hT.rearrange("(c p) t -> p c t", p=128)
    with ExitStack() as es:
        xb = [SB(es, nc, f"N_x{i}", [128, 16, 512], F32) for i in range(2)]
        hb = [SB(es, nc, f"N_h{i}", [128, 16, 512], BF16) for i in range(2)]
        sq = SB(es, nc, "N_sq", [128, 16, 512], BF16)
        rs = SB(es, nc, "N_rs", [128, 512], F32)
        tmp = [SB(es, nc, f"N_tmp{i}", [128, 512], F32) for i in range(2)]
        Tx = [Tok(), Tok()]
        Th = [Tok(), Tok()]
        Tsq, Trs = Tok(), Tok()
        Ttmp = [Tok(), Tok()]
        for ti, (t0, n) in enumerate(TILES):
            b = ti % 2
            grp = 1 if t0 < CTX else 0
            S.dma("sp", xb[b][:, :, 0:n], xT_r[:, :, t0:t0 + n], writes=[Tx[b]])
            for c in range(16):
                S.op("act", I_act(sq[:, c, 0:n], xb[b][:, c, 0:n], AF.Square), reads=[Tx[b]], writes=[Tsq])
            ps, Tp = next_ps(cx)
            for c in range(16):
                S.op("pe", I_mm(ps[:, 0:n], cx.ones_b[:], sq[:, c, 0:n], c == 0, c == 15), reads=[Tsq, cx.T_const],
                     writes=[Tp])
            S.op("act", I_act(rs[:, 0:n], ps[:, 0:n], AF.Sqrt, bias=EPS, scale=1.0 / D), reads=[Tp], writes=[Trs])
            S.op("dve", I_recip(rs[:, 0:n], rs[:, 0:n]), reads=[Trs], writes=[Trs])
            for c in range(16):
                tb = c % 2
                S.op("dve", I_tt(tmp[tb][:, 0:n], xb[b][:, c, 0:n], rs[:, 0:n], ALU.mult), reads=[Tx[b], Trs],
                     writes=[Ttmp[tb]])
                S.op("act", I_act(hb[b][:, c, 0:n], tmp[tb][:, 0:n], AF.Identity,
                                  bias=mod_ap(cx, 3 * sub, c, grp), scale=modA_ap(cx, sub, c, grp)),
                     reads=[Ttmp[tb], cx.T_mod], writes=[Th[b]])
            S.dma("act", hT_r[:, :, t0:t0 + n], hb[b][:, :, 0:n], reads=[Th[b]])


def run_jobs(cx, jobs, act, Tact, tiles, K, NB=6, tag="J"):
    nc, S = cx.nc, cx.S
    KC = K // 128
    with ExitStack() as es:
        wb = [SB(es, nc, f"{tag}_w{i}", [128, KC, 128], BF16) for i in range(NB)]
        Tw = [Tok() for _ in range(NB)]
        flat = []
        for ji, jb in enumerate(jobs):
            for ci in range(len(jb["cols"])):
                flat.append((ji, ci))
        slot = {}

        def issue(fi):
            ji, ci = flat[fi]
            s = fi % NB
            slot[(ji, ci)] = s
            load_w_chunk(cx, wb[s], Tw[s], jobs[ji]["cols"][ci], K)

        nxt = 0
        fpos = 0
        for ji, jb in enumerate(jobs):
            ncol = len(jb["cols"])
            assert ncol <= NB
            while nxt < len(flat) and nxt < fpos + NB:
                issue(nxt)
                nxt += 1
            for ti, (c0, n, t0) in enumerate(tiles):
                pss = []
                for ci in range(ncol):
                    s = slot[(ji, ci)]
                    ps, Tp = next_ps(cx)
                    for kc in range(KC):
                        S.op("pe", I_mm(ps[:, 0:n], wb[s][:, kc, :], act[:, kc, c0:c0 + n], kc == 0, kc == KC - 1),
                             reads=[Tw[s], Tact], writes=[Tp])
                    pss.append((ps, Tp))
                jb["epi"](jb, ti, t0, n, pss)
            fpos += ncol
        S.barrier()


class Stage:
    def __init__(self, nc, es, name, dt, n):
        self.t = [SB(es, nc, f"{name}{i}", [128, 512], dt) for i in range(n)]
        self.T = [Tok() for _ in range(n)]
        self.i = 0

    def next(self):
        i = self.i
        self.i = (i + 1) % len(self.t)
        return self.t[i], self.T[i]


def phase_D(cx, l):
    nc, S = cx.nc, cx.S
    W = cx.I["w_in"][l]
    hT_r = cx.hT.rearrange("(c p) t -> p c t", p=128)
    halves = [(0, 2304), (2304, 2048)]
    for (s0, sn) in halves:
        S.barrier()
        with ExitStack() as es:
            act = SB(es, nc, "D_act", [128, 16, 2304], BF16)
            Tact = Tok()
            rc = SB(es, nc, "D_rc", [128, 2304], F32)
            rsn = SB(es, nc, "D_rs", [128, 2304], F32)
            Trope = Tok()
            for c in range(16):
                S.dma("sp", act[:, c, 0:sn], hT_r[:, c, s0:s0 + sn], writes=[Tact], acc=True)
            S.dma("sp", rc[:, 0:sn], cx.I["ropec"][:, s0:s0 + sn], writes=[Trope], acc=True)
            S.dma("sp", rsn[:, 0:sn], cx.I["ropes"][:, s0:s0 + sn], writes=[Trope], acc=True)
            tiles = [(t0 - s0, n, t0) for (t0, n) in TILES if s0 <= t0 < s0 + sn]
            sf = Stage(nc, es, "D_sf", F32, 4)
            sb = Stage(nc, es, "D_sb", BF16, 4)
            st = Stage(nc, es, "D_st", F32, 4)

            def col(c0):
                return [(0, W[:, c0:c0 + 128])]

            def col_swapped(c0):
                return [(0, W[:, c0 + 32:c0 + 64]), (32, W[:, c0:c0 + 32]),
                        (64, W[:, c0 + 96:c0 + 128]), (96, W[:, c0 + 64:c0 + 96])]

            jobs = []

            def epi_glu(jb, ti, t0, n, pss):
                (pa, Ta), (pg, Tg) = pss
                t1, T1 = st.next()
                S.op("act", I_act(t1[:, 0:n], pg[:, 0:n], AF.Sigmoid), reads=[Tg], writes=[T1])
                o, To = sf.next()
                S.op("dve", I_tt(o[:, 0:n], pa[:, 0:n], t1[:, 0:n], ALU.mult), reads=[Ta, T1], writes=[To])
                r0 = jb["j"] * 128
                S.dma("pool", cx.yA[r0:r0 + 128, t0:t0 + n], o[:, 0:n], reads=[To])

            for j in range(4):
                jobs.append(dict(cols=[col(OFF_A + j * 128), col(OFF_G + j * 128)], epi=epi_glu, j=j))

            def epi_b(jb, ti, t0, n, pss):
                (p, Tp), = pss
                o, To = sf.next()
                S.op("act", I_act(o[:, 0:n], p[:, 0:n], AF.Identity), reads=[Tp], writes=[To])
                r0 = jb["j"] * 128
                S.dma("pool", cx.bT[r0:r0 + 128, t0:t0 + n], o[:, 0:n], reads=[To])

            for j in range(12):
                jobs.append(dict(cols=[col(OFF_B + j * 128)], epi=epi_b, j=j))

            def epi_rope(jb, ti, t0, n, pss):
                (p, Tp), (psw, Tsw) = pss
                c0 = t0 - s0
                t1, T1 = st.next()
                t2, T2 = st.next()
                S.op("dve", I_tt(t1[:, 0:n], p[:, 0:n], rc[:, c0:c0 + n], ALU.mult), reads=[Tp, Trope], writes=[T1])
                S.op("dve", I_tt(t2[:, 0:n], psw[:, 0:n], rsn[:, c0:c0 + n], ALU.mult), reads=[Tsw, Trope], writes=[T2])
                o, To = sb.next()
                S.op("dve", I_tt(o[:, 0:n], t1[:, 0:n], t2[:, 0:n], ALU.add), reads=[T1, T2], writes=[To])
                r0 = jb["j"] * 128
                S.dma("pool", jb["dst"][r0:r0 + 128, t0:t0 + n], o[:, 0:n], reads=[To])

            for j in range(8):
                jobs.append(dict(cols=[col(OFF_Q + j * 128), col_swapped(OFF_Q + j * 128)], epi=epi_rope, j=j, dst=cx.qT))
            for j in range(8):
                jobs.append(dict(cols=[col(OFF_K + j * 128), col_swapped(OFF_K + j * 128)], epi=epi_rope, j=j, dst=cx.kT))

            def epi_gate(jb, ti, t0, n, pss):
                (p, Tp), = pss
                o, To = sb.next()
                S.op("act", I_act(o[:, 0:n], p[:, 0:n], AF.Sigmoid, bias=vec_ap(cx, "b_gate", jb["j"])),
                     reads=[Tp, cx.T_vec], writes=[To])
                r0 = jb["j"] * 128
                S.dma("pool", cx.gates[r0:r0 + 128, t0:t0 + n], o[:, 0:n], reads=[To])

            for j in range(48):
                jobs.append(dict(cols=[col(OFF_GATE + j * 128)], epi=epi_gate, j=j))

            run_jobs(cx, jobs, act, Tact, tiles, D, NB=6, tag="D")

            wv = SB(es, nc, "D_wv", [128, 16, 1024], BF16)
            Twv = Tok()
            for hh in range(2):
                S.dma("pool", wv[:, :, hh * 512:(hh + 1) * 512],
                      W[:, OFF_V + hh * 512:OFF_V + (hh + 1) * 512].rearrange("(kc p) n -> p kc n", p=128), writes=[Twv], acc=True)
            for tb in range(sn // 128):
                for hh in range(2):
                    ps, Tp = next_ps(cx)
                    for kc in range(16):
                        S.op("pe", I_mm(ps[:], act[:, kc, tb * 128:(tb + 1) * 128], wv[:, kc, hh * 512:(hh + 1) * 512],
                                        kc == 0, kc == 15), reads=[Tact, Twv], writes=[Tp])
                    o, To = sb.next()
                    if hh == 0:
                        S.op("act", I_act(o[:], ps[:], AF.Identity), reads=[Tp], writes=[To])
                    else:
                        S.op("dve", I_copy(o[:], ps[:]), reads=[Tp], writes=[To])
                    tg = s0 + tb * 128
                    S.dma("pool", cx.V[tg:tg + 128, hh * 512:(hh + 1) * 512], o[:], reads=[To])


def dma_mid_split(S, q, out, in_, nmid, per, reads=(), writes=()):
    for a in range(0, nmid, per):
        b = min(nmid, a + per)
        S.dma(q, out[:, a:b, :], in_[:, a:b, :], reads=reads, writes=writes, acc=True)


def split_tiles(s0, sn, maxn=512):
    out = []
    t = s0
    end = s0 + sn
    while t < end:
        lim = CTX if t < CTX else end
        n = min(maxn, lim - t, end - t)
        out.append((t, n))
        t += n
    return out


def load_T(cx, es, dram2d, R, C, name):
    nc, S = cx.nc, cx.S
    dst = SB(es, nc, name + "_T", [128, (C // 128) * R], F32)
    Tdst = Tok()
    with ExitStack() as es2:
        raw = SB(es2, nc, name + "_raw", [R, C], F32)
        Traw = Tok()
        S.dma("sp", raw[:], dram2d, writes=[Traw])
        nch = C // 128
        per = 512 // R
        for g in range(0, nch, per):
            ps, Tp = next_ps(cx)
            k = min(per, nch - g)
            for j in range(k):
                ch = g + j
                S.op("pe", I_tr(ps[:, j * R:(j + 1) * R], raw[:, ch * 128:(ch + 1) * 128], cx.ident[0:R, 0:R]),
                     reads=[Traw, cx.T_const], writes=[Tp])
            S.op("dve", I_copy(dst[:, g * R:(g + k) * R], ps[:, 0:k * R]), reads=[Tp], writes=[Tdst])
        S.barrier()
    return dst, Tdst


def phase_E(cx, l):
    nc, S = cx.nc, cx.S
    segs = [(0, CTX), (CTX, SEQ)] if l < DEPTH - 1 else [(CTX, SEQ)]
    with ExitStack() as es:
        cw, Tcw = load_T(cx, es, cx.I["conv_a_w"][l], 31, 512, "E_cw")
        conv = SB(es, nc, "E_conv", [128, 4, TA], F32)
        Tconv = [Tok() for _ in range(4)]
        y = [SB(es, nc, f"E_y{i}", [128, TA], F32) for i in range(2)]
        Ty = [Tok(), Tok()]
        acc2 = SB(es, nc, "E_acc2", [128, TA], F32)
        Tacc = Tok()
        for cc in range(4):
            b = cc % 2
            S.dma("sp", y[b][:], cx.yA[cc * 128:(cc + 1) * 128, :], writes=[Ty[b]])
            for (g0, gl) in segs:
                S.op("dve", I_ts(conv[:, cc, g0:g0 + gl], y[b][:, g0:g0 + gl], cw[:, cc * 31 + 15:cc * 31 + 16],
                                 vec_ap(cx, "conv_a_b", cc), ALU.mult, ALU.add),
                     reads=[Ty[b], Tcw, cx.T_vec], writes=[Tconv[cc]])
                for k in range(15):
                    o = 15 - k
                    S.op("dve", I_stt(conv[:, cc, g0 + o:g0 + gl], y[b][:, g0:g0 + gl - o], cw[:, cc * 31 + k:cc * 31 + k + 1],
                                      conv[:, cc, g0 + o:g0 + gl], ALU.mult, ALU.add),
                         reads=[Ty[b], Tcw], writes=[Tconv[cc]])
                for k in range(16, 31):
                    o = k - 15
                    S.op("dve", I_stt(conv[:, cc, g0:g0 + gl - o], y[b][:, g0 + o:g0 + gl], cw[:, cc * 31 + k:cc * 31 + k + 1],
                                      conv[:, cc, g0:g0 + gl - o], ALU.mult, ALU.add),
                         reads=[Ty[b], Tcw], writes=[Tconv[cc]])
        sq = SB(es, nc, "E_sq", [128, 4, 512], F32)
        Tsq = Tok()
        mt = SB(es, nc, "E_m", [128, 512], F32)
        vt = SB(es, nc, "E_v", [128, 512], F32)
        Tm, Tv = Tok(), Tok()
        d = [SB(es, nc, f"E_d{i}", [128, 512], F32) for i in range(2)]
        Td = [Tok(), Tok()]
        so = Stage(nc, es, "E_so", BF16, 4)
        tiles = [t for t in TILES if (l < DEPTH - 1 or t[0] >= CTX)]
        for (t0, n) in tiles:
            ps1, Tp1 = next_ps(cx)
            for cc in range(4):
                S.op("pe", I_mm(ps1[:, 0:n], cx.ones_f[:], conv[:, cc, t0:t0 + n], cc == 0, cc == 3),
                     reads=[Tconv[cc], cx.T_const], writes=[Tp1])
            for cc in range(4):
                S.op("act", I_act(sq[:, cc, 0:n], conv[:, cc, t0:t0 + n], AF.Square), reads=[Tconv[cc]], writes=[Tsq])
            ps2, Tp2 = next_ps(cx)
            for cc in range(4):
                S.op("pe", I_mm(ps2[:, 0:n], cx.ones_f[:], sq[:, cc, 0:n], cc == 0, cc == 3), reads=[Tsq, cx.T_const],
                     writes=[Tp2])
            S.op("dve", I_ts(mt[:, 0:n], ps1[:, 0:n], 1.0 / 512, None, ALU.mult), reads=[Tp1], writes=[Tm])
            S.op("dve", I_tt(vt[:, 0:n], mt[:, 0:n], mt[:, 0:n], ALU.mult), reads=[Tm], writes=[Tv])
            S.op("dve", I_stt(vt[:, 0:n], ps2[:, 0:n], 1.0 / 512, vt[:, 0:n], ALU.mult, ALU.subtract), reads=[Tp2, Tv],
                 writes=[Tv])
            S.op("act", I_act(vt[:, 0:n], vt[:, 0:n], AF.Sqrt, bias=EPS, scale=1.0), reads=[Tv], writes=[Tv])
            S.op("dve", I_recip(vt[:, 0:n], vt[:, 0:n]), reads=[Tv], writes=[Tv])
            for cc in range(4):
                b = cc % 2
                S.op("dve", I_tt(d[b][:, 0:n], conv[:, cc, t0:t0 + n], mt[:, 0:n], ALU.subtract), reads=[Tconv[cc], Tm],
                     writes=[Td[b]])
                S.op("dve", I_tt(d[b][:, 0:n], d[b][:, 0:n], vt[:, 0:n], ALU.mult), reads=[Tv], writes=[Td[b]])
                o, To = so.next()
                S.op("act", I_act(o[:, 0:n], d[b][:, 0:n], AF.Silu, bias=vec_ap(cx, "ln_a_b", cc),
                                  scale=vec_ap(cx, "ln_a_g", cc)), reads=[Td[b], cx.T_vec], writes=[To])
                S.dma("act", cx.zcat[cc * 128:(cc + 1) * 128, t0:t0 + n], o[:, 0:n], reads=[To])


def phase_H(cx, l):
    nc, S = cx.nc, cx.S
    lam_init = 0.8 - 0.6 * math.exp(-0.3 * l)
    Vr = cx.V.rearrange("(kb p) c -> p kb c", p=128)
    NKB = TA // 128
    with ExitStack() as es:
        dl = SB(es, nc, "H_dl", [128, 256], F32)
        lam = SB(es, nc, "H_lam", [128, 8], F32)
        gs = SB(es, nc, "H_gs", [128, 1024], F32)
        Tl, Tgs = Tok(), Tok()
        S.dma("sp", dl[:], cx.I["diff_lambda"][l].to_broadcast([128, 256]), writes=[Tl])
        S.dma("sp", gs[:], cx.I["subln_g"][l].to_broadcast([128, 1024]), writes=[Tgs])
        S.op("dve", I_tt(dl[:, 0:64], dl[:, 0:64], dl[:, 64:128], ALU.mult), reads=[Tl], writes=[Tl])
        S.op("dve", I_tt(dl[:, 128:192], dl[:, 128:192], dl[:, 192:256], ALU.mult), reads=[Tl], writes=[Tl])
        S.op("dve", I_red(lam[:, 0:1], dl[:, 0:64], "sum"), reads=[Tl], writes=[Tl])
        S.op("dve", I_red(lam[:, 1:2], dl[:, 128:192], "sum"), reads=[Tl], writes=[Tl])
        S.op("act", I_act(lam[:, 2:4], lam[:, 0:2], AF.Exp), reads=[Tl], writes=[Tl])
        S.op("dve", I_tt(lam[:, 4:5], lam[:, 3:4], lam[:, 2:3], ALU.subtract), reads=[Tl], writes=[Tl])
        S.op("dve", I_ts(lam[:, 5:6], lam[:, 4:5], -lam_init, None, ALU.add), reads=[Tl], writes=[Tl])
        neglam = lam[:, 5:6]
        S.op("dve", I_ts(gs[:], gs[:], 1.0 - lam_init, None, ALU.mult), reads=[Tgs], writes=[Tgs])

        kT = SB(es, nc, "H_kT", [128, TA], BF16)
        qT = SB(es, nc, "H_qT", [128, TA], BF16)
        vh = SB(es, nc, "H_v", [128, NKB, 128], BF16)
        oT = SB(es, nc, "H_oT", [128, TA], BF16)
        Tk, Tq, Tv, ToT = Tok(), Tok(), Tok(), Tok()
        Sb = [SB(es, nc, f"H_S{m}", [128, TA], F32) for m in range(2)]
        TS_ = [Tok(), Tok()]
        Eb = [SB(es, nc, f"H_E{m}", [128, TA], BF16) for m in range(2)]
        TE = [Tok(), Tok()]
        tmpf = SB(es, nc, "H_tmp", [128, TA], F32)
        Ttmp = Tok()
        Ab = SB(es, nc, "H_A", [128, TA], BF16)
        TA_ = Tok()
        AT = SB(es, nc, "H_AT", [128, NKB, 128], BF16)
        TAT = Tok()
        st = SB(es, nc, "H_st", [128, 32], F32)
        Tst = Tok()
        on = SB(es, nc, "H_on", [128, 128], BF16)
        Ton = Tok()
        sqj = SB(es, nc, "H_sqj", [128, 128], F32)
        Tsqj = Tok()
        qblocks = list(range(NKB)) if l < DEPTH - 1 else list(range(2, NKB))
        qblocks = HOPT.get('qblocks', qblocks)
        stage = HOPT.get('stage', 9)
        for h in range(HOPT.get('heads', 8)):
            S.dma("sp", kT[:], cx.kT[h * 128:(h + 1) * 128, :], writes=[Tk])
            S.dma("sp", qT[:], cx.qT[h * 128:(h + 1) * 128, :], writes=[Tq])
            dma_mid_split(S, "sp", vh, Vr[:, :, h * 128:(h + 1) * 128], NKB, 12, writes=[Tv])
            for qb in qblocks:
                q0 = qb * 128
                nk = CTX if qb < 2 else TA
                nkb = nk // 128
                chunks = split_tiles(0, nk)
                for m in HOPT.get('ms', range(2)):
                    for ci, (k0, n) in enumerate(chunks):
                        ps, Tp = next_ps(cx)
                        S.op("pe", I_mm(ps[:, 0:n], qT[m * 64:(m + 1) * 64, q0:q0 + 128], kT[m * 64:(m + 1) * 64, k0:k0 + n],
                                        True, True), reads=[Tq, Tk], writes=[Tp])
                        if HOPT.get('sub', 3) >= 2:
                            S.op("dve", I_red(st[:, m * 9 + ci:m * 9 + ci + 1], ps[:, 0:n], "max"), reads=[Tp], writes=[Tst])
                        if HOPT.get('sub', 3) >= 3:
                            S.op("act", I_act(Sb[m][:, k0:k0 + n], ps[:, 0:n], AF.Identity), reads=[Tp], writes=[TS_[m]])
                    nch = len(chunks)
                    if stage < 2:
                        continue
                    S.op("dve", I_red(st[:, 18 + m:19 + m], st[:, m * 9:m * 9 + nch], "max"), reads=[Tst], writes=[Tst])
                    S.op("dve", I_ts(st[:, 20 + m:21 + m], st[:, 18 + m:19 + m], -0.125, None, ALU.mult), reads=[Tst],
                         writes=[Tst])
                    S.op("act", I_act(Eb[m][:, 0:nk], Sb[m][:, 0:nk], AF.Exp, bias=st[:, 20 + m:21 + m], scale=0.125,
                                      accum_out=st[:, 22 + m:23 + m]), reads=[TS_[m], Tst], writes=[TE[m], Tst])
                if stage < 3:
                    continue
                S.op("dve", I_recip(st[:, 24:26], st[:, 22:24]), reads=[Tst], writes=[Tst])
                S.op("dve", I_tt(st[:, 25:26], st[:, 25:26], neglam, ALU.mult), reads=[Tst, Tl], writes=[Tst])
                S.op("dve", I_ts(tmpf[:, 0:nk], Eb[1][:, 0:nk], st[:, 25:26], None, ALU.mult), reads=[TE[1], Tst],
                     writes=[Ttmp])
                S.op("dve", I_stt(Ab[:, 0:nk], Eb[0][:, 0:nk], st[:, 24:25], tmpf[:, 0:nk], ALU.mult, ALU.add),
                     reads=[TE[0], Ttmp, Tst], writes=[TA_])
                if stage < 4:
                    continue
                for g in range(0, nkb, 8):
                    ps, Tp = next_ps(cx)
                    psb = ps[:].bitcast(BF16)
                    k = min(8, nkb - g)
                    for j in range(k):
                        kb = g + j
                        S.op("pe", I_tr(psb[:, j * 128:(j + 1) * 128], Ab[:, kb * 128:(kb + 1) * 128], cx.identb[:]),
                             reads=[TA_, cx.T_const], writes=[Tp])
                    dst = AT[:, g:g + k, :]
                    src = psb[:, 0:k * 128].rearrange("p (a b) -> p a b", a=k)
                    if (g // 8) % 2 == 0:
                        S.op("act", I_act(dst, src, AF.Identity), reads=[Tp], writes=[TAT])
                    else:
                        S.op("dve", I_copy(dst, src), reads=[Tp], writes=[TAT])
                if stage < 5:
                    continue
                pso, Tpo = next_ps(cx)
                for kb in range(nkb):
                    S.op("pe", I_mm(pso[:, 0:128], AT[:, kb, :], vh[:, kb, :], kb == 0, kb == nkb - 1), reads=[TAT, Tv],
                         writes=[Tpo])
                if stage < 6:
                    continue
                S.op("act", I_act(sqj[:], pso[:, 0:128], AF.Square, accum_out=st[:, 26:27]), reads=[Tpo],
                     writes=[Tsqj, Tst])
                S.op("act", I_act(st[:, 27:28], st[:, 26:27], AF.Sqrt, bias=EPS, scale=1.0 / 128), reads=[Tst], writes=[Tst])
                S.op("dve", I_recip(st[:, 27:28], st[:, 27:28]), reads=[Tst], writes=[Tst])
                S.op("dve", I_stt(on[:], pso[:, 0:128], st[:, 27:28], gs[:, h * 128:(h + 1) * 128], ALU.mult, ALU.mult),
                     reads=[Tpo, Tst, Tgs], writes=[Ton])
                pst, Tpt = next_ps(cx)
                pstb = pst[:].bitcast(BF16)
                S.op("pe", I_tr(pstb[:, 0:128], on[:], cx.identb[:]), reads=[Ton, cx.T_const], writes=[Tpt])
                S.op("act", I_act(oT[:, q0:q0 + 128], pstb[:, 0:128], AF.Identity), reads=[Tpt], writes=[ToT])
            S.dma("sp", cx.zcat[1024 + h * 128:1024 + (h + 1) * 128, :], oT[:], reads=[ToT])


def phase_H2(cx, l):
    nc, S = cx.nc, cx.S
    lam_init = 0.8 - 0.6 * math.exp(-0.3 * l)
    Vr = cx.V.rearrange("(kb p) c -> p kb c", p=128)
    NKB = TA // 128
    with ExitStack() as es:
        dl = SB(es, nc, "H_dl", [128, 256], F32)
        lam = SB(es, nc, "H_lam", [128, 8], F32)
        gs = SB(es, nc, "H_gs", [128, 1024], F32)
        Tl, Tgs = Tok(), Tok()
        S.dma("sp", dl[:], cx.I["diff_lambda"][l].to_broadcast([128, 256]), writes=[Tl])
        S.dma("sp", gs[:], cx.I["subln_g"][l].to_broadcast([128, 1024]), writes=[Tgs])
        S.op("dve", I_tt(dl[:, 0:64], dl[:, 0:64], dl[:, 64:128], ALU.mult), reads=[Tl], writes=[Tl])
        S.op("dve", I_tt(dl[:, 128:192], dl[:, 128:192], dl[:, 192:256], ALU.mult), reads=[Tl], writes=[Tl])
        S.op("dve", I_red(lam[:, 0:1], dl[:, 0:64], "sum"), reads=[Tl], writes=[Tl])
        S.op("dve", I_red(lam[:, 1:2], dl[:, 128:192], "sum"), reads=[Tl], writes=[Tl])
        S.op("act", I_act(lam[:, 2:4], lam[:, 0:2], AF.Exp), reads=[Tl], writes=[Tl])
        S.op("dve", I_tt(lam[:, 4:5], lam[:, 3:4], lam[:, 2:3], ALU.subtract), reads=[Tl], writes=[Tl])
        S.op("dve", I_ts(lam[:, 5:6], lam[:, 4:5], -lam_init, None, ALU.add), reads=[Tl], writes=[Tl])
        neglam = lam[:, 5:6]
        S.op("dve", I_ts(gs[:], gs[:], 1.0 - lam_init, None, ALU.mult), reads=[Tgs], writes=[Tgs])

        kT = SB(es, nc, "H_kT", [128, TA], BF16)
        qT = SB(es, nc, "H_qT", [128, TA], BF16)
        vh = SB(es, nc, "H_v", [128, NKB, 128], BF16)
        oT = SB(es, nc, "H_oT", [128, TA], BF16)
        Tk, Tq, Tv, ToT = Tok(), Tok(), Tok(), Tok()
        sqb = SB(es, nc, "H_sqb", [128, TA], F32)
        Tsqb = Tok()
        nb = SB(es, nc, "H_nb", [128, 68], F32)
        Tnb = Tok()
        Eb = [[SB(es, nc, f"H_E{m}_{i}", [128, TA], BF16) for m in range(2)] for i in range(2)]
        TE = [[Tok(), Tok()] for i in range(2)]
        Ab = [SB(es, nc, f"H_A{i}", [128, TA], BF16) for i in range(2)]
        TA_ = [Tok(), Tok()]
        AT = [SB(es, nc, f"H_AT{i}", [128, NKB, 128], BF16) for i in range(2)]
        TAT = [Tok(), Tok()]
        st = [SB(es, nc, f"H_st{i}", [128, 40], F32) for i in range(2)]
        Tst = [Tok(), Tok()]
        Tst2 = [Tok(), Tok()]
        kst = SB(es, nc, "H_kst", [128, 24], F32)
        Tkst = Tok()
        on = [SB(es, nc, f"H_on{i}", [128, 128], BF16) for i in range(2)]
        Ton = [Tok(), Tok()]
        sqj = SB(es, nc, "H_sqj", [128, 128], F32)
        Tsqj = Tok()
        qblocks = list(range(NKB)) if l < DEPTH - 1 else list(range(2, NKB))
        qblocks = HOPT.get('qblocks', qblocks)
        allchunks = split_tiles(0, TA)
        it = 0
        for h in range(HOPT.get('heads', 8)):
            S.dma("sp", kT[:], cx.kT[h * 128:(h + 1) * 128, :], writes=[Tk])
            S.dma("sp", qT[:], cx.qT[h * 128:(h + 1) * 128, :], writes=[Tq])
            dma_mid_split(S, "sp", vh, Vr[:, :, h * 128:(h + 1) * 128], NKB, 12, writes=[Tv])
            S.op("act", I_act(sqb[:], kT[:], AF.Square), reads=[Tk], writes=[Tsqb])
            for m in range(2):
                for ci, (k0, n) in enumerate(allchunks):
                    ps, Tp = next_ps(cx)
                    S.op("pe", I_mm(ps[:, 0:n], cx.ones_f[m * 64:(m + 1) * 64, :], sqb[m * 64:(m + 1) * 64, k0:k0 + n], True, True),
                         reads=[Tsqb, cx.T_const], writes=[Tp])
                    S.op("dve", I_red(kst[:, m * 9 + ci:m * 9 + ci + 1], ps[:, 0:n], "max"), reads=[Tp], writes=[Tkst])
                S.op("dve", I_red(kst[:, 18 + m:19 + m], kst[:, m * 9:m * 9 + 9], "max"), reads=[Tkst], writes=[Tkst])
            S.op("act", I_act(sqb[:], qT[:], AF.Square), reads=[Tq], writes=[Tsqb])
            psq, Tpq = next_ps(cx)
            for m in range(2):
                for qb in range(NKB):
                    S.op("pe", I_mm(psq[:, m * NKB + qb:m * NKB + qb + 1], sqb[m * 64:(m + 1) * 64, qb * 128:(qb + 1) * 128],
                                    cx.ones_f[m * 64:(m + 1) * 64, 0:1], True, True), reads=[Tsqb, cx.T_const], writes=[Tpq])
            for m in range(2):
                S.op("dve", I_ts(nb[:, m * NKB:(m + 1) * NKB], psq[:, m * NKB:(m + 1) * NKB], kst[:, 18 + m:19 + m], None, ALU.mult),
                     reads=[Tpq, Tkst], writes=[Tnb])
            S.op("act", I_act(nb[:], nb[:], AF.Sqrt), reads=[Tnb], writes=[Tnb])
            S.op("dve", I_ts(nb[:], nb[:], -0.125 * 1.002, None, ALU.mult), reads=[Tnb], writes=[Tnb])
            def stage1(qb, b):
                q0 = qb * 128
                nk = CTX if qb < 2 else TA
                chunks = split_tiles(0, nk)
                nch = len(chunks)
                for m in range(2):
                    for ci, (k0, n) in enumerate(chunks):
                        ps, Tp = next_ps(cx)
                        S.op("pe", I_mm(ps[:, 0:n], qT[m * 64:(m + 1) * 64, q0:q0 + 128], kT[m * 64:(m + 1) * 64, k0:k0 + n],
                                        True, True), reads=[Tq, Tk], writes=[Tp])
                        S.op("act", I_act(Eb[b][m][:, k0:k0 + n], ps[:, 0:n], AF.Exp, bias=nb[:, m * NKB + qb:m * NKB + qb + 1],
                                          scale=0.125, accum_out=st[b][:, m * 9 + ci:m * 9 + ci + 1]),
                             reads=[Tp, Tnb], writes=[TE[b][m], Tst[b]])

            def stage2(qb, b):
                q0 = qb * 128
                nk = CTX if qb < 2 else TA
                nkb = nk // 128
                nch = len(split_tiles(0, nk))
                for m in range(2):
                    S.op("dve", I_red(st[b][:, 22 + m:23 + m], st[b][:, m * 9:m * 9 + nch], "sum"), reads=[Tst[b]], writes=[Tst2[b]])
                S.op("dve", I_recip(st[b][:, 24:26], st[b][:, 22:24]), reads=[Tst2[b]], writes=[Tst2[b]])
                S.op("dve", I_tt(st[b][:, 28:29], st[b][:, 22:23], st[b][:, 25:26], ALU.mult), reads=[Tst2[b]], writes=[Tst2[b]])
                S.op("dve", I_tt(st[b][:, 29:30], st[b][:, 28:29], neglam, ALU.mult), reads=[Tst2[b], Tl], writes=[Tst2[b]])
                S.op("dve", I_stt(Ab[b][:, 0:nk], Eb[b][1][:, 0:nk], st[b][:, 29:30], Eb[b][0][:, 0:nk], ALU.mult, ALU.add),
                     reads=[TE[b][0], TE[b][1], Tst2[b]], writes=[TA_[b]])
                for g in range(0, nkb, 8):
                    ps, Tp = next_ps(cx)
                    psb = ps[:].bitcast(BF16)
                    k = min(8, nkb - g)
                    for j in range(k):
                        kb = g + j
                        S.op("pe", I_tr(psb[:, j * 128:(j + 1) * 128], Ab[b][:, kb * 128:(kb + 1) * 128], cx.identb[:]),
                             reads=[TA_[b], cx.T_const], writes=[Tp])
                    dst = AT[b][:, g:g + k, :]
                    src = psb[:, 0:k * 128].rearrange("p (a b) -> p a b", a=k)
                    S.op("dve", I_copy(dst, src), reads=[Tp], writes=[TAT[b]])
                pso, Tpo = next_ps(cx)
                for kb in range(nkb):
                    S.op("pe", I_mm(pso[:, 0:128], AT[b][:, kb, :], vh[:, kb, :], kb == 0, kb == nkb - 1), reads=[TAT[b], Tv],
                         writes=[Tpo])
                S.op("act", I_act(sqj[:], pso[:, 0:128], AF.Square, scale=st[b][:, 24:25], accum_out=st[b][:, 26:27]),
                     reads=[Tpo, Tst2[b]], writes=[Tsqj, Tst2[b]])
                S.op("act", I_act(st[b][:, 27:28], st[b][:, 26:27], AF.Ln, bias=cx.epsc[:, 0:1], scale=1.0 / 128), reads=[Tst2[b], cx.T_const],
                     writes=[Tst2[b]])
                S.op("act", I_act(st[b][:, 27:28], st[b][:, 27:28], AF.Exp, scale=-0.5), reads=[Tst2[b]], writes=[Tst2[b]])
                S.op("dve", I_tt(st[b][:, 30:31], st[b][:, 27:28], st[b][:, 24:25], ALU.mult), reads=[Tst2[b]], writes=[Tst2[b]])
                S.op("dve", I_stt(on[b][:], pso[:, 0:128], st[b][:, 30:31], gs[:, h * 128:(h + 1) * 128], ALU.mult, ALU.mult),
                     reads=[Tpo, Tst2[b], Tgs], writes=[Ton[b]])
                pst, Tpt = next_ps(cx)
                pstb = pst[:].bitcast(BF16)
                S.op("pe", I_tr(pstb[:, 0:128], on[b][:], cx.identb[:]), reads=[Ton[b], cx.T_const], writes=[Tpt])
                S.op("dve", I_copy(oT[:, q0:q0 + 128], pstb[:, 0:128]), reads=[Tpt], writes=[ToT])

            seq = [(qb, (it + i) % 2) for i, qb in enumerate(qblocks)]
            it += len(qblocks)
            for i, (qb, b) in enumerate(seq):
                if i == 0:
                    stage1(qb, b)
                if i + 1 < len(seq):
                    stage1(*seq[i + 1])
                stage2(qb, b)
            S.dma("sp", cx.zcat[1024 + h * 128:1024 + (h + 1) * 128, :], oT[:], reads=[ToT])


def wrap_sin(cx, buf, n, Tb, tmp, Ttmp):
    S = cx.S
    PI = math.pi
    S.op("dve", I_ts(tmp[0:64, 0:n], buf, -PI, 2 * PI, ALU.is_lt, ALU.mult), reads=[Tb], writes=[Ttmp])
    S.op("dve", I_tt(buf, buf, tmp[0:64, 0:n], ALU.add), reads=[Ttmp], writes=[Tb])
    S.op("dve", I_ts(tmp[0:64, 0:n], buf, PI, -2 * PI, ALU.is_gt, ALU.mult), reads=[Tb], writes=[Ttmp])
    S.op("dve", I_tt(buf, buf, tmp[0:64, 0:n], ALU.add), reads=[Ttmp], writes=[Tb])
    S.op("act", I_act(buf, buf, AF.Sin), reads=[Tb], writes=[Tb])


def phase_F(cx, l):
    nc, S = cx.nc, cx.S
    I = cx.I
    seqs = [dict(t0=CTX, L=SEQ, LB=32, NFB=33, f0=0, TC=I["TC"], TS=I["TS"], TCf=I["TCf"], TSf=I["TSf"], wf0=0),
            dict(t0=0, L=CTX, LB=2, NFB=3, f0=NF, TC=I["TCc"], TS=I["TSc"], TCf=I["TCcf"], TSf=I["TScf"], wf0=33)]
    if l == DEPTH - 1:
        seqs = seqs[:1]
    with ExitStack() as es:
        sw, Tsw = load_T(cx, es, I["short_b_w"][l], 3, 1536, "F_sw")
        wfs = SB(es, nc, "F_wf", [128, 72], F32)
        Twf = Tok()
        S.dma("sp", wfs[:], I["hy_wf"][:, :], writes=[Twf])
        es0 = ExitStack()
        u = [SB(es0, nc, f"F_u{i}", [128, TA], F32) for i in range(2)]
        o = [SB(es0, nc, f"F_o{i}", [128, TA], F32) for i in range(2)]
        Tu, To = [Tok(), Tok()], [Tok(), Tok()]
        for jc in range(12):
            b = jc % 2
            S.dma("sp", u[b][:], cx.bT[jc * 128:(jc + 1) * 128, :], writes=[Tu[b]])
            for sq in seqs:
                a, e_ = sq["t0"], sq["t0"] + sq["L"]
                S.op("dve", I_ts(o[b][:, a:e_], u[b][:, a:e_], sw[:, jc * 3 + 1:jc * 3 + 2], vec_ap(cx, "short_b_b", jc),
                                 ALU.mult, ALU.add), reads=[Tu[b], Tsw, cx.T_vec], writes=[To[b]])
                S.op("dve", I_stt(o[b][:, a + 1:e_], u[b][:, a:e_ - 1], sw[:, jc * 3:jc * 3 + 1], o[b][:, a + 1:e_],
                                  ALU.mult, ALU.add), reads=[Tu[b], Tsw], writes=[To[b]])
                S.op("dve", I_stt(o[b][:, a:e_ - 1], u[b][:, a + 1:e_], sw[:, jc * 3 + 2:jc * 3 + 3], o[b][:, a:e_ - 1],
                                   ALU.mult, ALU.add), reads=[Tu[b], Tsw], writes=[To[b]])
                S.dma("act", cx.ubuf[jc * 128:(jc + 1) * 128, a:e_], o[b][:, a:e_], reads=[To[b]])
        S.barrier()
        es0.close()
        for sq in seqs:
            t0, L, LB, NFB, f0 = sq["t0"], sq["L"], sq["LB"], sq["NFB"], sq["f0"]
            S.barrier()
            with ExitStack() as es1:
                w1 = SB(es1, nc, "F_w1", [17, 64], F32)
                w2 = SB(es1, nc, "F_w2", [64, 64], F32)
                w3 = SB(es1, nc, "F_w3", [64, 2048], F32)
                feats = SB(es1, nc, "F_feats", [17, SEQ], F32)
                h1 = SB(es1, nc, "F_h1", [64, SEQ], F32)
                h2 = SB(es1, nc, "F_h2", [64, SEQ], F32)
                tmpw = SB(es1, nc, "F_tmpw", [64, 512], F32)
                Tw_, Tf, Th1, Th2, Ttw = Tok(), Tok(), Tok(), Tok(), Tok()
                S.dma("sp", w1[:], I["filt_w1"][l], writes=[Tw_])
                S.dma("sp", w2[:], I["filt_w2"][l], writes=[Tw_])
                S.dma("sp", w3[:], I["filt_w3"][l], writes=[Tw_])
                S.dma("sp", feats[:, 0:L], I["hy_feats"][:, t0:t0 + L], writes=[Tf])
                b1 = vec_ap(cx, "filt_b1", 0)[0:64, :]
                b2 = vec_ap(cx, "filt_b2", 0)[0:64, :]
                for c0 in range(0, L, 512):
                    n = min(512, L - c0)
                    ps, Tp = next_ps(cx)
                    S.op("pe", I_mm(ps[0:64, 0:n], w1[:, :], feats[:, c0:c0 + n], True, True), reads=[Tw_, Tf], writes=[Tp])
                    S.op("dve", I_ts(h1[:, c0:c0 + n], ps[0:64, 0:n], b1, None, ALU.add), reads=[Tp, cx.T_vec], writes=[Th1])
                    wrap_sin(cx, h1[:, c0:c0 + n], n, Th1, tmpw, Ttw)
                for c0 in range(0, L, 512):
                    n = min(512, L - c0)
                    ps, Tp = next_ps(cx)
                    S.op("pe", I_mm(ps[0:64, 0:n], w2[:, :], h1[:, c0:c0 + n], True, True), reads=[Tw_, Th1], writes=[Tp])
                    S.op("dve", I_ts(h2[:, c0:c0 + n], ps[0:64, 0:n], b2, None, ALU.add), reads=[Tp, cx.T_vec], writes=[Th2])
                    wrap_sin(cx, h2[:, c0:c0 + n], n, Th2, tmpw, Ttw)
                dec = [SB(es1, nc, f"F_dec{i}", [128, 1024], F32) for i in range(2)]
                Tdec = [Tok(), Tok()]
                hf = [SB(es1, nc, f"F_hf{i}", [128, 512], F32) for i in range(2)]
                hb = [SB(es1, nc, f"F_hb{i}", [128, 512], F32) for i in range(2)]
                ab = [SB(es1, nc, f"F_ab{i}", [128, 512], F32) for i in range(2)]
                ab2 = [SB(es1, nc, f"F_ab2{i}", [128, 512], F32) for i in range(2)]
                hpm_t = [SB(es1, nc, f"F_hpm{i}", [128, 1024], BF16) for i in range(2)]
                Thf, Thb, Tab, Tab2, Thpm = [Tok(), Tok()], [Tok(), Tok()], [Tok(), Tok()], [Tok(), Tok()], [Tok(), Tok()]
                rn = SB(es1, nc, "F_rn", [128, 1024], F32)
                Trn = Tok()
                psn = []
                for _ in range(2):
                    p_, T_ = next_ps(cx)
                    psn.append((p_, T_))
                for p_, _ in psn:
                    cx.ps_reserved.add(cx.ps.index(p_))
                it = 0
                for tb in range(LB):
                    db = tb % 2
                    r0 = t0 + tb * 128
                    S.dma("sp", dec[db][:, 0:512], I["hy_decay"][r0:r0 + 128, :], writes=[Tdec[db]], acc=True)
                    S.dma("sp", dec[db][:, 512:1024], I["hy_decayb"][r0:r0 + 128, :], writes=[Tdec[db]], acc=True)
                    for o_ in range(2):
                        k = it % 2
                        it += 1
                        psf, Tpf = next_ps(cx)
                        psb, Tpb = next_ps(cx)
                        S.op("pe", I_mm(psf[:], h2[:, tb * 128:(tb + 1) * 128], w3[:, (o_ * 2) * 512:(o_ * 2 + 1) * 512], True, True),
                             reads=[Th2, Tw_], writes=[Tpf])
                        S.op("pe", I_mm(psb[:], h2[:, tb * 128:(tb + 1) * 128], w3[:, (o_ * 2 + 1) * 512:(o_ * 2 + 2) * 512], True, True),
                             reads=[Th2, Tw_], writes=[Tpb])
                        S.op("dve", I_tt(hf[k][:], psf[:], dec[db][:, 0:512], ALU.mult), reads=[Tpf, Tdec[db]], writes=[Thf[k]])
                        S.op("dve", I_tt(hb[k][:], psb[:], dec[db][:, 512:1024], ALU.mult), reads=[Tpb, Tdec[db]], writes=[Thb[k]])
                        S.op("act", I_act(ab[k][:], hf[k][:], AF.Abs), reads=[Thf[k]], writes=[Tab[k]])
                        S.op("act", I_act(ab2[k][:], hb[k][:], AF.Abs), reads=[Thb[k]], writes=[Tab2[k]])
                        S.op("pool", I_tt(ab[k][:], ab[k][:], ab2[k][:], ALU.add), reads=[Tab2[k]], writes=[Tab[k]])
                        S.op("pe", I_mm(psn[o_][0][:], cx.ones_f[:], ab[k][:], tb == 0, tb == LB - 1), reads=[Tab[k], cx.T_const],
                             writes=[psn[o_][1]])
                        S.op("dve", I_tt(hpm_t[k][:, 0:512], hf[k][:], hb[k][:], ALU.add), reads=[Thf[k], Thb[k]], writes=[Thpm[k]])
                        S.op("pool", I_tt(hpm_t[k][:, 512:1024], hf[k][:], hb[k][:], ALU.subtract), reads=[Thf[k], Thb[k]],
                             writes=[Thpm[k]])
                        S.dma("act", cx.hpm[r0:r0 + 128, o_ * 512:(o_ + 1) * 512], hpm_t[k][:, 0:512], reads=[Thpm[k]])
                        S.dma("act", cx.hpm[r0:r0 + 128, 1024 + o_ * 512:1024 + (o_ + 1) * 512], hpm_t[k][:, 512:1024],
                              reads=[Thpm[k]])
                for o_ in range(2):
                    S.op("dve", I_recip(rn[:, o_ * 512:(o_ + 1) * 512], psn[o_][0][:]), reads=[psn[o_][1]], writes=[Trn])
                cx.ps_reserved.clear()
                S.barrier()
                X = SB(es1, nc, "F_X", [128, 32, 512], BF16)
                TX = Tok()
                tab = [SB(es1, nc, f"F_tab{i}", [128, 32, 128], BF16) for i in range(2)]
                Ttab = [Tok(), Tok()]
                so = Stage(nc, es1, "F_so", F32, 3)
                it = 0
                for pm in range(2):
                    tableF = sq["TCf"] if pm == 0 else sq["TSf"]
                    for o_ in range(2):
                        cbase = pm * 1024 + o_ * 512
                        dma_mid_split(S, "sp", X, cx.hpm[t0:t0 + L, cbase:cbase + 512].rearrange("(tb p) c -> p tb c", p=128),
                                      LB, 8, writes=[TX])
                        for fb in range(NFB):
                            k = it % 2
                            it += 1
                            S.dma("sp", tab[k][:, 0:LB, :], tableF[fb][:, 0:LB, :], writes=[Ttab[k]])
                            ps, Tp = next_ps(cx)
                            for db in range(LB):
                                S.op("pe", I_mm(ps[:], tab[k][:, db, :], X[:, db, :], db == 0, db == LB - 1), reads=[Ttab[k], TX],
                                     writes=[Tp])
                            wcol = sq["wf0"] + fb + (36 if pm == 1 else 0)
                            ot, Tot = so.next()
                            S.op("dve", I_stt(ot[:], ps[:], wfs[:, wcol:wcol + 1], rn[:, o_ * 512:(o_ + 1) * 512], ALU.mult, ALU.mult),
                                 reads=[Tp, Twf, Trn], writes=[Tot])
                            S.dma("act", cx.Hd[f0 + fb * 128:f0 + (fb + 1) * 128, cbase:cbase + 512], ot[:], reads=[Tot])
        for sq in seqs:
            t0, L, LB, NFB, f0 = sq["t0"], sq["L"], sq["LB"], sq["NFB"], sq["f0"]
            S.barrier()
            with ExitStack() as es2:
                zt = SB(es2, nc, "F_zt", [128, 32, 512], BF16)
                Y = SB(es2, nc, "F_Y", [128, 33, 1024], BF16)
                Tzt, TY = Tok(), Tok()
                ld = Stage(nc, es2, "F_ld", F32, 4)
                for cc in range(4):
                    for c0 in range(0, L, 512):
                        n = min(512, L - c0)
                        vt, Tvt = ld.next()
                        S.dma("sp", vt[:, 0:n], cx.ubuf[cc * 128:(cc + 1) * 128, t0 + c0:t0 + c0 + n], writes=[Tvt])
                        ps, Tp = next_ps(cx)
                        nb = n // 128
                        for j in range(nb):
                            S.op("pe", I_tr(ps[:, j * 128:(j + 1) * 128], vt[:, j * 128:(j + 1) * 128], cx.ident[:]),
                                 reads=[Tvt, cx.T_const], writes=[Tp])
                        tb0 = c0 // 128
                        S.op("act", I_act(zt[:, tb0:tb0 + nb, cc * 128:(cc + 1) * 128],
                                          ps[:, 0:nb * 128].rearrange("p (a b) -> p a b", a=nb), AF.Identity), reads=[Tp], writes=[Tzt])
                Tz1 = {}
                for o_ in range(2):
                    with ExitStack() as es3:
                        tc_ = [SB(es3, nc, f"F_tc{i}", [128, 32, 128], BF16) for i in range(2)]
                        ts_ = [SB(es3, nc, f"F_ts{i}", [128, 32, 128], BF16) for i in range(2)]
                        Hh = [SB(es3, nc, f"F_Hh{i}", [128, 1024], F32) for i in range(2)]
                        Ttc, Tts, THh = [Tok(), Tok()], [Tok(), Tok()], [Tok(), Tok()]
                        pa = Stage(nc, es3, "F_pa", F32, 4)
                        for fb in range(NFB):
                            k = fb % 2
                            S.dma("sp", tc_[k][:, 0:LB, :], sq["TCf"][fb][:, 0:LB, :], writes=[Ttc[k]])
                            S.dma("sp", ts_[k][:, 0:LB, :], sq["TSf"][fb][:, 0:LB, :], writes=[Tts[k]])
                            S.dma("sp", Hh[k][:, 0:512], cx.Hd[f0 + fb * 128:f0 + (fb + 1) * 128, o_ * 512:(o_ + 1) * 512],
                                  writes=[THh[k]], acc=True)
                            S.dma("sp", Hh[k][:, 512:1024],
                                  cx.Hd[f0 + fb * 128:f0 + (fb + 1) * 128, 1024 + o_ * 512:1024 + (o_ + 1) * 512], writes=[THh[k]],
                                  acc=True)
                            pr, Tpr = next_ps(cx)
                            pi, Tpi = next_ps(cx)
                            for db in range(LB):
                                S.op("pe", I_mm(pr[:], tc_[k][:, db, :], zt[:, db, :], db == 0, db == LB - 1), reads=[Ttc[k], Tzt],
                                     writes=[Tpr])
                            for db in range(LB):
                                S.op("pe", I_mm(pi[:], ts_[k][:, db, :], zt[:, db, :], db == 0, db == LB - 1), reads=[Tts[k], Tzt],
                                     writes=[Tpi])
                            a, Ta = pa.next()
                            b, Tb = pa.next()
                            S.op("dve", I_tt(a[:], pr[:], Hh[k][:, 0:512], ALU.mult), reads=[Tpr, THh[k]], writes=[Ta])
                            S.op("dve", I_tt(b[:], pi[:], Hh[k][:, 512:1024], ALU.mult), reads=[Tpi, THh[k]], writes=[Tb])
                            S.op("pool", I_tt(Y[:, fb, 0:512], a[:], b[:], ALU.add), reads=[Ta, Tb], writes=[TY])
                            c, Tc = pa.next()
                            d, Td = pa.next()
                            S.op("dve", I_tt(c[:], pi[:], Hh[k][:, 0:512], ALU.mult), reads=[Tpi, THh[k]], writes=[Tc])
                            S.op("dve", I_tt(d[:], pr[:], Hh[k][:, 512:1024], ALU.mult), reads=[Tpr, THh[k]], writes=[Td])
                            S.op("pool", I_tt(Y[:, fb, 512:1024], c[:], d[:], ALU.subtract), reads=[Tc, Td], writes=[TY])
                    S.barrier()
                    with ExitStack() as es3:
                        tci = [SB(es3, nc, f"F_tci{i}", [128, 33, 256], BF16) for i in range(2)]
                        tsi = [SB(es3, nc, f"F_tsi{i}", [128, 33, 256], BF16) for i in range(2)]
                        Ttci, Ttsi = [Tok(), Tok()], [Tok(), Tok()]
                        ld2 = Stage(nc, es3, "F_ld2", F32, 6)
                        sob = Stage(nc, es3, "F_sob", BF16, 3)
                        for ti, tt0 in enumerate(range(0, L, 256)):
                            tn = 256
                            k = ti % 2
                            dma_mid_split(S, "sp", tci[k], sq["TC"][0:NFB * 128, tt0:tt0 + tn].rearrange("(fb p) t -> p fb t", p=128),
                                          NFB, 8, writes=[Ttci[k]])
                            dma_mid_split(S, "sp", tsi[k], sq["TS"][0:NFB * 128, tt0:tt0 + tn].rearrange("(fb p) t -> p fb t", p=128),
                                          NFB, 8, writes=[Ttsi[k]])
                            for cc in range(4):
                                ps, Tp = next_ps(cx)
                                for fb in range(NFB):
                                    S.op("pe", I_mm(ps[:, 0:tn], Y[:, fb, cc * 128:(cc + 1) * 128], tci[k][:, fb, :], fb == 0, False),
                                         reads=[TY, Ttci[k]], writes=[Tp])
                                    S.op("pe", I_mm(ps[:, 0:tn], Y[:, fb, 512 + cc * 128:512 + (cc + 1) * 128], tsi[k][:, fb, :], False,
                                                    fb == NFB - 1), reads=[TY, Ttsi[k]], writes=[Tp])
                                zp, Tzp = ld2.next()
                                gt, Tgt = ld2.next()
                                g0 = t0 + tt0
                                if o_ == 0:
                                    S.dma("sp", zp[:, 0:tn], cx.ubuf[cc * 128:(cc + 1) * 128, g0:g0 + tn], writes=[Tzp])
                                else:
                                    S.dma("sp", zp[:, 0:tn], cx.z1[cc * 128:(cc + 1) * 128, g0:g0 + tn], reads=[Tz1[(cc, tt0)]],
                                          writes=[Tzp])
                                gr = (1 + o_) * 512 + cc * 128
                                S.dma("sp", gt[:, 0:tn], cx.ubuf[gr:gr + 128, g0:g0 + tn], writes=[Tgt])
                                S.op("dve", I_stt(zp[:, 0:tn], zp[:, 0:tn], vec_ap(cx, "hyena_skip", o_ * 4 + cc), ps[:, 0:tn],
                                                  ALU.mult, ALU.add), reads=[Tp, cx.T_vec], writes=[Tzp])
                                if o_ == 0:
                                    S.op("dve", I_tt(zp[:, 0:tn], zp[:, 0:tn], gt[:, 0:tn], ALU.mult), reads=[Tgt], writes=[Tzp])
                                    Tz1[(cc, tt0)] = Tok()
                                    S.dma("act", cx.z1[cc * 128:(cc + 1) * 128, g0:g0 + tn], zp[:, 0:tn], reads=[Tzp],
                                          writes=[Tz1[(cc, tt0)]])
                                    pt, Tpt = next_ps(cx)
                                    for j in range(2):
                                        S.op("pe", I_tr(pt[:, j * 128:(j + 1) * 128], zp[:, j * 128:(j + 1) * 128], cx.ident[:]),
                                             reads=[Tzp, cx.T_const], writes=[Tpt])
                                    tb0 = tt0 // 128
                                    S.op("act", I_act(zt[:, tb0:tb0 + 2, cc * 128:(cc + 1) * 128],
                                                      pt[:, 0:256].rearrange("p (a b) -> p a b", a=2), AF.Identity), reads=[Tpt],
                                         writes=[Tzt])
                                else:
                                    ob, Tob = sob.next()
                                    S.op("dve", I_tt(ob[:, 0:tn], zp[:, 0:tn], gt[:, 0:tn], ALU.mult), reads=[Tgt, Tzp], writes=[Tob])
                                    S.dma("act", cx.zcat[512 + cc * 128:512 + (cc + 1) * 128, g0:g0 + tn], ob[:, 0:tn], reads=[Tob])
                    S.barrier()


def run_jobs2(cx, jobs, act, Tact, tiles, KCmax, NB=6, tag="J"):
    nc, S = cx.nc, cx.S
    with ExitStack() as es:
        wb = [SB(es, nc, f"{tag}_w{i}", [128, KCmax, 128], BF16) for i in range(NB)]
        Tw = [Tok() for _ in range(NB)]
        flat = []
        for ji, jb in enumerate(jobs):
            for ci in range(len(jb["cols"])):
                flat.append((ji, ci))
        slot = {}

        def issue(fi):
            ji, ci = flat[fi]
            s = fi % NB
            slot[(ji, ci)] = s
            c = jobs[ji]["cols"][ci]
            load_w_chunk(cx, wb[s], Tw[s], c["segs"], c["KC"] * 128)

        nxt = 0
        fpos = 0
        for ji, jb in enumerate(jobs):
            ncol = len(jb["cols"])
            while nxt < len(flat) and nxt < fpos + NB:
                issue(nxt)
                nxt += 1
            for ti, (c0, n, t0) in enumerate(tiles):
                pss = []
                for ci in range(ncol):
                    s = slot[(ji, ci)]
                    c = jb["cols"][ci]
                    ps, Tp = next_ps(cx)
                    KC = c["KC"]
                    for kc in range(KC):
                        S.op("pe", I_mm(ps[:, 0:n], wb[s][:, kc, :], act[:, c["kc0"] + kc, c0:c0 + n], kc == 0, kc == KC - 1),
                             reads=[Tw[s], Tact], writes=[Tp])
                    pss.append((ps, Tp))
                jb["epi"](jb, ti, t0, n, pss)
            fpos += ncol
        S.barrier()


def phase_G(cx, l):
    nc, S = cx.nc, cx.S
    zr = cx.zcat.rearrange("(c p) t -> p c t", p=128)
    halves = [(0, 2304), (2304, 2048)] if l < DEPTH - 1 else [(CTX, 2048), (CTX + 2048, 2048)]
    Wa, Wb, Wc, Wo = cx.I["w_a_out"][l], cx.I["w_b_out"][l], cx.I["w_c_out"][l], cx.I["w_o"][l]
    for (s0, sn) in halves:
        S.barrier()
        with ExitStack() as es:
            act = SB(es, nc, "G_act", [128, 16, 2304], BF16)
            mT = SB(es, nc, "G_m", [128, 16, 2304], BF16)
            Tact, TmT = Tok(), Tok()
            for c in range(16):
                S.dma("sp", act[:, c, 0:sn], zr[:, c, s0:s0 + sn], writes=[Tact], acc=True)
            tiles = [(t0 - s0, n, t0) for (t0, n) in split_tiles(s0, sn)]
            gt = [SB(es, nc, f"G_g{i}", [128, 3, 512], BF16) for i in range(3)]
            Tg = [Tok() for _ in range(3)]
            gi = [0]
            tf = Stage(nc, es, "G_tf", F32, 4)
            jobs = []

            def epi_merge(jb, ti, t0, n, pss):
                fo = jb["j"]
                i = gi[0]
                gi[0] = (i + 1) % 3
                for br in range(3):
                    r0 = br * 2048 + fo * 128
                    S.dma("sp", gt[i][:, br, 0:n], cx.gates[r0:r0 + 128, t0:t0 + n], writes=[Tg[i]], acc=True)
                a, Ta = tf.next()
                b, Tb = tf.next()
                S.op("dve", I_tt(a[:, 0:n], pss[0][0][:, 0:n], gt[i][:, 0, 0:n], ALU.mult), reads=[pss[0][1], Tg[i]], writes=[Ta])
                S.op("dve", I_tt(b[:, 0:n], pss[1][0][:, 0:n], gt[i][:, 1, 0:n], ALU.mult), reads=[pss[1][1], Tg[i]], writes=[Tb])
                S.op("pool", I_tt(a[:, 0:n], a[:, 0:n], b[:, 0:n], ALU.add), reads=[Tb], writes=[Ta])
                S.op("dve", I_tt(b[:, 0:n], pss[2][0][:, 0:n], gt[i][:, 2, 0:n], ALU.mult), reads=[pss[2][1], Tg[i]], writes=[Tb])
                c0 = t0 - s0
                S.op("dve", I_tt(mT[:, fo, c0:c0 + n], a[:, 0:n], b[:, 0:n], ALU.add), reads=[Ta, Tb], writes=[TmT])

            for fo in range(16):
                cols = [dict(segs=[(0, Wa[:, fo * 128:(fo + 1) * 128])], kc0=0, KC=4),
                        dict(segs=[(0, Wb[:, fo * 128:(fo + 1) * 128])], kc0=4, KC=4),
                        dict(segs=[(0, Wc[:, fo * 128:(fo + 1) * 128])], kc0=8, KC=8)]
                jobs.append(dict(cols=cols, epi=epi_merge, j=fo))
            run_jobs2(cx, jobs, act, Tact, tiles, 8, NB=6, tag="G1")

            xt = [SB(es, nc, f"G_x{i}", [128, 512], F32) for i in range(3)]
            Tx = [Tok() for _ in range(3)]
            xi = [0]

            def epi_res(jb, ti, t0, n, pss):
                fo = jb["j"]
                grp = 1 if t0 < CTX else 0
                i = xi[0]
                xi[0] = (i + 1) % 3
                S.dma("sp", xt[i][:, 0:n], cx.xT[fo * 128:(fo + 1) * 128, t0:t0 + n], writes=[Tx[i]])
                S.op("dve", I_stt(xt[i][:, 0:n], pss[0][0][:, 0:n], mod_ap(cx, 2, fo, grp), xt[i][:, 0:n], ALU.mult, ALU.add),
                     reads=[pss[0][1], cx.T_mod], writes=[Tx[i]])
                S.dma("act", cx.xT[fo * 128:(fo + 1) * 128, t0:t0 + n], xt[i][:, 0:n], reads=[Tx[i]])

            jobs = [dict(cols=[dict(segs=[(0, Wo[:, fo * 128:(fo + 1) * 128])], kc0=0, KC=16)], epi=epi_res, j=fo)
                    for fo in range(16)]
            run_jobs2(cx, jobs, mT, TmT, tiles, 16, NB=4, tag="G2")


def phase_J(cx, l):
    nc, S = cx.nc, cx.S
    hT_r = cx.hT.rearrange("(c p) t -> p c t", p=128)
    Wu, Wd = cx.I["w_up"][l], cx.I["w_down"][l]
    with_ctx = l < DEPTH - 1
    halves = [(0, 2304), (2304, 2048)] if with_ctx else [(CTX, 2048), (CTX + 2048, 2048)]
    with ExitStack() as es0:
        fw, Tfw = load_T(cx, es0, cx.I["conv_f_w"][l], 3, 2 * DFF, "J_fw")
        for (s0, sn) in halves:
            S.barrier()
            with ExitStack() as es:
                lo = s0 - 1 if s0 > CTX else s0
                hi = s0 + sn + 1 if s0 + sn < TA else s0 + sn
                W = hi - lo
                act = SB(es, nc, "J_act", [128, 16, 2306], BF16)
                Tact = Tok()
                for c in range(16):
                    S.dma("sp", act[:, c, 0:W], hT_r[:, c, lo:hi], writes=[Tact], acc=True)
                tl = []
                if lo < s0:
                    tl.append((lo, 1))
                tl += split_tiles(s0, sn)
                if hi > s0 + sn:
                    tl.append((s0 + sn, 1))
                tiles = [(t0 - lo, n, t0) for (t0, n) in tl]
                ua = [SB(es, nc, f"J_ua{i}", [128, 2306], F32) for i in range(2)]
                uv = [SB(es, nc, f"J_uv{i}", [128, 2306], F32) for i in range(2)]
                ca = SB(es, nc, "J_ca", [128, 2306], F32)
                cv = SB(es, nc, "J_cv", [128, 2306], F32)
                fo_ = [SB(es, nc, f"J_f{i}", [128, 2306], BF16) for i in range(2)]
                Tua, Tuv = [Tok(), Tok()], [Tok(), Tok()]
                Tca, Tcv = Tok(), Tok()
                Tfo = [Tok(), Tok()]
                segs = []
                if s0 < CTX:
                    segs.append((0, CTX, False, False))
                    segs.append((CTX, s0 + sn, False, hi > s0 + sn))
                else:
                    segs.append((s0, s0 + sn, lo < s0, hi > s0 + sn))

                def conv3(eng, dst, Tdst, src, Tsrc, ch):
                    for (a, b, hl, hr) in segs:
                        ca0, cb0 = a - lo, b - lo
                        w0 = fw[:, ch * 3:ch * 3 + 1]
                        w1 = fw[:, ch * 3 + 1:ch * 3 + 2]
                        w2 = fw[:, ch * 3 + 2:ch * 3 + 3]
                        S.op(eng, I_ts(dst[:, ca0:cb0], src[:, ca0:cb0], w1, vec_ap(cx, "conv_f_b", ch), ALU.mult, ALU.add),
                             reads=[Tsrc, Tfw, cx.T_vec], writes=[Tdst])
                        ol = ca0 if hl else ca0 + 1
                        S.op(eng, I_stt(dst[:, ol:cb0], src[:, ol - 1:cb0 - 1], w0, dst[:, ol:cb0], ALU.mult, ALU.add),
                             reads=[Tsrc, Tfw], writes=[Tdst])
                        oh = cb0 if hr else cb0 - 1
                        S.op(eng, I_stt(dst[:, ca0:oh], src[:, ca0 + 1:oh + 1], w2, dst[:, ca0:oh], ALU.mult, ALU.add),
                             reads=[Tsrc, Tfw], writes=[Tdst])

                jobs = []
                ntile = len(tiles)

                def epi_up(jb, ti, t0, n, pss):
                    j = jb["j"]
                    b = j % 2
                    c0 = t0 - lo
                    S.op("act", I_act(ua[b][:, c0:c0 + n], pss[0][0][:, 0:n], AF.Identity), reads=[pss[0][1]], writes=[Tua[b]])
                    S.op("act", I_act(uv[b][:, c0:c0 + n], pss[1][0][:, 0:n], AF.Identity), reads=[pss[1][1]], writes=[Tuv[b]])
                    if ti == ntile - 1:
                        conv3("dve", ca, Tca, ua[b], Tua[b], j)
                        conv3("dve", cv, Tcv, uv[b], Tuv[b], 44 + j)
                        a0, b0 = s0 - lo, s0 - lo + sn
                        S.op("act", I_act(ca[:, a0:b0], ca[:, a0:b0], AF.Silu), reads=[Tca], writes=[Tca])
                        S.op("dve", I_tt(fo_[b][:, a0:b0], ca[:, a0:b0], cv[:, a0:b0], ALU.mult), reads=[Tca, Tcv],
                             writes=[Tfo[b]])
                        S.dma("act", cx.fT[j * 128:(j + 1) * 128, s0:s0 + sn], fo_[b][:, a0:b0], reads=[Tfo[b]])

                for j in range(44):
                    cols = [dict(segs=[(0, Wu[:, j * 128:(j + 1) * 128])], kc0=0, KC=16),
                            dict(segs=[(0, Wu[:, (44 + j) * 128:(45 + j) * 128])], kc0=0, KC=16)]
                    jobs.append(dict(cols=cols, epi=epi_up, j=j))
                run_jobs2(cx, jobs, act, Tact, tiles, 16, NB=6, tag="J1")
    fr = cx.fT.rearrange("(c p) t -> p c t", p=128)
    quarters = [(i * 1088, 1088) for i in range(4)] if with_ctx else [(CTX + i * 1024, 1024) for i in range(4)]
    for (s0, sn) in quarters:
        S.barrier()
        with ExitStack() as es:
            act = SB(es, nc, "J2_act", [128, 44, 1088], BF16)
            Tact = Tok()
            for c in range(44):
                S.dma("sp", act[:, c, 0:sn], fr[:, c, s0:s0 + sn], writes=[Tact], acc=True)
            tiles = [(t0 - s0, n, t0) for (t0, n) in split_tiles(s0, sn)]
            xt = [SB(es, nc, f"J2_x{i}", [128, 512], F32) for i in range(3)]
            Tx = [Tok() for _ in range(3)]
            xi = [0]

            def epi_res(jb, ti, t0, n, pss):
                fo = jb["j"]
                grp = 1 if t0 < CTX else 0
                i = xi[0]
                xi[0] = (i + 1) % 3
                S.dma("sp", xt[i][:, 0:n], cx.xT[fo * 128:(fo + 1) * 128, t0:t0 + n], writes=[Tx[i]])
                S.op("dve", I_stt(xt[i][:, 0:n], pss[0][0][:, 0:n], mod_ap(cx, 5, fo, grp), xt[i][:, 0:n], ALU.mult, ALU.add),
                     reads=[pss[0][1], cx.T_mod], writes=[Tx[i]])
                S.dma("act", cx.xT[fo * 128:(fo + 1) * 128, t0:t0 + n], xt[i][:, 0:n], reads=[Tx[i]])

            jobs = [dict(cols=[dict(segs=[(0, Wd[:, fo * 128:(fo + 1) * 128])], kc0=0, KC=44)], epi=epi_res, j=fo)
                    for fo in range(16)]
            run_jobs2(cx, jobs, act, Tact, tiles, 44, NB=3, tag="J2")


def phase_final(cx):
    nc, S = cx.nc, cx.S
    xT_r = cx.xT.rearrange("(c p) t -> p c t", p=128)
    with ExitStack() as es:
        gb = SB(es, nc, "F_gb", [128, D], F32)
        Tgb = Tok()
        S.dma("sp", gb[:], cx.I["final_g"].to_broadcast([128, D]), writes=[Tgb])
        xb = [SB(es, nc, f"F_x{i}", [128, 16, 128], F32) for i in range(2)]
        ob = [SB(es, nc, f"F_o{i}", [128, D], F32) for i in range(2)]
        Tx, To = [Tok(), Tok()], [Tok(), Tok()]
        st = SB(es, nc, "F_st", [128, 8], F32)
        Tst = Tok()
        junk = SB(es, nc, "F_junk", [128, 512], F32)
        Tj = Tok()
        for tb in range(SEQ // 128):
            b = tb % 2
            t0 = CTX + tb * 128
            S.dma("sp", xb[b][:], xT_r[:, :, t0:t0 + 128], writes=[Tx[b]])
            pss = []
            for g4 in range(4):
                ps, Tp = next_ps(cx)
                for j in range(4):
                    c = g4 * 4 + j
                    S.op("pe", I_tr(ps[:, j * 128:(j + 1) * 128], xb[b][:, c, :], cx.ident[:]), reads=[Tx[b], cx.T_const],
                         writes=[Tp])
                S.op("act", I_act(junk[:], ps[:], AF.Square, accum_out=st[:, g4:g4 + 1]), reads=[Tp], writes=[Tj, Tst])
                pss.append((ps, Tp))
            S.op("dve", I_red(st[:, 4:5], st[:, 0:4], "sum"), reads=[Tst], writes=[Tst])
            S.op("act", I_act(st[:, 5:6], st[:, 4:5], AF.Sqrt, bias=EPS, scale=1.0 / D), reads=[Tst], writes=[Tst])
            S.op("dve", I_recip(st[:, 5:6], st[:, 5:6]), reads=[Tst], writes=[Tst])
            for g4 in range(4):
                ps, Tp = pss[g4]
                S.op("dve", I_stt(ob[b][:, g4 * 512:(g4 + 1) * 512], ps[:], st[:, 5:6], gb[:, g4 * 512:(g4 + 1) * 512],
                                  ALU.mult, ALU.mult), reads=[Tp, Tst, Tgb], writes=[To[b]])
            S.dma("pool", cx.out[tb * 128:(tb + 1) * 128, :], ob[b][:], reads=[To[b]])


def host_consts():
    c = {}
    c["ident"] = np.eye(128, dtype=np.float32)
    rows = SEQ // 64
    row = np.repeat(np.arange(rows), 64).astype(np.float32)
    col = np.tile(np.arange(64), rows).astype(np.float32)
    nf = 16
    inv = (10000.0 ** (-np.arange(nf, dtype=np.float32) / nf)).astype(np.float32)
    ang = np.concatenate([row[:, None] * inv, col[:, None] * inv], -1)
    cos = np.cos(ang).astype(np.float32).T
    sin = np.sin(ang).astype(np.float32).T
    rc = np.ones((128, TA), np.float32)
    rs = np.zeros((128, TA), np.float32)
    for m in range(2):
        rc[m * 64:m * 64 + 32, CTX:] = cos
        rc[m * 64 + 32:m * 64 + 64, CTX:] = cos
        rs[m * 64:m * 64 + 32, CTX:] = -sin
        rs[m * 64 + 32:m * 64 + 64, CTX:] = sin
    c["ropec"] = rc
    c["ropes"] = rs
    feats = np.zeros((17, TA), np.float32)
    decay = np.zeros((TA, 512), np.float32)
    decayb = np.zeros((TA, 512), np.float32)
    deltas = np.abs(np.linspace(math.log(1e-2) / 1.5, math.log(1e-2) / 0.3, 512, dtype=np.float32))
    for (t0, L) in ((0, CTX), (CTX, SEQ)):
        pos = np.arange(L, dtype=np.float32)
        t = pos / max(L - 1, 1)
        bands = np.linspace(1e-4, 7, 8, dtype=np.float32)
        ang = (2.0 * math.pi * pos / L)[:, None] * bands[None]
        f = np.concatenate([t[:, None], np.cos(ang), -np.sin(ang)], -1).astype(np.float32)
        feats[:, t0:t0 + L] = f.T
        dk = np.exp(-t[:, None] * deltas[None]).astype(np.float32)
        decay[t0:t0 + L] = dk
        decayb[t0:t0 + L] = dk
        decayb[t0] = 0.0
    c["hy_feats"] = feats
    c["hy_decay"] = decay
    c["hy_decayb"] = decayb

    def dft_tables(N, npad):
        half = N // 2
        a = np.arange(npad, dtype=np.int64)
        prod = (a[:, None] * a[None, :]) % N
        angm = prod.astype(np.float64) * (2.0 * math.pi / N)
        valid = (a[:, None] <= half) & (a[None, :] <= half)
        tc = np.where(valid, np.cos(angm), 0.0).astype(np.float32).astype(ml_dtypes.bfloat16)
        ts = np.where(valid, np.sin(angm), 0.0).astype(np.float32).astype(ml_dtypes.bfloat16)
        return tc, ts

    c["TC"], c["TS"] = dft_tables(2 * SEQ, NF)
    c["TCc"], c["TSc"] = dft_tables(2 * CTX, NFC)

    def blocked(t):
        nb = t.shape[0] // 128
        return np.ascontiguousarray(t.reshape(nb, 128, nb, 128).transpose(2, 1, 0, 3))

    c["TCf"], c["TSf"] = blocked(c["TC"]), blocked(c["TS"])
    c["TCcf"], c["TScf"] = blocked(c["TCc"]), blocked(c["TSc"])
    wf = np.zeros((128, 72), np.float32)
    for (col0, N, nfb) in ((0, 2 * SEQ, 33), (33, 2 * CTX, 3)):
        for fb in range(nfb):
            f = fb * 128 + np.arange(128)
            w = np.where(f > N // 2, 0.0, np.where((f == 0) | (f == N // 2), 1.0 / N, 2.0 / N)).astype(np.float32)
            wf[:, col0 + fb] = w
            wf[:, 36 + col0 + fb] = -w
    c["hy_wf"] = wf
    return c


def pack_vecs(inp):
    v = np.zeros((DEPTH, NVR, 12288), np.float32)
    for l in range(DEPTH):
        for r, name in enumerate(VEC_ROWS):
            a = np.asarray(inp[name][l], np.float32).reshape(-1)
            v[l, r, :a.size] = a
    return v


def make_in_maps(inp, ncores=8):
    consts = host_consts()
    vecs = pack_vecs(inp)
    shared = {k: np.ascontiguousarray(np.asarray(inp[k], np.float32)) for k in
              ["w_ada", "w_in", "w_a_out", "w_b_out", "w_c_out", "w_o", "w_up", "w_down", "filt_w1", "filt_w2",
               "filt_w3", "conv_a_w", "short_b_w", "conv_f_w"]}
    shared["vecs"] = vecs
    shared["diff_lambda"] = np.asarray(inp["diff_lambda"], np.float32).reshape(DEPTH, 1, 256)
    shared["subln_g"] = np.asarray(inp["subln_g"], np.float32).reshape(DEPTH, 1, 1024)
    shared["final_g"] = np.asarray(inp["final_g"], np.float32).reshape(1, D)
    shared.update(consts)
    maps = []
    for core in range(ncores):
        b = core % 4
        m = dict(shared)
        m["x"] = np.ascontiguousarray(np.asarray(inp["x"][b], np.float32))
        m["ctx"] = np.ascontiguousarray(np.asarray(inp["ctx"][b], np.float32))
        m["c2"] = np.stack([np.asarray(inp["c"][b], np.float32), np.asarray(inp["c_ctx"], np.float32)], 0)
        maps.append(m)
    return maps


NCORES = 4


def kernel(**inputs):
    nc, cx = build()
    maps = make_in_maps(inputs, ncores=NCORES)
    res = run_bass_kernel_spmd(nc, maps, core_ids=list(range(NCORES)))
    out = np.stack([np.asarray(res.results[b]["out"]) for b in range(4)], 0)
    return out.astype(np.float32)
```
